# Optimizing a Trainium2 kernel written in Bass

```python
import math
import jax, jax.numpy as jnp
from jax import lax
import numpy as np

D_MODEL = 1024
BATCH = 8
SEQ = 2048
DEPTH = 1

GDN_HEADS = 8
GDN_HEAD_DIM = 128
GDN_WIDTH = GDN_HEADS * GDN_HEAD_DIM
GDN_CHUNK = 64
CONV_K = 5
DIL_GROUPS = ((128, 1), (512, 4), (2048, 16))
DIL_N_GROUPS = 3
DIL_HEADS_PER_GROUP = 4
DIL_HEADS = DIL_N_GROUPS * DIL_HEADS_PER_GROUP
DIL_HEAD_DIM = 128
DIL_WIDTH = DIL_HEADS * DIL_HEAD_DIM
DIL_OUT_WIDTH = DIL_HEADS_PER_GROUP * DIL_HEAD_DIM
DIL_BLOCK = 64
REL_BUCKETS = 32
REL_MAX_DIST = 1024
D_FF = 4 * D_MODEL
EPS = 1e-6
NEG = -1e30

C_QA = 0
C_KA = C_QA + GDN_WIDTH
C_VA = C_KA + GDN_WIDTH
C_ZA = C_VA + GDN_WIDTH
C_AF = C_ZA + GDN_WIDTH
C_AB = C_AF + GDN_HEADS
C_BF = C_AB + GDN_HEADS
C_BB = C_BF + GDN_HEADS
C_QB = C_BB + GDN_HEADS
C_KB = C_QB + DIL_WIDTH
C_VB = C_KB + DIL_WIDTH
C_GA = C_VB + DIL_WIDTH
C_GB = C_GA + D_MODEL
IN_COLS = C_GB + D_MODEL

kernel_name = "hybrid_gdn_dilated_attn_block"


def _rmsnorm(x, w):
    xf = x.astype(jnp.float32)
    y = xf * lax.rsqrt(jnp.mean(xf * xf, axis=-1, keepdims=True) + EPS) * w.astype(jnp.float32)
    return y.astype(x.dtype)


def _l2norm(x):
    return x * lax.rsqrt(jnp.sum(x * x, axis=-1, keepdims=True) + EPS)


def _t5_bucket(rel):
    nb = REL_BUCKETS // 2
    ret = (rel > 0).astype(np.int32) * nb
    n = np.abs(rel)
    max_exact = nb // 2
    large = max_exact + (np.log(np.maximum(n, 1) / max_exact) / math.log(REL_MAX_DIST / max_exact)
                         * (nb - max_exact)).astype(np.int32)
    large = np.minimum(large, nb - 1)
    return ret + np.where(n < max_exact, n, large).astype(np.int32)


def _chunk_gated_delta(q, k, v, g, beta):
    b, s, h, dk = q.shape
    dv = v.shape[-1]
    n = s // GDN_CHUNK

    def chunks(t):
        return jnp.moveaxis(t.reshape(b, n, GDN_CHUNK, h, -1), 3, 1)

    qc, kc, vc = chunks(q), chunks(k), chunks(v)
    gc = jnp.cumsum(chunks(g[..., None])[..., 0], axis=-1)
    bc = chunks(beta[..., None])
    tri = np.tril(np.ones((GDN_CHUNK, GDN_CHUNK), bool))
    strict = np.tril(np.ones((GDN_CHUNK, GDN_CHUNK), bool), -1)
    dd = gc[..., :, None] - gc[..., None, :]
    gam = jnp.where(tri, jnp.exp(jnp.where(tri, dd, 0.0)), 0.0)
    kb = kc * bc
    a_kk = jnp.where(strict, jnp.einsum('bhnic,bhnjc->bhnij', kb, kc) * gam, 0.0)
    eye = jnp.eye(GDN_CHUNK, dtype=a_kk.dtype)
    rhs = jnp.concatenate([vc * bc, kb * jnp.exp(gc)[..., None]], axis=-1)
    sol = lax.linalg.triangular_solve(a_kk + eye, rhs, left_side=True, lower=True,
                                      unit_diagonal=True)
    u, w = sol[..., :dv], sol[..., dv:]
    a_qk = jnp.einsum('bhnic,bhnjc->bhnij', qc, kc) * gam

    def step(state, inp):
        qn, kn, un, wn, gn, aqk = inp
        v_new = un - jnp.einsum('bhck,bhkv->bhcv', wn, state)
        o = (jnp.einsum('bhck,bhkv->bhcv', qn * jnp.exp(gn)[..., None], state)
             + jnp.einsum('bhij,bhjv->bhiv', aqk, v_new))
        g_last = gn[..., -1]
        state = (state * jnp.exp(g_last)[..., None, None]
                 + jnp.einsum('bhck,bhcv->bhkv', kn * jnp.exp(g_last[..., None] - gn)[..., None], v_new))
        return state, o

    xs = tuple(jnp.moveaxis(t, 2, 0) for t in (qc, kc, u, w, gc, a_qk))
    state0 = jnp.zeros((b, h, dk, dv), jnp.float32)
    _, o = lax.scan(step, state0, xs)
    o = jnp.moveaxis(o, 0, 2)
    return jnp.moveaxis(o, 1, 3).reshape(b, s, h, dv)


def _dilated_group(q, k, v, bias, dil, half):
    b, s, h, hd = q.shape
    blk = DIL_BLOCK
    L = s // dil
    nb = -(-L // blk)
    lp = nb * blk

    def res(t):
        return t.reshape(b, L, dil, h, hd).transpose(0, 2, 1, 3, 4)

    qr = jnp.pad(res(q), ((0, 0), (0, 0), (0, lp - L), (0, 0), (0, 0))).reshape(b, dil, nb, blk, h, hd)

    def win(t):
        tp = jnp.pad(res(t), ((0, 0), (0, 0), (blk, lp - L + blk), (0, 0), (0, 0)))
        tp = tp.reshape(b, dil, nb + 2, blk, h, hd)
        return jnp.concatenate([tp[:, :, :-2], tp[:, :, 1:-1], tp[:, :, 2:]], axis=3)

    kw, vw = win(k), win(v)
    off = np.arange(3 * blk)[None, :] - blk - np.arange(blk)[:, None]
    band = np.abs(off) <= half
    key_pos = np.arange(nb)[:, None] * blk - blk + np.arange(3 * blk)[None, :]
    key_ok = (key_pos >= 0) & (key_pos < L)
    mask = band[None] & key_ok[:, None, :]
    logits = (jnp.einsum('brnqhc,brnkhc->brnhqk', qr, kw) * (hd ** -0.5)
              + bias.astype(jnp.float32))
    logits = jnp.where(mask[:, None], logits, NEG)
    lse = jax.nn.logsumexp(logits, axis=-1)
    p = jnp.exp(logits - lse[..., None])
    o = jnp.einsum('brnhqk,brnkhc->brnqhc', p, vw).reshape(b, dil, lp, h, hd)[:, :, :L]
    o = o.transpose(0, 2, 1, 3, 4).reshape(b, s, h, hd)
    lse = lse.transpose(0, 1, 2, 4, 3).reshape(b, dil, lp, h)[:, :, :L]
    lse = lse.transpose(0, 2, 1, 3).reshape(b, s, h)
    return o, lse


def _gdn_decay(a, a_log, dt_bias):
    return -jnp.exp(a_log.astype(jnp.float32)) * jax.nn.softplus(a + dt_bias.astype(jnp.float32))


def setup_inputs(seed: int = 0) -> dict:
    key = jax.random.key(seed)
    ks = jax.random.split(key, 24)
    f32 = jnp.float32

    def nrm(k, shape, scale):
        return jax.random.normal(k, shape, f32) * scale

    def gain(k, shape):
        return 1.0 + 0.05 * jax.random.normal(k, shape, f32)

    dt = jnp.exp(jax.random.uniform(ks[6], (2, DEPTH, GDN_HEADS), f32, math.log(1e-3), math.log(1e-1)))
    dt_bias = dt + jnp.log(-jnp.expm1(-dt))
    a_log = jnp.log(jax.random.uniform(ks[7], (2, DEPTH, GDN_HEADS), f32, 1.0, 16.0))
    return {
        "x": jax.random.normal(ks[0], (BATCH, SEQ, D_MODEL), f32),
        "rel_bias": nrm(ks[1], (REL_BUCKETS, DIL_HEADS), 0.5),
        "ln_mix_pre": gain(ks[2], (DEPTH, D_MODEL)),
        "w_in": nrm(ks[3], (DEPTH, D_MODEL, IN_COLS), D_MODEL ** -0.5),
        "conv_w": nrm(ks[4], (DEPTH, CONV_K, 3 * GDN_WIDTH), CONV_K ** -0.5),
        "a_log_f": a_log[0],
        "a_log_b": a_log[1],
        "dt_bias_f": dt_bias[0],
        "dt_bias_b": dt_bias[1],
        "norm_a": gain(ks[5], (DEPTH, GDN_HEAD_DIM)),
        "w_branch_a": nrm(ks[8], (DEPTH, GDN_WIDTH, D_MODEL), GDN_WIDTH ** -0.5),
        "w_branch_b": nrm(ks[9], (DEPTH, DIL_OUT_WIDTH, D_MODEL), DIL_OUT_WIDTH ** -0.5),
        "w_out": nrm(ks[10], (DEPTH, D_MODEL, D_MODEL), D_MODEL ** -0.5),
        "ln_mix_post": gain(ks[11], (DEPTH, D_MODEL)),
        "ln_mlp_pre": gain(ks[12], (DEPTH, D_MODEL)),
        "w_ff1": nrm(ks[13], (DEPTH, D_MODEL, D_FF), D_MODEL ** -0.5),
        "w_ff2": nrm(ks[14], (DEPTH, D_FF, D_MODEL), D_FF ** -0.5),
        "ln_mlp_post": gain(ks[15], (DEPTH, D_MODEL)),
    }


def reference(x, rel_bias, ln_mix_pre, w_in, conv_w, a_log_f, a_log_b, dt_bias_f, dt_bias_b,
              norm_a, w_branch_a, w_branch_b, w_out, ln_mix_post, ln_mlp_pre, w_ff1, w_ff2,
              ln_mlp_post):
    b, s, _ = x.shape
    f32 = jnp.float32
    blk = DIL_BLOCK
    off = np.arange(3 * blk)[None, :] - blk - np.arange(blk)[:, None]
    group_bias = []
    for gi, (window, dil) in enumerate(DIL_GROUPS):
        bt = rel_bias[_t5_bucket(off * dil)]
        hs = slice(gi * DIL_HEADS_PER_GROUP, (gi + 1) * DIL_HEADS_PER_GROUP)
        group_bias.append(jnp.transpose(bt[:, :, hs], (2, 0, 1)))

    for l in range(DEPTH):
        h = _rmsnorm(x, ln_mix_pre[l])
        proj = jnp.einsum('bsd,dc->bsc', h, w_in[l]).astype(f32)

        qkv = lax.conv_general_dilated(
            proj[..., C_QA:C_ZA], conv_w[l].astype(f32)[:, None, :], window_strides=(1,),
            padding=[(CONV_K // 2, CONV_K // 2)], dimension_numbers=('NWC', 'WIO', 'NWC'),
            feature_group_count=3 * GDN_WIDTH)
        qkv = jax.nn.silu(qkv)
        qa = _l2norm(qkv[..., :GDN_WIDTH].reshape(b, s, GDN_HEADS, GDN_HEAD_DIM)) * (GDN_HEAD_DIM ** -0.5)
        ka = _l2norm(qkv[..., GDN_WIDTH:2 * GDN_WIDTH].reshape(b, s, GDN_HEADS, GDN_HEAD_DIM))
        va = qkv[..., 2 * GDN_WIDTH:].reshape(b, s, GDN_HEADS, GDN_HEAD_DIM)
        g_f = _gdn_decay(proj[..., C_AF:C_AB], a_log_f[l], dt_bias_f[l])
        g_b = _gdn_decay(proj[..., C_AB:C_BF], a_log_b[l], dt_bias_b[l])
        beta_f = jax.nn.sigmoid(proj[..., C_BF:C_BB])
        beta_b = jax.nn.sigmoid(proj[..., C_BB:C_QB])
        flip = lambda t: jnp.flip(t, axis=1)
        o_a = (_chunk_gated_delta(qa, ka, va, g_f, beta_f)
               + flip(_chunk_gated_delta(flip(qa), flip(ka), flip(va), flip(g_b), flip(beta_b))))
        z = proj[..., C_ZA:C_AF].reshape(b, s, GDN_HEADS, GDN_HEAD_DIM)
        o_a = _rmsnorm(o_a, norm_a[l]) * jax.nn.silu(z)
        o_a = o_a.reshape(b, s, GDN_WIDTH).astype(x.dtype)

        qb = proj[..., C_QB:C_KB].reshape(b, s, DIL_HEADS, DIL_HEAD_DIM)
        kb = proj[..., C_KB:C_VB].reshape(b, s, DIL_HEADS, DIL_HEAD_DIM)
        vb = proj[..., C_VB:C_GA].reshape(b, s, DIL_HEADS, DIL_HEAD_DIM)
        outs, lses = [], []
        for gi, (window, dil) in enumerate(DIL_GROUPS):
            hs = slice(gi * DIL_HEADS_PER_GROUP, (gi + 1) * DIL_HEADS_PER_GROUP)
            o_g, lse_g = _dilated_group(qb[:, :, hs], kb[:, :, hs], vb[:, :, hs], group_bias[gi],
                                        dil, window // (2 * dil))
            outs.append(o_g)
            lses.append(lse_g)
        wgt = jax.nn.softmax(jnp.stack(lses, axis=0), axis=0)
        o_b = jnp.sum(wgt[..., None] * jnp.stack(outs, axis=0), axis=0)
        o_b = o_b.reshape(b, s, DIL_OUT_WIDTH).astype(x.dtype)

        gate_a = jax.nn.sigmoid(proj[..., C_GA:C_GB]).astype(x.dtype)
        gate_b = jax.nn.sigmoid(proj[..., C_GB:IN_COLS]).astype(x.dtype)
        merged = (gate_a * jnp.einsum('bsc,cd->bsd', o_a, w_branch_a[l])
                  + gate_b * jnp.einsum('bsc,cd->bsd', o_b, w_branch_b[l]))
        y = jnp.einsum('bsd,de->bse', merged, w_out[l])
        x = x + _rmsnorm(y, ln_mix_post[l])

        h2 = _rmsnorm(x, ln_mlp_pre[l])
        f = jnp.square(jax.nn.relu(jnp.einsum('bsd,df->bsf', h2, w_ff1[l])))
        f = jnp.einsum('bsf,fd->bsd', f, w_ff2[l])
        x = x + _rmsnorm(f, ln_mlp_post[l])
    return x
```

```python
import math
from contextlib import ExitStack

import numpy as np
import concourse.bass as bass
import concourse.mybir as mybir
from concourse.bass_utils import run_bass_kernel_spmd

F32 = mybir.dt.float32
BF16 = mybir.dt.bfloat16
F32R = mybir.dt.float32r
AF = mybir.ActivationFunctionType
ALU = mybir.AluOpType

S = 2048
D = 1024
NT = S // 128
NH = 8
HG = 2
P2_BF16 = False
C_QA, C_KA, C_VA, C_ZA = 0, 1024, 2048, 3072
C_AF = 4096
C_QB = 4128
C_KB = C_QB + 1536
C_VB = C_KB + 1536
C_GA = C_VB + 1536
C_GB = C_GA + 1024
IN_COLS = C_GB + 1024
DFF = 4096
EPS = 1e-6
NEGBIG = -30000.0
DILS = (1, 4, 16)


class Trk:
    __slots__ = ("w", "r")

    def __init__(self):
        self.w = None
        self.r = []


class Tile:
    def __init__(self, ap, ntrk=1):
        self.ap = ap
        self.trk = [Trk() for _ in range(ntrk)]

    def __getitem__(self, k):
        return self.ap[k]


def _trks(xs):
    out = []
    for x in xs:
        if isinstance(x, Tile):
            out.extend(x.trk)
        elif isinstance(x, Trk):
            out.append(x)
        elif isinstance(x, tuple):
            t, i = x
            if isinstance(i, int):
                out.append(t.trk[i])
            else:
                out.extend(t.trk[i])
        elif isinstance(x, list):
            out.extend(_trks(x))
        else:
            raise TypeError(type(x))
    return out


class Op:
    __slots__ = ("eng", "fn", "deps", "sig", "cnt", "dma", "dsem", "dval", "dprev")

    def __init__(self, eng, fn, dma):
        self.eng = eng
        self.fn = fn
        self.deps = []
        self.sig = False
        self.cnt = 0
        self.dma = dma
        self.dsem = None
        self.dval = None
        self.dprev = 0


ENGS = ("pe", "act", "dve", "pool", "sp")
ATTACH_WAIT = True
HANDLE = {"pe": "tensor", "act": "scalar", "dve": "vector", "pool": "gpsimd", "sp": "sync"}


class Prog:
    def __init__(self, nc, n_dma_sems=32):
        self.nc = nc
        self.ops = {e: [] for e in ENGS}
        self.all_ops = []
        self.n_dma_sems = n_dma_sems
        self.pending_dma = []
        self.barrier_op = {e: None for e in ENGS}

    def op(self, eng, fn, R=(), W=(), dma=False):
        o = Op(eng, fn, dma)
        deps = []
        rt = _trks(R)
        wt = _trks(W)
        raw = set()
        for tr in rt:
            if tr.w is not None:
                deps.append(tr.w)
                raw.add(id(tr.w))
        for tr in wt:
            if tr.w is not None:
                deps.append(tr.w)
            deps.extend(tr.r)
        seen = set()
        d2 = []
        for d in deps:
            if id(d) not in seen and d is not o:
                seen.add(id(d))
                d2.append(d)
        o.deps = d2
        for tr in rt:
            tr.r.append(o)
        for tr in wt:
            tr.w = o
            tr.r = []
        self.ops[eng].append(o)
        self.all_ops.append(o)
        if dma:
            self.pending_dma.append(o)
        return o

    def barrier(self):
        lasts = []
        for e in ("pe", "act", "dve", "pool"):
            for o in reversed(self.ops[e]):
                if not o.dma and o.fn is not None:
                    lasts.append(o)
                    break
        deps = lasts + list(self.pending_dma)
        self.pending_dma = []
        for e in ENGS:
            b = Op(e, None, False)
            b.deps = [d for d in deps]
            self.ops[e].append(b)
            self.all_ops.append(b)

    def emit(self, final_ops=()):
        nc = self.nc
        for o in self.all_ops:
            nd = []
            for d in o.deps:
                if d.dma:
                    nd.append(d)
                    continue
                if d.eng == o.eng and not o.dma:
                    if o.eng == "pe":
                        continue
                    if o.fn is None:
                        continue
                nd.append(d)
            o.deps = nd
            for d in nd:
                d.sig = True
        for o in final_ops:
            o.sig = True
        for e in ENGS:
            c = 0
            for o in self.ops[e]:
                if o.dma or o.fn is None:
                    o.cnt = c
                    continue
                if o.sig:
                    c += 1
                o.cnt = c
        dvals = [0] * self.n_dma_sems
        rr = 0
        for o in self.all_ops:
            if o.dma:
                i = rr % self.n_dma_sems
                rr += 1
                o.dsem = i
                o.dprev = dvals[i]
                dvals[i] += 16
                o.dval = dvals[i]
        with ExitStack() as es:
            sems = {e: es.enter_context(nc.semaphore("s_" + e)) for e in ("pe", "act", "dve", "pool")}
            dsems = [es.enter_context(nc.semaphore("dm%d" % i)) for i in range(self.n_dma_sems)]
            block = es.enter_context(nc.Block())

            def run_engine(ename, eng):
                known = {}

                def wait(key, sem, val):
                    if val <= 0 or known.get(key, 0) >= val:
                        return
                    eng.wait_ge(sem, val)
                    known[key] = val

                def wait_op(d):
                    if d.dma:
                        wait(("d", d.dsem), dsems[d.dsem], d.dval)
                    else:
                        wait(d.eng, sems[d.eng], d.cnt)

                def need(d):
                    if d.dma:
                        key, sem, val = ("d", d.dsem), dsems[d.dsem], d.dval
                    else:
                        key, sem, val = d.eng, sems[d.eng], d.cnt
                    if val <= 0 or known.get(key, 0) >= val:
                        return None
                    return key, sem, val

                for o in self.ops[ename]:
                    if o.fn is None or o.dma or not ATTACH_WAIT:
                        for d in o.deps:
                            wait_op(d)
                        if o.fn is None:
                            continue
                    if o.dma:
                        wait(("d", o.dsem), dsems[o.dsem], o.dprev)
                        ins = o.fn(eng)
                        ins.then_inc(dsems[o.dsem], 16)
                    else:
                        last = None
                        if ATTACH_WAIT:
                            pend = {}
                            for d in o.deps:
                                nd = need(d)
                                if nd is not None:
                                    k_, sem_, val_ = nd
                                    if k_ not in pend or pend[k_][1] < val_:
                                        pend[k_] = (sem_, val_)
                            items = list(pend.items())
                            for k_, (sem_, val_) in items[:-1]:
                                wait(k_, sem_, val_)
                            if items:
                                last = items[-1]
                        ins = o.fn(eng)
                        if last is not None:
                            k_, (sem_, val_) = last
                            ins._wait_ge(sem_, val_)
                            known[k_] = val_
                        if o.sig:
                            ins.then_inc(sems[ename], 1)
                if ename == "sp":
                    for o in final_ops:
                        wait_op(o)

            for ename in ENGS:
                def mk(ename):
                    def f(eng):
                        run_engine(ename, eng)
                    return f
                getattr(block, HANDLE[ename])(mk(ename))


class Arena:
    def __init__(self, handle, nwords, base=0):
        self.h = handle
        self.n = nwords
        self.top = base

    def mark(self):
        return self.top

    def release(self, m):
        self.top = m

    def alloc(self, shape, dt=F32, ntrk=1):
        free = int(np.prod(shape[1:]))
        words = free if dt in (F32, F32R) else (free + 1) // 2
        a = self.top
        self.top += words
        if self.top > self.n:
            raise MemoryError("arena overflow: need %d have %d" % (self.top, self.n))
        v = self.h[0:shape[0], a:a + words]
        if dt not in (F32, F32R):
            v = v.bitcast(dt)
            if words * 2 != free:
                v = v[:, 0:free]
        if len(shape) == 3:
            v = v.rearrange("p (a b) -> p a b", b=shape[2])
        elif len(shape) == 4:
            v = v.rearrange("p (a b c) -> p a b c", b=shape[2], c=shape[3])
        return Tile(v, ntrk)


class Ring:
    def __init__(self, tiles):
        self.tiles = tiles
        self.i = 0

    def next(self):
        t = self.tiles[self.i % len(self.tiles)]
        self.i += 1
        return t


def sst(start, n, step):
    return slice(start, start + (n - 1) * step + 1, step)


def bc(ap, shape, axis):
    return ap.unsqueeze(axis).to_broadcast(list(shape))


def _t5_bucket(rel):
    nb = 16
    ret = (rel > 0).astype(np.int32) * nb
    n = np.abs(rel)
    max_exact = nb // 2
    large = max_exact + (np.log(np.maximum(n, 1) / max_exact) / math.log(1024 / max_exact)
                         * (nb - max_exact)).astype(np.int32)
    large = np.minimum(large, nb - 1)
    return ret + np.where(n < max_exact, n, large).astype(np.int32)


def _consts():
    p = np.arange(128)[:, None]
    f = np.arange(128)[None, :]
    c = {}
    c["ident"] = np.eye(128, dtype=np.float32)
    c["ones"] = np.ones((128, 128), np.float32)
    c["tri0"] = (p <= f).astype(np.float32)
    c["tri1"] = (p >= f).astype(np.float32)
    c["maskT0"] = np.where(f >= p, 0.0, -1e9).astype(np.float32)
    c["maskT1"] = np.where(f <= p, 0.0, -1e9).astype(np.float32)
    c["maskS0"] = np.where(f < p, 0.0, -1e9).astype(np.float32)
    c["maskS1"] = np.where(f > p, 0.0, -1e9).astype(np.float32)
    c["negst0"] = np.where(f > p, -1.0, 0.0).astype(np.float32)
    c["negst1"] = np.where(f < p, -1.0, 0.0).astype(np.float32)
    return np.stack([c[k] for k in CONST_NAMES], axis=1)


CONST_NAMES = ["ident", "ones", "tri0", "tri1", "maskT0", "maskT1", "maskS0", "maskS1", "negst0", "negst1"]


def _attn_tables(rel_bias):
    p = np.arange(128)[:, None]
    f = np.arange(256)[None, :]
    off = p - f + 64
    band = np.abs(off) <= 64
    tbl = np.zeros((128, 12, 256), np.float32)
    for g, dil in enumerate(DILS):
        bidx = _t5_bucket(off * dil)
        for hh in range(4):
            h = g * 4 + hh
            tbl[:, h, :] = rel_bias[bidx, h]
    mask = np.where(band, 0.0, NEGBIG).astype(np.float32)
    return tbl, mask


def build(taps=(), stop=None):
    taps = set(taps)
    nc = bass.Bass("TRN2", target_bir_lowering=False)

    def din(name, shape):
        return nc.dram_tensor(name, list(shape), F32, kind="ExternalInput").ap()

    x_d = din("x", [S, D])
    w_in_d = din("w_in", [D, IN_COLS])
    wba_d = din("w_branch_a", [1024, D])
    wbb_d = din("w_branch_b", [512, D])
    wout_d = din("w_out", [D, D])
    wff1_d = din("w_ff1", [D, DFF])
    wff2_d = din("w_ff2", [DFF, D])
    consts_d = din("consts", [128, len(CONST_NAMES), 128])
    tbl_d = din("attn_tbl", [128, 12, 256])
    amask_d = din("attn_mask", [128, 256])
    vec_d = din("vecs", [128, 8 + 8 + 1 + 24 * 5 + 32])
    lnpost_d = din("lnpost", [128, 2, D])
    out_d = nc.dram_tensor("out", [S, D], F32, kind="ExternalOutput").ap()
    tap_d = {}

    def tapout(name, shape):
        tap_d[name] = nc.dram_tensor("tap_" + name, list(shape), F32, kind="ExternalOutput").ap()
        return tap_d[name]

    es = ExitStack()
    with es:
        ARENA_R_WORDS = 4096
        ARENA_WORDS = 48700 - ARENA_R_WORDS
        arena_h = es.enter_context(nc.sbuf_tensor("arena", [128, ARENA_WORDS], F32))
        A = Arena(arena_h, ARENA_WORDS)
        arena_r_h = es.enter_context(nc.sbuf_tensor("arena_r", [128, ARENA_R_WORDS], F32R))
        AR = Arena(arena_r_h, ARENA_R_WORDS)
        psum_f = [Tile(es.enter_context(nc.psum_tensor("ps%d" % i, [128, 512], F32))) for i in range(8)]
        psum_b = Tile(psum_f[7][:, :].bitcast(BF16).rearrange("p (a b) -> p a b", b=128))
        PSR = Ring(psum_f[0:7])
        P = Prog(nc)
        final_ops = []

        def dma(out_ap, in_ap, R=(), W=(), eng="sp"):
            return P.op(eng, lambda e: e.dma_start(out=out_ap, in_=in_ap), R=R, W=W, dma=True)

        def mm(out_ap, lhsT, rhs, start, stop, R, W):
            return P.op("pe", lambda e: e.matmul(out_ap, lhsT=lhsT, rhs=rhs, start=start, stop=stop), R=R, W=W)

        def transpose(out_ap, in_ap, ident_ap, R, W):
            return P.op("pe", lambda e: e.transpose(out=out_ap, in_=in_ap, identity=ident_ap), R=R, W=W)

        def act(out_ap, in_ap, func, R, W, **kw):
            return P.op("act", lambda e: e.activation(out=out_ap, in_=in_ap, func=func, **kw), R=R, W=W)

        def tt(eng, out_ap, in0, in1, op, R, W):
            return P.op(eng, lambda e: e.tensor_tensor(out=out_ap, in0=in0, in1=in1, op=op), R=R, W=W)

        def ts(eng, out_ap, in0, s1, s2, op0, op1, R, W):
            if s2 is None:
                return P.op(eng, lambda e: e.tensor_scalar(out=out_ap, in0=in0, scalar1=s1, scalar2=None, op0=op0), R=R, W=W)
            return P.op(eng, lambda e: e.tensor_scalar(out=out_ap, in0=in0, scalar1=s1, scalar2=s2, op0=op0, op1=op1), R=R, W=W)

        def stt(eng, out_ap, in0, scalar, in1, op0, op1, R, W):
            return P.op(eng, lambda e: e.scalar_tensor_tensor(out=out_ap, in0=in0, scalar=scalar, in1=in1, op0=op0, op1=op1), R=R, W=W)

        def copy(eng, out_ap, in_ap, R, W):
            if eng == "act":
                return P.op("act", lambda e: e.copy(out=out_ap, in_=in_ap), R=R, W=W)
            return P.op(eng, lambda e: e.tensor_copy(out=out_ap, in_=in_ap), R=R, W=W)

        def memset(eng, ap, val, W):
            return P.op(eng, lambda e: e.memset(ap, val), W=W)

        def tap(name, tile_ap, shape, R):
            if name in taps:
                d = tapout(name, shape)
                o = dma(d, tile_ap, R=R)
                final_ops.append(o)

        CONST = A.alloc([128, len(CONST_NAMES), 128])
        dma(CONST[:], consts_d, W=[CONST])
        cidx = {n: i for i, n in enumerate(CONST_NAMES)}

        def CK(name):
            return CONST[:, cidx[name], :]

        VEC = A.alloc([128, 8 + 8 + 1 + 120 + 32])
        dma(VEC[:], vec_d, W=[VEC])
        LNW1 = VEC[:, 0:8]
        LNW2 = VEC[:, 8:16]
        NORMA = VEC[:, 16:17]
        CW = VEC[:, 17:137].rearrange("p (t k) -> p t k", k=5)
        ALOG = [VEC[:, 137:145], VEC[:, 145:153]]
        DTB = [VEC[:, 153:161], VEC[:, 161:169]]
        CB16 = A.alloc([128, 2, 128], BF16)
        copy("dve", CB16[:, 0, :], CK("ident"), R=[CONST], W=[CB16])
        copy("dve", CB16[:, 1, :], CK("ones"), R=[CONST], W=[CB16])
        IDB = CB16[:, 0, :]
        ONESB = CB16[:, 1, :]

        consts_mark = A.mark()
        hT = A.alloc([128, 8, S], BF16, ntrk=NT)
        oaT = A.alloc([128, 8, S], BF16, ntrk=8)
        persist_mark = A.mark()

        def rms_rstd(src_ap, width, R, junk_tile, ss_tile, rstd_tile):
            act(junk_tile[:, 0:width], src_ap, AF.Square, R=R, W=[junk_tile, ss_tile], accum_out=ss_tile[:, 0:1])
            act(rstd_tile[:, 0:1], ss_tile[:, 0:1], AF.Sqrt, R=[ss_tile], W=[rstd_tile], scale=1.0 / width, bias=EPS)
            P.op("dve", lambda e: e.reciprocal(out=rstd_tile[:, 0:1], in_=rstd_tile[:, 0:1]), R=[rstd_tile], W=[rstd_tile])

        def norm_transpose(src_tile, lnw_ap, dstT, col0, dst_trk, junk, ss, rstd, hb, pb_t=None, defer=False):
            pb_t = psum_b if pb_t is None else pb_t
            rms_rstd(src_tile[:, :], D, [src_tile], junk, ss, rstd)
            ts("dve", hb[:, :], src_tile[:, :], rstd[:, 0:1], None, ALU.mult, None, R=[src_tile, rstd], W=[hb])
            for c in range(8):
                transpose(pb_t[:, c, :], hb[:, c * 128:(c + 1) * 128], IDB, R=[hb, CB16], W=[pb_t])

            def evac():
                tt("dve", dstT[:, :, col0:col0 + 128], pb_t[:, :, :], bc(lnw_ap, [128, 8, 128], 2), ALU.mult,
                   R=[pb_t, VEC], W=dst_trk)
            if defer:
                return evac
            evac()

        class WStream:
            def __init__(self, kc, n, nstage, nbf, cast_engs=("pool",)):
                self.kc, self.n = kc, n
                self.stage = Ring([A.alloc([128, kc, n]) for _ in range(nstage)])
                self.bf = Ring([A.alloc([128, kc, n], BF16) for _ in range(nbf)])
                self.cast_engs = cast_engs
                self.k = 0

            def load(self, dram_ap):
                st = self.stage.next()
                bf = self.bf.next()
                dma(st[:], dram_ap.rearrange("(c p) n -> p c n", p=128), W=[st])
                eng = self.cast_engs[self.k % len(self.cast_engs)]
                self.k += 1
                copy(eng, bf[:], st[:], R=[st], W=[bf])
                return bf

        mA = A.mark()
        xr = Ring([A.alloc([128, D]) for _ in range(3)])
        junk = A.alloc([128, D])
        hbr = Ring([A.alloc([128, D], BF16) for _ in range(2)])
        ssr = Ring([A.alloc([128, 1]) for _ in range(4)])
        rsr = Ring([A.alloc([128, 1]) for _ in range(4)])
        psum_b2 = Tile(psum_f[6][:, :].bitcast(BF16).rearrange("p (a b) -> p a b", b=128))
        pend = None
        for t in range(NT):
            xt = xr.next()
            dma(xt[:], x_d[t * 128:(t + 1) * 128, :], W=[xt])
            ev = norm_transpose(xt, LNW1, hT, t * 128, [(hT, t)], junk, ssr.next(), rsr.next(), hbr.next(),
                                pb_t=(psum_b, psum_b2)[t % 2], defer=True)
            if pend is not None:
                pend()
            pend = ev
        pend()
        if "hT" in taps:
            tmph = A.alloc([128, 8, S])
            copy("dve", tmph[:, :, :], hT[:, :, :], R=[hT], W=[tmph])
            tap("hT", tmph[:, :, :], [128, 8, S], [tmph])
        P.barrier()
        A.release(mA)
        if stop == "A":
            P.emit(final_ops=final_ops)
            return nc

        mB = A.mark()
        GB = A.alloc([128, 4, NT, 8])
        DEC = A.alloc([128, 10, NT, 8])
        mB0 = A.mark()
        wab = WStream(8, 32, 1, 1)
        wab_b = wab.load(w_in_d[:, C_AF:C_AF + 32])
        psab = PSR.next()
        psab_v = psab[:, :].rearrange("p (t c) -> p t c", c=32)
        for t in range(NT):
            for c in range(8):
                mm(psab_v[:, t, :], hT[:, c, t * 128:(t + 1) * 128], wab_b[:, c, :], c == 0, c == 7,
                   R=[(hT, t), wab_b], W=[psab])
        nea = A.alloc([128, 2, 8])
        for d in range(2):
            act(nea[:, d, :], ALOG[d], AF.Exp, R=[VEC], W=[nea])
        ts("dve", nea[:, :, :], nea[:, :, :], -1.0, None, ALU.mult, None, R=[nea], W=[nea])
        tmpab = A.alloc([128, NT, 8])
        for d in range(2):
            tt("dve", tmpab[:, :, :], psab_v[:, :, d * 8:(d + 1) * 8], bc(DTB[d], [128, NT, 8], 1), ALU.add,
               R=[psab, VEC], W=[tmpab])
            act(tmpab[:, :, :], tmpab[:, :, :], AF.Exp, R=[tmpab], W=[tmpab])
            act(tmpab[:, :, :], tmpab[:, :, :], AF.Ln, R=[tmpab], W=[tmpab], bias=1.0)
            tt("dve", GB[:, d, :, :], tmpab[:, :, :], bc(nea[:, d, :], [128, NT, 8], 1), ALU.mult,
               R=[tmpab, nea], W=[GB])
            act(GB[:, 2 + d, :, :], psab_v[:, :, 16 + d * 8:16 + (d + 1) * 8], AF.Sigmoid, R=[psab], W=[GB])
        if "gb" in taps:
            tap("gb", GB[:, :, :, :], [128, 4, NT, 8], [GB])
        for d in range(2):
            pgc = PSR.next()
            g_all = GB[:, d, :, :].rearrange("p t h -> p (t h)")
            mm(pgc[:, 0:128], CK("tri%d" % d), g_all, True, True, R=[CONST, GB], W=[pgc])
            mm(pgc[:, 128:256], CK("ones"), g_all, True, True, R=[CONST, GB], W=[pgc])
            dv = lambda q, d=d: DEC[:, d * 5 + q, :, :].rearrange("p t h -> p (t h)")
            copy("act", dv(0), pgc[:, 0:128], R=[pgc], W=[DEC])
            act(dv(1), pgc[:, 0:128], AF.Exp, R=[pgc], W=[DEC])
            tt("dve", dv(2), pgc[:, 128:256], dv(0), ALU.subtract, R=[pgc, DEC], W=[DEC])
            act(dv(2), dv(2), AF.Exp, R=[DEC], W=[DEC])
            act(dv(3), pgc[:, 128:256], AF.Exp, R=[pgc], W=[DEC])
            ts("dve", dv(4), GB[:, 2 + d, :, :].rearrange("p t h -> p (t h)"), -1.0, None, ALU.mult, None, R=[GB], W=[DEC])
        P.barrier()
        A.release(mB0)
        if stop == "B0":
            P.emit(final_ops=final_ops)
            return nc

        wq = None
        for ps_ in range(NH // HG):
            h0 = ps_ * HG
            A.release(mB0)
            qT = A.alloc([128, HG, S], BF16)
            kT = A.alloc([128, HG, S], BF16)
            ktok = A.alloc([128, NT, HG, 128], BF16)
            vtok = A.alloc([128, NT, HG, 128])
            oT = A.alloc([128, HG, S])
            memset("pool", oT[:, :, :], 0.0, W=[oT])
            mB1 = A.mark()
            wst = WStream(8, 128, 1, 1, cast_engs=("act",))
            prawr = Ring([A.alloc([128, S + 4]) for _ in range(2)])
            accr = Ring([A.alloc([128, S]) for _ in range(2)])
            rnbr = Ring([A.alloc([128, 512]) for _ in range(1)])
            for pr in prawr.tiles:
                memset("pool", pr[:, 0:2], 0.0, W=[pr])
                memset("pool", pr[:, S + 2:S + 4], 0.0, W=[pr])
            units = [(hi, which) for hi in range(HG) for which in range(3)]

            wb_of = {}

            def st_W(u):
                hi, which = units[u]
                col0 = which * 1024 + (h0 + hi) * 128
                wb_of[u] = wst.load(w_in_d[:, col0:col0 + 128])

            def st_P(u):
                hi, which = units[u]
                wb = wb_of[u]
                praw = prawr.tiles[u % 2]
                for tb in range(4):
                    pst = PSR.next()
                    for c in range(8):
                        mm(pst[:, :], wb[:, c, :], hT[:, c, tb * 512:(tb + 1) * 512], c == 0, c == 7,
                           R=[wb, (hT, slice(tb * 4, tb * 4 + 4))], W=[pst])
                    copy("dve", praw[:, 2 + tb * 512:2 + (tb + 1) * 512], pst[:, :], R=[pst], W=[praw])

            def st_C(u):
                hi, which = units[u]
                h = h0 + hi
                praw = prawr.tiles[u % 2]
                acc = accr.tiles[u % 2]
                ctile = which * 8 + h
                P.op("act", lambda e, acc=acc, praw=praw, ctile=ctile: e.mul(out=acc[:, :], in_=praw[:, 0:S], mul=CW[:, ctile, 0:1]),
                     R=[praw, VEC], W=[acc])
                for k in range(1, 5):
                    stt("dve", acc[:, :], praw[:, k:k + S], CW[:, ctile, k:k + 1], acc[:, :], ALU.mult, ALU.add,
                        R=[praw, VEC, acc], W=[acc])
                act(acc[:, :], acc[:, :], AF.Silu, R=[acc], W=[acc])
                if which < 2:
                    act(praw[:, 2:2 + S], acc[:, :], AF.Square, R=[acc, praw], W=[praw])

            n_ps = {}

            def st_Na(u):
                hi, which = units[u]
                if which == 2:
                    return
                praw = prawr.tiles[u % 2]
                lst = []
                for tb in range(4):
                    pst = PSR.next()
                    mm(pst[:, :], CK("ones"), praw[:, 2 + tb * 512:2 + (tb + 1) * 512], True, True, R=[CONST, praw], W=[pst])
                    act(pst[:, :], pst[:, :], AF.Ln, R=[pst], W=[pst], bias=EPS)
                    act(pst[:, :], pst[:, :], AF.Exp, R=[pst], W=[pst], scale=-0.5)
                    lst.append(pst)
                n_ps[u] = lst

            def st_Nb(u):
                hi, which = units[u]
                if which == 2:
                    return
                acc = accr.tiles[u % 2]
                for tb in range(4):
                    bs = slice(tb * 512, (tb + 1) * 512)
                    rnb = n_ps[u][tb]
                    if which == 0:
                        stt("dve", qT[:, hi, bs], acc[:, bs], 128.0 ** -0.5, rnb[:, :], ALU.mult, ALU.mult,
                            R=[acc, rnb], W=[qT])
                    else:
                        tt("dve", kT[:, hi, bs], acc[:, bs], rnb[:, :], ALU.mult, R=[acc, rnb], W=[kT])

            def st_T(u):
                hi, which = units[u]
                if which == 0:
                    return
                acc = accr.tiles[u % 2]
                if which == 1:
                    for th_ in range(2):
                        for j in range(8):
                            t = th_ * 8 + j
                            transpose(psum_b[:, j, :], kT[:, hi, t * 128:(t + 1) * 128], IDB, R=[kT, CB16], W=[psum_b])
                        copy("act", ktok[:, th_ * 8:(th_ + 1) * 8, hi, :], psum_b[:, :, :], R=[psum_b], W=[ktok])
                    return
                dst = vtok
                for tq in range(4):
                    pst = PSR.next()
                    pv = pst[:, :].rearrange("p (a b) -> p a b", b=128)
                    for j in range(4):
                        t = tq * 4 + j
                        transpose(pv[:, j, :], acc[:, t * 128:(t + 1) * 128], CK("ident"), R=[acc, CONST], W=[pst])
                    copy("act", dst[:, tq * 4:(tq + 1) * 4, hi, :], pv[:, :, :], R=[pst], W=[dst])

            nu = len(units)
            for it in range(nu + 3):
                if it < nu:
                    st_W(it)
                if 0 <= it - 3 < nu:
                    st_T(it - 3)
                if 0 <= it - 2 < nu:
                    st_Na(it - 2)
                if 0 <= it - 1 < nu:
                    st_C(it - 1)
                if 0 <= it - 2 < nu:
                    st_Nb(it - 2)
                if it < nu:
                    st_P(it)
            P.barrier()
            A.release(mB1)
            if ps_ == 0:
                if "qT" in taps:
                    tmpq = A.alloc([128, HG, S])
                    copy("dve", tmpq[:, :, :], qT[:, :, :], R=[qT], W=[tmpq])
                    tap("qT", tmpq[:, :, :], [128, HG, S], [tmpq])
                    tmpk = A.alloc([128, HG, S])
                    copy("dve", tmpk[:, :, :], kT[:, :, :], R=[kT], W=[tmpk])
                    tap("kT", tmpk[:, :, :], [128, HG, S], [tmpk])
                    P.barrier()
                    A.release(mB1)
                tap("ktok", ktok[:, :, :, :], [128, NT, HG, 128], [ktok])
                tap("vtok", vtok[:, :, :, :], [128, NT, HG, 128], [vtok])
            if stop == "B1":
                P.emit(final_ops=final_ops)
                return nc

            W_ = HG * 128
            Sst = [A.alloc([128, HG, 128]) for _ in range(2)]
            Sbf = [A.alloc([128, HG, 128], BF16) for _ in range(2)]
            for d in range(2):
                memset("pool", Sst[d][:, :, :], 0.0, W=[Sst[d]])
                memset("pool", Sbf[d][:, :, :], 0.0, W=[Sbf[d]])

            SCAN_F32 = ("aqkT", "qgT", "kdec", "wT", "vnew")
            SCAN_BF = ()
            def mkjob(zero_pad=True):
                j = {}
                j["bcr_gc"] = A.alloc([128, HG, 128])
                j["bcr_e"] = A.alloc([128, HG, 128], BF16)
                for n in ("t0", "gamT", "gamS", "Mf", "Dm", "tmpS", "TDT", "vnew"):
                    j[n] = A.alloc([128, HG, 128])
                j["vnb"] = A.alloc([128, HG, 128], BF16)
                for n in ("kgT", "kdec"):
                    j[n + "2"] = [A.alloc([128, HG, 128]) for _ in range(2)]
                for n in ("aqkT", "qgT"):
                    j[n + "2"] = [A.alloc([128, HG, 128], BF16 if P2_BF16 else F32) for _ in range(2)]
                j["NP"] = [AR.alloc([128, HG, 2, 128], F32R) for _ in range(2)]
                j["Mm"] = [AR.alloc([128, HG, 2, 128], F32R) for _ in range(2)]
                for m_ in (j["Mm"] if zero_pad else []):
                    ts("pool", m_[:, :, 1, :], bc(CK("ident"), [128, HG, 128], 1), 0.0, None, ALU.mult, None, R=[CONST], W=[m_])
                return j

            AR.release(0)
            jobs = [[mkjob(ps_ == 0) for _ in range(1)] for _ in range(2)]
            hs = slice(h0, h0 + HG)

            def act_mul_heads(out3, in3, scal2, R, W):
                for hi_ in range(HG):
                    P.op("act", lambda e, hi_=hi_: e.mul(out=out3[:, hi_, :], in_=in3[:, hi_, :], mul=scal2[:, hi_:hi_ + 1]),
                         R=R, W=W)

            def stage1_phases(j, d, c, par):
                cs = slice(c * 128, (c + 1) * 128)
                gc_ap = DEC[:, d * 5 + 0, c, hs]
                egc_ap = DEC[:, d * 5 + 1, c, hs]
                edec_ap = DEC[:, d * 5 + 2, c, hs]
                negb_ap = DEC[:, d * 5 + 4, c, hs]
                st = {}
                PSR = S1R
                jkgT, jkdec, jqgT, jaqkT = (j[n_ + "2"][par] for n_ in ("kgT", "kdec", "qgT", "aqkT"))
                t0, gamT, gamS, Mf = j["t0"], j["gamT"], j["gamS"], j["Mf"]
                NP0, Mm0 = j["NP"][0], j["Mm"][0]

                def pa_():
                    tt("dve", j["bcr_gc"][:, :, :], bc(CK("ident"), [128, HG, 128], 1), bc(gc_ap, [128, HG, 128], 2), ALU.mult,
                       R=[CONST, DEC], W=[j["bcr_gc"]])
                    tt("dve", j["bcr_e"][:, :, :], bc(CK("ident"), [128, HG, 128], 1), bc(egc_ap, [128, HG, 128], 2), ALU.mult,
                       R=[CONST, DEC], W=[j["bcr_e"]])
                    act_mul_heads(jkdec, ktok[:, c, :, :], edec_ap, R=[ktok, DEC], W=[jkdec])

                def pb_():
                    pbc = PSR.next()
                    pkq = PSR.next()
                    st["pbc"], st["pkq"] = pbc, pkq
                    mm(pbc[:, 0:W_], CK("ones"), j["bcr_gc"][:, :, :].rearrange("p a b -> p (a b)"), True, True,
                       R=[CONST, j["bcr_gc"]], W=[pbc])
                    mm(pbc[:, W_:2 * W_], ONESB, j["bcr_e"][:, :, :].rearrange("p a b -> p (a b)"), True, True,
                       R=[CB16, j["bcr_e"]], W=[pbc])
                    pkq_v = pkq[:, :].rearrange("p (t h i) -> p t h i", t=2, i=128)
                    for hi in range(HG):
                        mm(pkq_v[:, 0, hi, :], kT[:, hi, cs], kT[:, hi, cs], True, True, R=[kT], W=[pkq])
                        mm(pkq_v[:, 1, hi, :], kT[:, hi, cs], qT[:, hi, cs], True, True, R=[kT, qT], W=[pkq])

                def pc_():
                    pbc = st["pbc"]
                    bc_gc = pbc[:, 0:W_].rearrange("p (h i) -> p h i", i=128)
                    tt("dve", t0[:, :, :], bc_gc, bc(gc_ap, [128, HG, 128], 2), ALU.subtract, R=[pbc, DEC], W=[t0])
                    tt("dve", gamT[:, :, :], t0[:, :, :], bc(CK("maskT%d" % d), [128, HG, 128], 1), ALU.add, R=[t0, CONST], W=[gamT])
                    tt("dve", gamS[:, :, :], bc(CK("maskS%d" % d), [128, HG, 128], 1), t0[:, :, :], ALU.subtract, R=[t0, CONST], W=[gamS])

                def pd_():
                    pbc = st["pbc"]
                    bc_egc = pbc[:, W_:2 * W_].rearrange("p (h i) -> p h i", i=128)
                    act(gamS[:, :, :], gamS[:, :, :], AF.Exp, R=[gamS], W=[gamS])
                    act(gamT[:, :, :], gamT[:, :, :], AF.Exp, R=[gamT], W=[gamT])
                    tt("dve", jqgT[:, :, :], qT[:, :, cs], bc_egc, ALU.mult, R=[qT, pbc], W=[jqgT])
                    tt("dve", jkgT[:, :, :], kT[:, :, cs], bc_egc, ALU.mult, R=[kT, pbc], W=[jkgT])
                    copy("act", NP0[:, :, 1, :], bc(CK("ident"), [128, HG, 128], 1), R=[CONST], W=[NP0])

                def pe_():
                    pkq = st["pkq"]
                    pkq_v = pkq[:, :].rearrange("p (t h i) -> p t h i", t=2, i=128)
                    tt("dve", gamS[:, :, :], pkq_v[:, 0, :, :], gamS[:, :, :], ALU.mult, R=[pkq, gamS], W=[gamS])
                    act_mul_heads(Mf, gamS, negb_ap, R=[gamS, DEC], W=[Mf])
                    tt("dve", jaqkT[:, :, :], pkq_v[:, 1, :, :], gamT[:, :, :], ALU.mult, R=[pkq, gamT], W=[jaqkT])

                def pf_():
                    pT = PSR.next()
                    st["pT"] = pT
                    pT_v = pT[:, 0:W_].rearrange("p (h i) -> p h i", i=128)
                    for hi in range(HG):
                        transpose(pT_v[:, hi, :], Mf[:, hi, :], CK("ident"), R=[Mf, CONST], W=[pT])
                    copy("act", Mm0[:, :, 0, :], Mf[:, :, :], R=[Mf], W=[Mm0])

                def pg_():
                    pT = st["pT"]
                    pT_v = pT[:, 0:W_].rearrange("p (h i) -> p h i", i=128)
                    copy("act", NP0[:, :, 0, :], pT_v, R=[pT], W=[NP0])

                return [pa_, pb_, pc_, pd_, pe_, pf_, pg_]

            def solve_level(j, lvl, d):
                cur = lvl % 2
                nxt = 1 - cur
                NPc, NPn = j["NP"][cur], j["NP"][nxt]
                Mc, Mn = j["Mm"][cur], j["Mm"][nxt]
                last = (lvl == 6)
                PSR = LR
                pa = PSR.next()
                pa_v = pa[:, 0:2 * W_].rearrange("p (h t i) -> p h t i", t=2, i=128)
                for hi in range(HG):
                    mm(pa[:, hi * 256:(hi + 1) * 256], Mc[:, hi, 0, :], NPc[:, hi, :, :].rearrange("p t i -> p (t i)"), True, True,
                       R=[Mc, NPc], W=[pa])
                if not last:
                    pb = PSR.next()
                    pb_v = pb[:, 0:2 * W_].rearrange("p (h t i) -> p h t i", t=2, i=128)
                    for hi in range(HG):
                        mm(pb[:, hi * 256:(hi + 1) * 256], NPc[:, hi, 0, :], Mc[:, hi, :, :].rearrange("p t i -> p (t i)"), True, True,
                           R=[Mc, NPc], W=[pb])
                    copy("act", NPn[:, :, 0, :], pa_v[:, :, 0, :], R=[pa], W=[NPn])
                    copy("act", Mn[:, :, 0, :], pb_v[:, :, 0, :], R=[pb], W=[Mn])
                tt("dve", NPn[:, :, 1, :], pa_v[:, :, 1, :], NPc[:, :, 1, :].bitcast(F32), ALU.add, R=[pa, NPc], W=[NPn])

            def finish_scan_phases(j, d, c, par):
                cs = slice(c * 128, (c + 1) * 128)
                PSR = LR
                St = Sst[d]
                NPf = j["NP"][1]
                TDT = j["TDT"]
                jkgT, jkdec, jqgT, jaqkT = (j[n_ + "2"][par] for n_ in ("kgT", "kdec", "qgT", "aqkT"))
                Sb = Sbf[d]
                st = {}

                def f1():
                    act_mul_heads(TDT, NPf[:, :, 1, :].bitcast(F32), GB[:, 2 + d, c, hs], R=[NPf, GB], W=[TDT])

                def f2():
                    pz = PSR.next()
                    st["pz"] = pz
                    pz_v = pz[:, 0:W_].rearrange("p (h i) -> p h i", i=128)
                    for hi in range(HG):
                        mm(pz_v[:, hi, :], jkgT[:, hi, :], St[:, hi, :], True, True, R=[jkgT, St], W=[pz])

                def f3():
                    pz_v = st["pz"][:, 0:W_].rearrange("p (h i) -> p h i", i=128)
                    tt("dve", j["Dm"][:, :, :], vtok[:, c, :, :], pz_v, ALU.subtract, R=[vtok, st["pz"]], W=[j["Dm"]])

                def s1():
                    p1 = PSR.next()
                    st["p1"] = p1
                    p1_v = p1[:, 0:W_].rearrange("p (h i) -> p h i", i=128)
                    for hi in range(HG):
                        mm(p1_v[:, hi, :], TDT[:, hi, :], j["Dm"][:, hi, :], True, True, R=[TDT, j["Dm"]], W=[p1])

                def s2():
                    p1_v = st["p1"][:, 0:W_].rearrange("p (h i) -> p h i", i=128)
                    copy("act", j["vnew"][:, :, :], p1_v, R=[st["p1"]], W=[j["vnew"]])
                    if P2_BF16:
                        copy("dve", j["vnb"][:, :, :], p1_v, R=[st["p1"]], W=[j["vnb"]])

                def s3():
                    p2 = PSR.next()
                    p3 = PSR.next()
                    st["p2"], st["p3"] = p2, p3
                    p2_v = p2[:, 0:W_].rearrange("p (h i) -> p h i", i=128)
                    p3_v = p3[:, 0:W_].rearrange("p (h i) -> p h i", i=128)
                    for hi in range(HG):
                        if P2_BF16:
                            mm(p2_v[:, hi, :], Sb[:, hi, :], jqgT[:, hi, :], True, False, R=[Sb, jqgT], W=[p2])
                            mm(p2_v[:, hi, :], j["vnb"][:, hi, :], jaqkT[:, hi, :], False, True, R=[j["vnb"], jaqkT], W=[p2])
                        else:
                            mm(p2_v[:, hi, :], St[:, hi, :], jqgT[:, hi, :], True, False, R=[St, jqgT], W=[p2])
                            mm(p2_v[:, hi, :], j["vnew"][:, hi, :], jaqkT[:, hi, :], False, True, R=[j["vnew"], jaqkT], W=[p2])
                    for hi in range(HG):
                        mm(p3_v[:, hi, :], jkdec[:, hi, :], j["vnew"][:, hi, :], True, True, R=[jkdec, j["vnew"]], W=[p3])

                def s4():
                    p2_v = st["p2"][:, 0:W_].rearrange("p (h i) -> p h i", i=128)
                    p3_v = st["p3"][:, 0:W_].rearrange("p (h i) -> p h i", i=128)
                    act_mul_heads(j["tmpS"], St, DEC[:, d * 5 + 3, c, hs], R=[St, DEC], W=[j["tmpS"]])
                    tt("dve", St[:, :, :], j["tmpS"][:, :, :], p3_v, ALU.add, R=[j["tmpS"], st["p3"]], W=[St])
                    if P2_BF16:
                        copy("act", Sb[:, :, :], St[:, :, :], R=[St], W=[Sb])
                    tt("dve", oT[:, :, cs], oT[:, :, cs], p2_v, ALU.add, R=[oT, st["p2"]], W=[oT])

                return [f1, f2, f3, s1, s2, s3, s4]

            LR = Ring(psum_f[0:4])
            S1R = Ring(psum_f[4:8])
            js = [jobs[0][0], jobs[1][0]]
            ph0 = [stage1_phases(js[d], d, [0, NT - 1][d], 0) for d in range(2)]
            for k in range(len(ph0[0])):
                for d in range(2):
                    ph0[d][k]()
            for step in range(NT):
                cc = [step, NT - 1 - step]
                par = step % 2
                for lvl in range(7):
                    for d in range(2):
                        solve_level(js[d], lvl, d)
                fs = [finish_scan_phases(js[d], d, cc[d], par) for d in range(2)]
                fs_seq = [fs[d][k] for k in range(len(fs[0])) for d in range(2)]
                s1_seq = []
                if step + 1 < NT:
                    cn = [step + 1, NT - 2 - step]
                    phn = [stage1_phases(js[d], d, cn[d], 1 - par) for d in range(2)]
                    s1_seq = [phn[d][k] for k in range(len(phn[0])) for d in range(2)]
                for k in range(max(len(fs_seq), len(s1_seq))):
                    if k < len(fs_seq):
                        fs_seq[k]()
                    if k < len(s1_seq):
                        s1_seq[k]()
            if ps_ == 0:
                tap("oT", oT[:, :, :], [128, HG, S], [oT])
            P.barrier()
            A.release(mB1)
            if stop == "B2":
                P.emit(final_ops=final_ops)
                return nc

            wz = WStream(8, 128, 2, 2)
            sq = A.alloc([128, S])
            rn = A.alloc([128, S])
            zs = A.alloc([128, S])
            for hi in range(HG):
                h = h0 + hi
                wb = wz.load(w_in_d[:, C_ZA + h * 128:C_ZA + (h + 1) * 128])
                for tb in range(4):
                    pst = PSR.next()
                    for c in range(8):
                        mm(pst[:, :], wb[:, c, :], hT[:, c, tb * 512:(tb + 1) * 512], c == 0, c == 7,
                           R=[wb, (hT, slice(tb * 4, tb * 4 + 4))], W=[pst])
                    act(zs[:, tb * 512:(tb + 1) * 512], pst[:, :], AF.Silu, R=[pst], W=[zs])
                tt("pool", sq[:, :], oT[:, hi, :], oT[:, hi, :], ALU.mult, R=[oT], W=[sq])
                for tb in range(4):
                    pst = PSR.next()
                    mm(pst[:, :], CK("ones"), sq[:, tb * 512:(tb + 1) * 512], True, True, R=[CONST, sq], W=[pst])
                    act(rn[:, tb * 512:(tb + 1) * 512], pst[:, :], AF.Ln, R=[pst], W=[rn], scale=1.0 / 128, bias=EPS)
                act(rn[:, :], rn[:, :], AF.Exp, R=[rn], W=[rn], scale=-0.5)
                stt("dve", zs[:, :], zs[:, :], NORMA, rn[:, :], ALU.mult, ALU.mult, R=[zs, VEC, rn], W=[zs])
                tt("dve", oaT[:, h, :], oT[:, hi, :], zs[:, :], ALU.mult, R=[oT, zs], W=[(oaT, h)])
            P.barrier()
        if "oaT" in taps:
            A.release(mB0)
            tmpo = A.alloc([128, 8, S])
            copy("dve", tmpo[:, :, :], oaT[:, :, :], R=[oaT], W=[tmpo])
            tap("oaT", tmpo[:, :, :], [128, 8, S], [tmpo])
            P.barrier()
        A.release(mB)
        if stop == "B":
            P.emit(final_ops=final_ops)
            return nc

        obT = A.alloc([128, 4, S], BF16, ntrk=4)
        mC = A.mark()
        TBL = A.alloc([128, 12, 256], BF16)
        tblst = A.alloc([128, 12, 256])
        amask = A.alloc([128, 256])
        dma(tblst[:], tbl_d, W=[tblst])
        dma(amask[:], amask_d, W=[amask])
        tt("pool", TBL[:, :, :], tblst[:, :, :], bc(amask[:, :], [128, 12, 256], 1), ALU.add, R=[tblst, amask], W=[TBL])
        acc_o = A.alloc([128, S])
        acc_d = A.alloc([128, S])
        wqk = WStream(8, 128, 2, 3)
        QTr = Ring([A.alloc([128, S], BF16) for _ in range(2)])
        KTr = Ring([A.alloc([128, S], BF16) for _ in range(2)])
        Vr = Ring([A.alloc([128, NT, 128], BF16) for _ in range(2)])
        ETr = Ring([A.alloc([128, 256], BF16) for _ in range(3)])
        rden = A.alloc([128, S])
        for hh in range(4):
            memset("pool", acc_o[:, :], 0.0, W=[acc_o])
            memset("pool", acc_d[:, :], 0.0, W=[acc_d])
            for g, dil in enumerate(DILS):
                h = g * 4 + hh
                L = S // dil
                QT = QTr.next()
                KT = KTr.next()
                V = Vr.next()
                wq_b = wqk.load(w_in_d[:, C_QB + h * 128:C_QB + (h + 1) * 128])
                wk_b = wqk.load(w_in_d[:, C_KB + h * 128:C_KB + (h + 1) * 128])
                wv_b = wqk.load(w_in_d[:, C_VB + h * 128:C_VB + (h + 1) * 128])
                for tb in range(4):
                    pst = PSR.next()
                    for c in range(8):
                        mm(pst[:, :], wq_b[:, c, :], hT[:, c, tb * 512:(tb + 1) * 512], c == 0, c == 7,
                           R=[wq_b, (hT, slice(tb * 4, tb * 4 + 4))], W=[pst])
                    P.op("act", lambda e, QT=QT, pst=pst, tb=tb: e.mul(out=QT[:, tb * 512:(tb + 1) * 512], in_=pst[:, :], mul=128.0 ** -0.5),
                         R=[pst], W=[QT])
                    pst = PSR.next()
                    for c in range(8):
                        mm(pst[:, :], wk_b[:, c, :], hT[:, c, tb * 512:(tb + 1) * 512], c == 0, c == 7,
                           R=[wk_b, (hT, slice(tb * 4, tb * 4 + 4))], W=[pst])
                    copy("dve", KT[:, tb * 512:(tb + 1) * 512], pst[:, :], R=[pst], W=[KT])
                ntile_r = L // 128
                for tq in range(4):
                    pst = PSR.next()
                    pv = pst[:, :].rearrange("p (a b) -> p a b", b=128)
                    for jj in range(4):
                        ti = tq * 4 + jj
                        r, jt = ti // ntile_r, ti % ntile_r
                        t0_ = jt * 128 * dil + r
                        for c in range(8):
                            mm(pv[:, jj, :], hT[:, c, sst(t0_, 128, dil)], wv_b[:, c, :], c == 0, c == 7,
                               R=[wv_b, hT], W=[pst])
                    copy("act", V[:, tq * 4:(tq + 1) * 4, :], pv[:, :, :], R=[pst], W=[V])
                tiles_ = []
                for r in range(dil):
                    for kt in range(ntile_r):
                        ti = r * ntile_r + kt
                        k0 = kt * 128
                        qlo = max(0, k0 - 64)
                        qhi = min(L, k0 + 192)
                        nq = qhi - qlo
                        f0 = qlo - (k0 - 64)
                        ks = sst(k0 * dil + r, 128, dil)
                        qs = sst(qlo * dil + r, nq, dil)
                        tiles_.append((ti, nq, f0, ks, qs))

                def att1(td, QT=QT, KT=KT, h=h):
                    ti, nq, f0, ks, qs = td
                    pS = PSR.next()
                    mm(pS[:, 0:nq], KT[:, ks], QT[:, qs], True, False, R=[KT, QT], W=[pS])
                    mm(pS[:, 0:nq], IDB, TBL[:, h, f0:f0 + nq], False, True, R=[CB16, TBL], W=[pS])
                    ET = ETr.next()
                    act(ET[:, 0:nq], pS[:, 0:nq], AF.Exp, R=[pS], W=[ET])
                    return ET

                def att2(td, ET, V=V):
                    ti, nq, f0, ks, qs = td
                    pO = PSR.next()
                    mm(pO[:, 0:nq], V[:, ti, :], ET[:, 0:nq], True, True, R=[V, ET], W=[pO])
                    mm(pO[:, 256:256 + nq], ONESB, ET[:, 0:nq], True, True, R=[CB16, ET], W=[pO])
                    tt("dve", acc_o[:, qs], acc_o[:, qs], pO[:, 0:nq], ALU.add, R=[acc_o, pO], W=[acc_o])
                    tt("dve", acc_d[:, qs], acc_d[:, qs], pO[:, 256:256 + nq], ALU.add, R=[acc_d, pO], W=[acc_d])

                et_prev = att1(tiles_[0])
                for i_ in range(len(tiles_)):
                    et_next = att1(tiles_[i_ + 1]) if i_ + 1 < len(tiles_) else None
                    att2(tiles_[i_], et_prev)
                    et_prev = et_next
            act(rden[:, :], acc_d[:, :], AF.Ln, R=[acc_d], W=[rden])
            act(rden[:, :], rden[:, :], AF.Exp, R=[rden], W=[rden], scale=-1.0)
            tt("dve", obT[:, hh, :], acc_o[:, :], rden[:, :], ALU.mult, R=[acc_o, rden], W=[(obT, hh)])
        if "obT" in taps:
            P.barrier()
            A.release(mC)
            tmpo = A.alloc([128, 4, S])
            copy("dve", tmpo[:, :, :], obT[:, :, :], R=[obT], W=[tmpo])
            tap("obT", tmpo[:, :, :], [128, 4, S], [tmpo])
        P.barrier()
        A.release(mC)
        if stop == "C":
            P.emit(final_ops=final_ops)
            return nc

        mD = A.mark()
        mergedT_lo = A.top
        mergedT = A.alloc([128, 8, S], BF16)
        mergedT_hi = A.top
        wg8 = WStream(8, 128, 3, 6, cast_engs=("act",))
        wg4 = WStream(4, 128, 2, 2, cast_engs=("act",))
        sgr = Ring([A.alloc([128, 512]) for _ in range(2)])
        m1r = Ring([A.alloc([128, 512]) for _ in range(2)])

        def load_d1(ft):
            fs = slice(ft * 128, (ft + 1) * 128)
            return (wg8.load(w_in_d[:, C_GA + ft * 128:C_GA + (ft + 1) * 128]),
                    wg8.load(w_in_d[:, C_GB + ft * 128:C_GB + (ft + 1) * 128]),
                    wg8.load(wba_d[:, fs]),
                    wg4.load(wbb_d[:, fs]))

        wnext = load_d1(0)
        for ft in range(8):
            wga, wgb, wba, wbb = wnext
            if ft + 1 < 8:
                wnext = load_d1(ft + 1)
            for tb in range(4):
                tsl = slice(tb * 512, (tb + 1) * 512)
                hR = (hT, slice(tb * 4, tb * 4 + 4))
                pga = PSR.next()
                for c in range(8):
                    mm(pga[:, :], wga[:, c, :], hT[:, c, tsl], c == 0, c == 7, R=[wga, hR], W=[pga])
                pba = PSR.next()
                for c in range(8):
                    mm(pba[:, :], wba[:, c, :], oaT[:, c, tsl], c == 0, c == 7, R=[wba, oaT], W=[pba])
                sga = sgr.next()
                act(sga[:, :], pga[:, :], AF.Sigmoid, R=[pga], W=[sga])
                m1 = m1r.next()
                tt("dve", m1[:, :], sga[:, :], pba[:, :], ALU.mult, R=[sga, pba], W=[m1])
                pgb = PSR.next()
                for c in range(8):
                    mm(pgb[:, :], wgb[:, c, :], hT[:, c, tsl], c == 0, c == 7, R=[wgb, hR], W=[pgb])
                pbb = PSR.next()
                for c in range(4):
                    mm(pbb[:, :], wbb[:, c, :], obT[:, c, tsl], c == 0, c == 3, R=[wbb, obT], W=[pbb])
                sgb = sgr.next()
                act(sgb[:, :], pgb[:, :], AF.Sigmoid, R=[pgb], W=[sgb])
                tt("dve", sgb[:, :], sgb[:, :], pbb[:, :], ALU.mult, R=[sgb, pbb], W=[sgb])
                tt("dve", mergedT[:, ft, tsl], m1[:, :], sgb[:, :], ALU.add, R=[m1, sgb], W=[mergedT])
        P.barrier()
        A.release(mD)

        AL = Arena(arena_h, mergedT_lo, base=consts_mark)
        AH = Arena(arena_h, ARENA_WORDS, base=mergedT_hi)
        W1 = AL.alloc([128, 32, 8, 128], BF16, ntrk=32)
        woutb = AL.alloc([128, 8, D], BF16)
        LNP = AH.alloc([128, 2, D])
        dma(LNP[:], lnpost_d, W=[LNP])
        wo_st = Ring([AH.alloc([128, D]) for _ in range(2)])
        for c in range(8):
            st = wo_st.next()
            dma(st[:], wout_d[c * 128:(c + 1) * 128, :], W=[st])
            copy("act", woutb[:, c, :], st[:], R=[st], W=[woutb])
        yr = Ring([AH.alloc([128, D]) for _ in range(2)])
        xr = Ring([AH.alloc([128, D]) for _ in range(3)])
        junk = AH.alloc([128, D])
        ssr = Ring([AH.alloc([128, 1]) for _ in range(4)])
        rsr = Ring([AH.alloc([128, 1]) for _ in range(4)])
        w1st = Ring([AH.alloc([128, 2048]) for _ in range(2)])

        def norm_residual(y, lnp_ap, lnp_tile, xres, ss, rstd, junk_t):
            rms_rstd(y[:, :], D, [y], junk_t, ss, rstd)
            stt("dve", y[:, :], y[:, :], rstd[:, 0:1], lnp_ap, ALU.mult, ALU.mult, R=[y, rstd, lnp_tile], W=[y])
            tt("dve", y[:, :], y[:, :], xres[:, :], ALU.add, R=[y, xres], W=[y])

        def load_w1(i):
            c, hf = i // 2, i % 2
            st = w1st.next()
            dma(st[:], wff1_d[c * 128:(c + 1) * 128, hf * 2048:(hf + 1) * 2048], W=[st])
            copy("act", W1[:, hf * 16:(hf + 1) * 16, c, :], st[:].rearrange("p (f n) -> p f n", n=128), R=[st], W=[W1])

        for t in range(NT):
            y = yr.next()
            xt = xr.next()
            dma(xt[:], x_d[t * 128:(t + 1) * 128, :], W=[xt])
            load_w1(t)
            for half in range(2):
                pst = PSR.next()
                for c in range(8):
                    mm(pst[:, :], mergedT[:, c, t * 128:(t + 1) * 128], woutb[:, c, half * 512:(half + 1) * 512], c == 0, c == 7,
                       R=[mergedT, woutb], W=[pst])
                copy("act", y[:, half * 512:(half + 1) * 512], pst[:, :], R=[pst], W=[y])
            norm_residual(y, LNP[:, 0, :], LNP, xt, ssr.next(), rsr.next(), junk)
            dma(out_d[t * 128:(t + 1) * 128, :], y[:, :], R=[y], eng="pool")
        P.barrier()

        A.release(consts_mark)
        W1e = A.alloc([128, 32, 8, 128], BF16)
        W2 = A.alloc([128, 32, D], BF16, ntrk=32)
        LNP = A.alloc([128, D])
        dma(LNP[:], lnpost_d[:, 1, :], W=[LNP])
        w2st = Ring([A.alloc([128, 512]) for _ in range(3)])
        xr = Ring([A.alloc([128, D]) for _ in range(2)])
        yr = Ring([A.alloc([128, D]) for _ in range(2)])
        hb1 = A.alloc([128, D], BF16)
        h2r = Ring([A.alloc([128, 8, 256], BF16) for _ in range(2)])
        rlr = Ring([A.alloc([128, 256]) for _ in range(2)])
        fcr = Ring([A.alloc([128, 256], BF16) for _ in range(3)])
        ssr = Ring([A.alloc([128, 1]) for _ in range(4)])
        rsr = Ring([A.alloc([128, 1]) for _ in range(4)])
        acc_banks = psum_f[0:4]
        ffr = Ring(psum_f[4:7])
        NBLK = S // 256
        k2 = [0]

        def load_w2(fc):
            for hf in range(2):
                st = w2st.next()
                dma(st[:], wff2_d[fc * 128:(fc + 1) * 128, hf * 512:(hf + 1) * 512], W=[st])
                copy("act", W2[:, fc, hf * 512:(hf + 1) * 512], st[:], R=[st], W=[(W2, fc)])
                k2[0] += 1

        def prep_block(b):
            h2 = h2r.next()
            for tl in range(2):
                t = b * 2 + tl
                xt = xr.next()
                dma(xt[:], out_d[t * 128:(t + 1) * 128, :], W=[xt])
                norm_transpose(xt, LNW2, h2, tl * 128, [h2], hb1, ssr.next(), rsr.next(), hb1)
            return h2

        def ff1(h2, fc):
            pst = ffr.next()
            for c in range(8):
                mm(pst[:, 0:256], W1[:, fc, c, :], h2[:, c, :], c == 0, c == 7, R=[(W1, fc), h2], W=[pst])
            rl = rlr.next()
            act(rl[:, :], pst[:, 0:256], AF.Relu, R=[pst], W=[rl])
            fc_t = fcr.next()
            tt("dve", fc_t[:, :], rl[:, :], rl[:, :], ALU.mult, R=[rl], W=[fc_t])
            return fc_t

        def ff2(fc_t, fc):
            for tl in range(2):
                for hf in range(2):
                    bank = acc_banks[tl * 2 + hf]
                    mm(bank[:, :], fc_t[:, tl * 128:(tl + 1) * 128], W2[:, fc, hf * 512:(hf + 1) * 512], fc == 0, fc == 31,
                       R=[fc_t, (W2, fc)], W=[bank])

        h2_next = prep_block(0)
        for b in range(NBLK):
            h2 = h2_next
            if b == 0:
                load_w2(0)
                load_w2(1)
            prev = ff1(h2, 0)
            for fc in range(32):
                if b == 0 and fc + 2 < 32:
                    load_w2(fc + 2)
                nxt = ff1(h2, fc + 1) if fc + 1 < 32 else None
                ff2(prev, fc)
                prev = nxt
            if b + 1 < NBLK:
                h2_next = prep_block(b + 1)
            for tl in range(2):
                t = b * 2 + tl
                y = yr.next()
                xt = xr.next()
                dma(xt[:], out_d[t * 128:(t + 1) * 128, :], W=[xt])
                for hf in range(2):
                    copy("act", y[:, hf * 512:(hf + 1) * 512], acc_banks[tl * 2 + hf][:, :], R=[acc_banks[tl * 2 + hf]], W=[y])
                norm_residual(y, LNP[:, :], LNP, xt, ssr.next(), rsr.next(), hb1)
                final_ops.append(dma(out_d[t * 128:(t + 1) * 128, :], y[:, :], R=[y], eng="pool"))
        P.emit(final_ops=final_ops)
    return nc


def host_inputs(inputs):
    f = lambda a: np.ascontiguousarray(np.asarray(a, dtype=np.float32))
    rel_bias = f(inputs["rel_bias"])
    tbl, amask = _attn_tables(rel_bias)
    vec = np.zeros((128, 8 + 8 + 1 + 120 + 32), np.float32)
    vec[:, 0:8] = f(inputs["ln_mix_pre"])[0].reshape(8, 128).T
    vec[:, 8:16] = f(inputs["ln_mlp_pre"])[0].reshape(8, 128).T
    vec[:, 16] = f(inputs["norm_a"])[0]
    cw = f(inputs["conv_w"])[0]
    vec[:, 17:137] = cw.reshape(5, 24, 128).transpose(2, 1, 0).reshape(128, 120)
    vec[:, 137:145] = f(inputs["a_log_f"])[0][None, :]
    vec[:, 145:153] = f(inputs["a_log_b"])[0][None, :]
    vec[:, 153:161] = f(inputs["dt_bias_f"])[0][None, :]
    vec[:, 161:169] = f(inputs["dt_bias_b"])[0][None, :]
    lnpost = np.stack([np.broadcast_to(f(inputs["ln_mix_post"])[0], (128, D)),
                       np.broadcast_to(f(inputs["ln_mlp_post"])[0], (128, D))], axis=1)
    shared = {
        "w_in": f(inputs["w_in"])[0],
        "w_branch_a": f(inputs["w_branch_a"])[0],
        "w_branch_b": f(inputs["w_branch_b"])[0],
        "w_out": f(inputs["w_out"])[0],
        "w_ff1": f(inputs["w_ff1"])[0],
        "w_ff2": f(inputs["w_ff2"])[0],
        "consts": _consts(),
        "attn_tbl": tbl,
        "attn_mask": amask,
        "vecs": vec,
        "lnpost": np.ascontiguousarray(lnpost),
    }
    return shared


_NC_CACHE = {}


def kernel(**inputs):
    x = np.ascontiguousarray(np.asarray(inputs["x"], dtype=np.float32))
    B = x.shape[0]
    shared = host_inputs(inputs)
    if "nc" not in _NC_CACHE:
        _NC_CACHE["nc"] = build()
    nc = _NC_CACHE["nc"]
    in_maps = []
    for b in range(B):
        m = dict(shared)
        m["x"] = x[b]
        in_maps.append(m)
    res = run_bass_kernel_spmd(nc, in_maps, core_ids=list(range(B)))
    return np.stack([np.asarray(r["out"]) for r in res.results], axis=0).astype(np.float32)
```

```python
import math
from contextlib import ExitStack

import numpy as np
import concourse.bass as bass
import concourse.mybir as mybir
from concourse.bass_utils import run_bass_kernel_spmd

F32 = mybir.dt.float32
BF16 = mybir.dt.bfloat16
F32R = mybir.dt.float32r
AF = mybir.ActivationFunctionType
ALU = mybir.AluOpType

S = 2048
D = 1024
NT = S // 128
NH = 8
HG = 2
P2_BF16 = False
C_QA, C_KA, C_VA, C_ZA = 0, 1024, 2048, 3072
C_AF = 4096
C_QB = 4128
C_KB = C_QB + 1536
C_VB = C_KB + 1536
C_GA = C_VB + 1536
C_GB = C_GA + 1024
IN_COLS = C_GB + 1024
DFF = 4096
EPS = 1e-6
NEGBIG = -30000.0
DILS = (1, 4, 16)


class Trk:
    __slots__ = ("w", "r")

    def __init__(self):
        self.w = None
        self.r = []


class Tile:
    def __init__(self, ap, ntrk=1):
        self.ap = ap
        self.trk = [Trk() for _ in range(ntrk)]

    def __getitem__(self, k):
        return self.ap[k]


def _trks(xs):
    out = []
    for x in xs:
        if isinstance(x, Tile):
            out.extend(x.trk)
        elif isinstance(x, Trk):
            out.append(x)
        elif isinstance(x, tuple):
            t, i = x
            if isinstance(i, int):
                out.append(t.trk[i])
            else:
                out.extend(t.trk[i])
        elif isinstance(x, list):
            out.extend(_trks(x))
        else:
            raise TypeError(type(x))
    return out


class Op:
    __slots__ = ("eng", "fn", "deps", "sig", "cnt", "dma", "dsem", "dval", "dprev")

    def __init__(self, eng, fn, dma):
        self.eng = eng
        self.fn = fn
        self.deps = []
        self.sig = False
        self.cnt = 0
        self.dma = dma
        self.dsem = None
        self.dval = None
        self.dprev = 0


ENGS = ("pe", "act", "dve", "pool", "sp")
ATTACH_WAIT = True
HANDLE = {"pe": "tensor", "act": "scalar", "dve": "vector", "pool": "gpsimd", "sp": "sync"}


class Prog:
    def __init__(self, nc, n_dma_sems=32):
        self.nc = nc
        self.ops = {e: [] for e in ENGS}
        self.all_ops = []
        self.n_dma_sems = n_dma_sems
        self.pending_dma = []
        self.barrier_op = {e: None for e in ENGS}

    def op(self, eng, fn, R=(), W=(), dma=False):
        o = Op(eng, fn, dma)
        deps = []
        rt = _trks(R)
        wt = _trks(W)
        raw = set()
        for tr in rt:
            if tr.w is not None:
                deps.append(tr.w)
                raw.add(id(tr.w))
        for tr in wt:
            if tr.w is not None:
                deps.append(tr.w)
            deps.extend(tr.r)
        seen = set()
        d2 = []
        for d in deps:
            if id(d) not in seen and d is not o:
                seen.add(id(d))
                d2.append(d)
        o.deps = d2
        for tr in rt:
            tr.r.append(o)
        for tr in wt:
            tr.w = o
            tr.r = []
        self.ops[eng].append(o)
        self.all_ops.append(o)
        if dma:
            self.pending_dma.append(o)
        return o

    def barrier(self):
        lasts = []
        for e in ("pe", "act", "dve", "pool"):
            for o in reversed(self.ops[e]):
                if not o.dma and o.fn is not None:
                    lasts.append(o)
                    break
        deps = lasts + list(self.pending_dma)
        self.pending_dma = []
        for e in ENGS:
            b = Op(e, None, False)
            b.deps = [d for d in deps]
            self.ops[e].append(b)
            self.all_ops.append(b)

    def emit(self, final_ops=()):
        nc = self.nc
        for o in self.all_ops:
            nd = []
            for d in o.deps:
                if d.dma:
                    nd.append(d)
                    continue
                if d.eng == o.eng and not o.dma:
                    if o.eng == "pe":
                        continue
                    if o.fn is None:
                        continue
                nd.append(d)
            o.deps = nd
            for d in nd:
                d.sig = True
        for o in final_ops:
            o.sig = True
        for e in ENGS:
            c = 0
            for o in self.ops[e]:
                if o.dma or o.fn is None:
                    o.cnt = c
                    continue
                if o.sig:
                    c += 1
                o.cnt = c
        dvals = [0] * self.n_dma_sems
        rr = 0
        for o in self.all_ops:
            if o.dma:
                i = rr % self.n_dma_sems
                rr += 1
                o.dsem = i
                o.dprev = dvals[i]
                dvals[i] += 16
                o.dval = dvals[i]
        with ExitStack() as es:
            sems = {e: es.enter_context(nc.semaphore("s_" + e)) for e in ("pe", "act", "dve", "pool")}
            dsems = [es.enter_context(nc.semaphore("dm%d" % i)) for i in range(self.n_dma_sems)]
            block = es.enter_context(nc.Block())

            def run_engine(ename, eng):
                known = {}

                def wait(key, sem, val):
                    if val <= 0 or known.get(key, 0) >= val:
                        return
                    eng.wait_ge(sem, val)
                    known[key] = val

                def wait_op(d):
                    if d.dma:
                        wait(("d", d.dsem), dsems[d.dsem], d.dval)
                    else:
                        wait(d.eng, sems[d.eng], d.cnt)

                def need(d):
                    if d.dma:
                        key, sem, val = ("d", d.dsem), dsems[d.dsem], d.dval
                    else:
                        key, sem, val = d.eng, sems[d.eng], d.cnt
                    if val <= 0 or known.get(key, 0) >= val:
                        return None
                    return key, sem, val

                for o in self.ops[ename]:
                    if o.fn is None or o.dma or not ATTACH_WAIT:
                        for d in o.deps:
                            wait_op(d)
                        if o.fn is None:
                            continue
                    if o.dma:
                        wait(("d", o.dsem), dsems[o.dsem], o.dprev)
                        ins = o.fn(eng)
                        ins.then_inc(dsems[o.dsem], 16)
                    else:
                        last = None
                        if ATTACH_WAIT:
                            pend = {}
                            for d in o.deps:
                                nd = need(d)
                                if nd is not None:
                                    k_, sem_, val_ = nd
                                    if k_ not in pend or pend[k_][1] < val_:
                                        pend[k_] = (sem_, val_)
                            items = list(pend.items())
                            for k_, (sem_, val_) in items[:-1]:
                                wait(k_, sem_, val_)
                            if items:
                                last = items[-1]
                        ins = o.fn(eng)
                        if last is not None:
                            k_, (sem_, val_) = last
                            ins._wait_ge(sem_, val_)
                            known[k_] = val_
                        if o.sig:
                            ins.then_inc(sems[ename], 1)
                if ename == "sp":
                    for o in final_ops:
                        wait_op(o)

            for ename in ENGS:
                def mk(ename):
                    def f(eng):
                        run_engine(ename, eng)
                    return f
                getattr(block, HANDLE[ename])(mk(ename))


class Arena:
    def __init__(self, handle, nwords, base=0):
        self.h = handle
        self.n = nwords
        self.top = base

    def mark(self):
        return self.top

    def release(self, m):
        self.top = m

    def alloc(self, shape, dt=F32, ntrk=1):
        free = int(np.prod(shape[1:]))
        words = free if dt in (F32, F32R) else (free + 1) // 2
        a = self.top
        self.top += words
        if self.top > self.n:
            raise MemoryError("arena overflow: need %d have %d" % (self.top, self.n))
        v = self.h[0:shape[0], a:a + words]
        if dt not in (F32, F32R):
            v = v.bitcast(dt)
            if words * 2 != free:
                v = v[:, 0:free]
        if len(shape) == 3:
            v = v.rearrange("p (a b) -> p a b", b=shape[2])
        elif len(shape) == 4:
            v = v.rearrange("p (a b c) -> p a b c", b=shape[2], c=shape[3])
        return Tile(v, ntrk)


class Ring:
    def __init__(self, tiles):
        self.tiles = tiles
        self.i = 0

    def next(self):
        t = self.tiles[self.i % len(self.tiles)]
        self.i += 1
        return t


def sst(start, n, step):
    return slice(start, start + (n - 1) * step + 1, step)


def bc(ap, shape, axis):
    return ap.unsqueeze(axis).to_broadcast(list(shape))


def _t5_bucket(rel):
    nb = 16
    ret = (rel > 0).astype(np.int32) * nb
    n = np.abs(rel)
    max_exact = nb // 2
    large = max_exact + (np.log(np.maximum(n, 1) / max_exact) / math.log(1024 / max_exact)
                         * (nb - max_exact)).astype(np.int32)
    large = np.minimum(large, nb - 1)
    return ret + np.where(n < max_exact, n, large).astype(np.int32)


def _consts():
    p = np.arange(128)[:, None]
    f = np.arange(128)[None, :]
    c = {}
    c["ident"] = np.eye(128, dtype=np.float32)
    c["ones"] = np.ones((128, 128), np.float32)
    c["tri0"] = (p <= f).astype(np.float32)
    c["tri1"] = (p >= f).astype(np.float32)
    c["maskT0"] = np.where(f >= p, 0.0, -1e9).astype(np.float32)
    c["maskT1"] = np.where(f <= p, 0.0, -1e9).astype(np.float32)
    c["maskS0"] = np.where(f < p, 0.0, -1e9).astype(np.float32)
    c["maskS1"] = np.where(f > p, 0.0, -1e9).astype(np.float32)
    c["negst0"] = np.where(f > p, -1.0, 0.0).astype(np.float32)
    c["negst1"] = np.where(f < p, -1.0, 0.0).astype(np.float32)
    return np.stack([c[k] for k in CONST_NAMES], axis=1)


CONST_NAMES = ["ident", "ones", "tri0", "tri1", "maskT0", "maskT1", "maskS0", "maskS1", "negst0", "negst1"]


def _attn_tables(rel_bias):
    p = np.arange(128)[:, None]
    f = np.arange(256)[None, :]
    off = p - f + 64
    band = np.abs(off) <= 64
    tbl = np.zeros((128, 12, 256), np.float32)
    for g, dil in enumerate(DILS):
        bidx = _t5_bucket(off * dil)
        for hh in range(4):
            h = g * 4 + hh
            tbl[:, h, :] = rel_bias[bidx, h]
    mask = np.where(band, 0.0, NEGBIG).astype(np.float32)
    return tbl, mask


def build(taps=(), stop=None):
    taps = set(taps)
    nc = bass.Bass("TRN2", target_bir_lowering=False)

    def din(name, shape):
        return nc.dram_tensor(name, list(shape), F32, kind="ExternalInput").ap()

    x_d = din("x", [S, D])
    w_in_d = din("w_in", [D, IN_COLS])
    wba_d = din("w_branch_a", [1024, D])
    wbb_d = din("w_branch_b", [512, D])
    wout_d = din("w_out", [D, D])
    wff1_d = din("w_ff1", [D, DFF])
    wff2_d = din("w_ff2", [DFF, D])
    consts_d = din("consts", [128, len(CONST_NAMES), 128])
    tbl_d = din("attn_tbl", [128, 12, 256])
    amask_d = din("attn_mask", [128, 256])
    vec_d = din("vecs", [128, 8 + 8 + 1 + 24 * 5 + 32])
    lnpost_d = din("lnpost", [128, 2, D])
    out_d = nc.dram_tensor("out", [S, D], F32, kind="ExternalOutput").ap()
    tap_d = {}

    def tapout(name, shape):
        tap_d[name] = nc.dram_tensor("tap_" + name, list(shape), F32, kind="ExternalOutput").ap()
        return tap_d[name]

    es = ExitStack()
    with es:
        ARENA_R_WORDS = 4096
        ARENA_WORDS = 48700 - ARENA_R_WORDS
        arena_h = es.enter_context(nc.sbuf_tensor("arena", [128, ARENA_WORDS], F32))
        A = Arena(arena_h, ARENA_WORDS)
        arena_r_h = es.enter_context(nc.sbuf_tensor("arena_r", [128, ARENA_R_WORDS], F32R))
        AR = Arena(arena_r_h, ARENA_R_WORDS)
        psum_f = [Tile(es.enter_context(nc.psum_tensor("ps%d" % i, [128, 512], F32))) for i in range(8)]
        psum_b = Tile(psum_f[7][:, :].bitcast(BF16).rearrange("p (a b) -> p a b", b=128))
        PSR = Ring(psum_f[0:7])
        P = Prog(nc)
        final_ops = []

        def dma(out_ap, in_ap, R=(), W=(), eng="sp"):
            return P.op(eng, lambda e: e.dma_start(out=out_ap, in_=in_ap), R=R, W=W, dma=True)

        def mm(out_ap, lhsT, rhs, start, stop, R, W):
            return P.op("pe", lambda e: e.matmul(out_ap, lhsT=lhsT, rhs=rhs, start=start, stop=stop), R=R, W=W)

        def transpose(out_ap, in_ap, ident_ap, R, W):
            return P.op("pe", lambda e: e.transpose(out=out_ap, in_=in_ap, identity=ident_ap), R=R, W=W)

        def act(out_ap, in_ap, func, R, W, **kw):
            return P.op("act", lambda e: e.activation(out=out_ap, in_=in_ap, func=func, **kw), R=R, W=W)

        def tt(eng, out_ap, in0, in1, op, R, W):
            return P.op(eng, lambda e: e.tensor_tensor(out=out_ap, in0=in0, in1=in1, op=op), R=R, W=W)

        def ts(eng, out_ap, in0, s1, s2, op0, op1, R, W):
            if s2 is None:
                return P.op(eng, lambda e: e.tensor_scalar(out=out_ap, in0=in0, scalar1=s1, scalar2=None, op0=op0), R=R, W=W)
            return P.op(eng, lambda e: e.tensor_scalar(out=out_ap, in0=in0, scalar1=s1, scalar2=s2, op0=op0, op1=op1), R=R, W=W)

        def stt(eng, out_ap, in0, scalar, in1, op0, op1, R, W):
            return P.op(eng, lambda e: e.scalar_tensor_tensor(out=out_ap, in0=in0, scalar=scalar, in1=in1, op0=op0, op1=op1), R=R, W=W)

        def copy(eng, out_ap, in_ap, R, W):
            if eng == "act":
                return P.op("act", lambda e: e.copy(out=out_ap, in_=in_ap), R=R, W=W)
            return P.op(eng, lambda e: e.tensor_copy(out=out_ap, in_=in_ap), R=R, W=W)

        def memset(eng, ap, val, W):
            return P.op(eng, lambda e: e.memset(ap, val), W=W)

        def tap(name, tile_ap, shape, R):
            if name in taps:
                d = tapout(name, shape)
                o = dma(d, tile_ap, R=R)
                final_ops.append(o)

        CONST = A.alloc([128, len(CONST_NAMES), 128])
        dma(CONST[:], consts_d, W=[CONST])
        cidx = {n: i for i, n in enumerate(CONST_NAMES)}

        def CK(name):
            return CONST[:, cidx[name], :]

        VEC = A.alloc([128, 8 + 8 + 1 + 120 + 32])
        dma(VEC[:], vec_d, W=[VEC])
        LNW1 = VEC[:, 0:8]
        LNW2 = VEC[:, 8:16]
        NORMA = VEC[:, 16:17]
        CW = VEC[:, 17:137].rearrange("p (t k) -> p t k", k=5)
        ALOG = [VEC[:, 137:145], VEC[:, 145:153]]
        DTB = [VEC[:, 153:161], VEC[:, 161:169]]
        CB16 = A.alloc([128, 2, 128], BF16)
        copy("dve", CB16[:, 0, :], CK("ident"), R=[CONST], W=[CB16])
        copy("dve", CB16[:, 1, :], CK("ones"), R=[CONST], W=[CB16])
        IDB = CB16[:, 0, :]
        ONESB = CB16[:, 1, :]

        consts_mark = A.mark()
        hT = A.alloc([128, 8, S], BF16, ntrk=NT)
        oaT = A.alloc([128, 8, S], BF16, ntrk=8)
        persist_mark = A.mark()

        def rms_rstd(src_ap, width, R, junk_tile, ss_tile, rstd_tile):
            act(junk_tile[:, 0:width], src_ap, AF.Square, R=R, W=[junk_tile, ss_tile], accum_out=ss_tile[:, 0:1])
            act(rstd_tile[:, 0:1], ss_tile[:, 0:1], AF.Sqrt, R=[ss_tile], W=[rstd_tile], scale=1.0 / width, bias=EPS)
            P.op("dve", lambda e: e.reciprocal(out=rstd_tile[:, 0:1], in_=rstd_tile[:, 0:1]), R=[rstd_tile], W=[rstd_tile])

        def norm_transpose(src_tile, lnw_ap, dstT, col0, dst_trk, junk, ss, rstd, hb, pb_t=None, defer=False):
            pb_t = psum_b if pb_t is None else pb_t
            rms_rstd(src_tile[:, :], D, [src_tile], junk, ss, rstd)
            ts("dve", hb[:, :], src_tile[:, :], rstd[:, 0:1], None, ALU.mult, None, R=[src_tile, rstd], W=[hb])
            for c in range(8):
                transpose(pb_t[:, c, :], hb[:, c * 128:(c + 1) * 128], IDB, R=[hb, CB16], W=[pb_t])

            def evac():
                tt("dve", dstT[:, :, col0:col0 + 128], pb_t[:, :, :], bc(lnw_ap, [128, 8, 128], 2), ALU.mult,
                   R=[pb_t, VEC], W=dst_trk)
            if defer:
                return evac
            evac()

        class WStream:
            def __init__(self, kc, n, nstage, nbf, cast_engs=("pool",)):
                self.kc, self.n = kc, n
                self.stage = Ring([A.alloc([128, kc, n]) for _ in range(nstage)])
                self.bf = Ring([A.alloc([128, kc, n], BF16) for _ in range(nbf)])
                self.cast_engs = cast_engs
                self.k = 0

            def load(self, dram_ap):
                st = self.stage.next()
                bf = self.bf.next()
                dma(st[:], dram_ap.rearrange("(c p) n -> p c n", p=128), W=[st])
                eng = self.cast_engs[self.k % len(self.cast_engs)]
                self.k += 1
                copy(eng, bf[:], st[:], R=[st], W=[bf])
                return bf

        mA = A.mark()
        xr = Ring([A.alloc([128, D]) for _ in range(3)])
        junk = A.alloc([128, D])
        hbr = Ring([A.alloc([128, D], BF16) for _ in range(2)])
        ssr = Ring([A.alloc([128, 1]) for _ in range(4)])
        rsr = Ring([A.alloc([128, 1]) for _ in range(4)])
        psum_b2 = Tile(psum_f[6][:, :].bitcast(BF16).rearrange("p (a b) -> p a b", b=128))
        pend = None
        for t in range(NT):
            xt = xr.next()
            dma(xt[:], x_d[t * 128:(t + 1) * 128, :], W=[xt])
            ev = norm_transpose(xt, LNW1, hT, t * 128, [(hT, t)], junk, ssr.next(), rsr.next(), hbr.next(),
                                pb_t=(psum_b, psum_b2)[t % 2], defer=True)
            if pend is not None:
                pend()
            pend = ev
        pend()
        if "hT" in taps:
            tmph = A.alloc([128, 8, S])
            copy("dve", tmph[:, :, :], hT[:, :, :], R=[hT], W=[tmph])
            tap("hT", tmph[:, :, :], [128, 8, S], [tmph])
        P.barrier()
        A.release(mA)
        if stop == "A":
            P.emit(final_ops=final_ops)
            return nc

        mB = A.mark()
        GB = A.alloc([128, 4, NT, 8])
        DEC = A.alloc([128, 10, NT, 8])
        mB0 = A.mark()
        wab = WStream(8, 32, 1, 1)
        wab_b = wab.load(w_in_d[:, C_AF:C_AF + 32])
        psab = PSR.next()
        psab_v = psab[:, :].rearrange("p (t c) -> p t c", c=32)
        for t in range(NT):
            for c in range(8):
                mm(psab_v[:, t, :], hT[:, c, t * 128:(t + 1) * 128], wab_b[:, c, :], c == 0, c == 7,
                   R=[(hT, t), wab_b], W=[psab])
        nea = A.alloc([128, 2, 8])
        for d in range(2):
            act(nea[:, d, :], ALOG[d], AF.Exp, R=[VEC], W=[nea])
        ts("dve", nea[:, :, :], nea[:, :, :], -1.0, None, ALU.mult, None, R=[nea], W=[nea])
        tmpab = A.alloc([128, NT, 8])
        for d in range(2):
            tt("dve", tmpab[:, :, :], psab_v[:, :, d * 8:(d + 1) * 8], bc(DTB[d], [128, NT, 8], 1), ALU.add,
               R=[psab, VEC], W=[tmpab])
            act(tmpab[:, :, :], tmpab[:, :, :], AF.Exp, R=[tmpab], W=[tmpab])
            act(tmpab[:, :, :], tmpab[:, :, :], AF.Ln, R=[tmpab], W=[tmpab], bias=1.0)
            tt("dve", GB[:, d, :, :], tmpab[:, :, :], bc(nea[:, d, :], [128, NT, 8], 1), ALU.mult,
               R=[tmpab, nea], W=[GB])
            act(GB[:, 2 + d, :, :], psab_v[:, :, 16 + d * 8:16 + (d + 1) * 8], AF.Sigmoid, R=[psab], W=[GB])
        if "gb" in taps:
            tap("gb", GB[:, :, :, :], [128, 4, NT, 8], [GB])
        for d in range(2):
            pgc = PSR.next()
            g_all = GB[:, d, :, :].rearrange("p t h -> p (t h)")
            mm(pgc[:, 0:128], CK("tri%d" % d), g_all, True, True, R=[CONST, GB], W=[pgc])
            mm(pgc[:, 128:256], CK("ones"), g_all, True, True, R=[CONST, GB], W=[pgc])
            dv = lambda q, d=d: DEC[:, d * 5 + q, :, :].rearrange("p t h -> p (t h)")
            copy("act", dv(0), pgc[:, 0:128], R=[pgc], W=[DEC])
            act(dv(1), pgc[:, 0:128], AF.Exp, R=[pgc], W=[DEC])
            tt("dve", dv(2), pgc[:, 128:256], dv(0), ALU.subtract, R=[pgc, DEC], W=[DEC])
            act(dv(2), dv(2), AF.Exp, R=[DEC], W=[DEC])
            act(dv(3), pgc[:, 128:256], AF.Exp, R=[pgc], W=[DEC])
            ts("dve", dv(4), GB[:, 2 + d, :, :].rearrange("p t h -> p (t h)"), -1.0, None, ALU.mult, None, R=[GB], W=[DEC])
        P.barrier()
        A.release(mB0)
        if stop == "B0":
            P.emit(final_ops=final_ops)
            return nc

        wq = None
        for ps_ in range(NH // HG):
            h0 = ps_ * HG
            A.release(mB0)
            qT = A.alloc([128, HG, S], BF16)
            kT = A.alloc([128, HG, S], BF16)
            ktok = A.alloc([128, NT, HG, 128], BF16)
            vtok = A.alloc([128, NT, HG, 128])
            oT = A.alloc([128, HG, S])
            memset("pool", oT[:, :, :], 0.0, W=[oT])
            mB1 = A.mark()
            wst = WStream(8, 128, 1, 1, cast_engs=("act",))
            prawr = Ring([A.alloc([128, S + 4]) for _ in range(2)])
            accr = Ring([A.alloc([128, S]) for _ in range(2)])
            rnbr = Ring([A.alloc([128, 512]) for _ in range(1)])
            for pr in prawr.tiles:
                memset("pool", pr[:, 0:2], 0.0, W=[pr])
                memset("pool", pr[:, S + 2:S + 4], 0.0, W=[pr])
            units = [(hi, which) for hi in range(HG) for which in range(3)]

            wb_of = {}

            def st_W(u):
                hi, which = units[u]
                col0 = which * 1024 + (h0 + hi) * 128
                wb_of[u] = wst.load(w_in_d[:, col0:col0 + 128])

            def st_P(u):
                hi, which = units[u]
                wb = wb_of[u]
                praw = prawr.tiles[u % 2]
                for tb in range(4):
                    pst = PSR.next()
                    for c in range(8):
                        mm(pst[:, :], wb[:, c, :], hT[:, c, tb * 512:(tb + 1) * 512], c == 0, c == 7,
                           R=[wb, (hT, slice(tb * 4, tb * 4 + 4))], W=[pst])
                    copy("dve", praw[:, 2 + tb * 512:2 + (tb + 1) * 512], pst[:, :], R=[pst], W=[praw])

            def st_C(u):
                hi, which = units[u]
                h = h0 + hi
                praw = prawr.tiles[u % 2]
                acc = accr.tiles[u % 2]
                ctile = which * 8 + h
                ts("dve", acc[:, :], praw[:, 0:S], CW[:, ctile, 0:1], None, ALU.mult, None, R=[praw, VEC], W=[acc])
                for k in range(1, 5):
                    stt("dve", acc[:, :], praw[:, k:k + S], CW[:, ctile, k:k + 1], acc[:, :], ALU.mult, ALU.add,
                        R=[praw, VEC, acc], W=[acc])
                act(acc[:, :], acc[:, :], AF.Silu, R=[acc], W=[acc])
                if which < 2:
                    act(praw[:, 2:2 + S], acc[:, :], AF.Square, R=[acc, praw], W=[praw])

            n_ps = {}

            def st_Na(u):
                hi, which = units[u]
                if which == 2:
                    return
                praw = prawr.tiles[u % 2]
                lst = []
                for tb in range(4):
                    pst = PSR.next()
                    mm(pst[:, :], CK("ones"), praw[:, 2 + tb * 512:2 + (tb + 1) * 512], True, True, R=[CONST, praw], W=[pst])
                    act(pst[:, :], pst[:, :], AF.Ln, R=[pst], W=[pst], bias=EPS)
                    act(pst[:, :], pst[:, :], AF.Exp, R=[pst], W=[pst], scale=-0.5)
                    lst.append(pst)
                n_ps[u] = lst

            def st_Nb(u):
                hi, which = units[u]
                if which == 2:
                    return
                acc = accr.tiles[u % 2]
                for tb in range(4):
                    bs = slice(tb * 512, (tb + 1) * 512)
                    rnb = n_ps[u][tb]
                    if which == 0:
                        stt("dve", qT[:, hi, bs], acc[:, bs], 128.0 ** -0.5, rnb[:, :], ALU.mult, ALU.mult,
                            R=[acc, rnb], W=[qT])
                    else:
                        tt("dve", kT[:, hi, bs], acc[:, bs], rnb[:, :], ALU.mult, R=[acc, rnb], W=[kT])

            def st_T(u):
                hi, which = units[u]
                if which == 0:
                    return
                acc = accr.tiles[u % 2]
                if which == 1:
                    for th_ in range(2):
                        for j in range(8):
                            t = th_ * 8 + j
                            transpose(psum_b[:, j, :], kT[:, hi, t * 128:(t + 1) * 128], IDB, R=[kT, CB16], W=[psum_b])
                        copy("act", ktok[:, th_ * 8:(th_ + 1) * 8, hi, :], psum_b[:, :, :], R=[psum_b], W=[ktok])
                    return
                dst = vtok
                for tq in range(4):
                    pst = PSR.next()
                    pv = pst[:, :].rearrange("p (a b) -> p a b", b=128)
                    for j in range(4):
                        t = tq * 4 + j
                        transpose(pv[:, j, :], acc[:, t * 128:(t + 1) * 128], CK("ident"), R=[acc, CONST], W=[pst])
                    copy("act", dst[:, tq * 4:(tq + 1) * 4, hi, :], pv[:, :, :], R=[pst], W=[dst])

            nu = len(units)
            for it in range(nu + 3):
                if it < nu:
                    st_W(it)
                if 0 <= it - 3 < nu:
                    st_T(it - 3)
                if 0 <= it - 2 < nu:
                    st_Na(it - 2)
                if 0 <= it - 1 < nu:
                    st_C(it - 1)
                if 0 <= it - 2 < nu:
                    st_Nb(it - 2)
                if it < nu:
                    st_P(it)
            P.barrier()
            A.release(mB1)
            if ps_ == 0:
                if "qT" in taps:
                    tmpq = A.alloc([128, HG, S])
                    copy("dve", tmpq[:, :, :], qT[:, :, :], R=[qT], W=[tmpq])
                    tap("qT", tmpq[:, :, :], [128, HG, S], [tmpq])
                    tmpk = A.alloc([128, HG, S])
                    copy("dve", tmpk[:, :, :], kT[:, :, :], R=[kT], W=[tmpk])
                    tap("kT", tmpk[:, :, :], [128, HG, S], [tmpk])
                    P.barrier()
                    A.release(mB1)
                tap("ktok", ktok[:, :, :, :], [128, NT, HG, 128], [ktok])
                tap("vtok", vtok[:, :, :, :], [128, NT, HG, 128], [vtok])
            if stop == "B1":
                P.emit(final_ops=final_ops)
                return nc

            W_ = HG * 128
            Sst = [A.alloc([128, HG, 128]) for _ in range(2)]
            Sbf = [A.alloc([128, HG, 128], BF16) for _ in range(2)]
            for d in range(2):
                memset("pool", Sst[d][:, :, :], 0.0, W=[Sst[d]])
                memset("pool", Sbf[d][:, :, :], 0.0, W=[Sbf[d]])

            SCAN_F32 = ("aqkT", "qgT", "kdec", "wT", "vnew")
            SCAN_BF = ()
            def mkjob(zero_pad=True):
                j = {}
                j["bcr_gc"] = A.alloc([128, HG, 128])
                j["bcr_e"] = A.alloc([128, HG, 128], BF16)
                for n in ("t0", "gamT", "gamS", "Mf", "Dm", "tmpS", "TDT", "vnew"):
                    j[n] = A.alloc([128, HG, 128])
                j["vnb"] = A.alloc([128, HG, 128], BF16)
                for n in ("kgT", "kdec"):
                    j[n + "2"] = [A.alloc([128, HG, 128]) for _ in range(2)]
                for n in ("aqkT", "qgT"):
                    j[n + "2"] = [A.alloc([128, HG, 128], BF16 if P2_BF16 else F32) for _ in range(2)]
                j["NP"] = [AR.alloc([128, HG, 2, 128], F32R) for _ in range(2)]
                j["Mm"] = [AR.alloc([128, HG, 2, 128], F32R) for _ in range(2)]
                for m_ in (j["Mm"] if zero_pad else []):
                    ts("pool", m_[:, :, 1, :], bc(CK("ident"), [128, HG, 128], 1), 0.0, None, ALU.mult, None, R=[CONST], W=[m_])
                return j

            AR.release(0)
            jobs = [[mkjob(ps_ == 0) for _ in range(1)] for _ in range(2)]
            hs = slice(h0, h0 + HG)

            def act_mul_heads(out3, in3, scal2, R, W):
                for hi_ in range(HG):
                    P.op("act", lambda e, hi_=hi_: e.mul(out=out3[:, hi_, :], in_=in3[:, hi_, :], mul=scal2[:, hi_:hi_ + 1]),
                         R=R, W=W)

            def stage1_phases(j, d, c, par):
                cs = slice(c * 128, (c + 1) * 128)
                gc_ap = DEC[:, d * 5 + 0, c, hs]
                egc_ap = DEC[:, d * 5 + 1, c, hs]
                edec_ap = DEC[:, d * 5 + 2, c, hs]
                negb_ap = DEC[:, d * 5 + 4, c, hs]
                st = {}
                PSR = S1R
                jkgT, jkdec, jqgT, jaqkT = (j[n_ + "2"][par] for n_ in ("kgT", "kdec", "qgT", "aqkT"))
                t0, gamT, gamS, Mf = j["t0"], j["gamT"], j["gamS"], j["Mf"]
                NP0, Mm0 = j["NP"][0], j["Mm"][0]

                def pa_():
                    tt("dve", j["bcr_gc"][:, :, :], bc(CK("ident"), [128, HG, 128], 1), bc(gc_ap, [128, HG, 128], 2), ALU.mult,
                       R=[CONST, DEC], W=[j["bcr_gc"]])
                    tt("dve", j["bcr_e"][:, :, :], bc(CK("ident"), [128, HG, 128], 1), bc(egc_ap, [128, HG, 128], 2), ALU.mult,
                       R=[CONST, DEC], W=[j["bcr_e"]])
                    act_mul_heads(jkdec, ktok[:, c, :, :], edec_ap, R=[ktok, DEC], W=[jkdec])

                def pb_():
                    pbc = PSR.next()
                    pkq = PSR.next()
                    st["pbc"], st["pkq"] = pbc, pkq
                    mm(pbc[:, 0:W_], CK("ones"), j["bcr_gc"][:, :, :].rearrange("p a b -> p (a b)"), True, True,
                       R=[CONST, j["bcr_gc"]], W=[pbc])
                    mm(pbc[:, W_:2 * W_], ONESB, j["bcr_e"][:, :, :].rearrange("p a b -> p (a b)"), True, True,
                       R=[CB16, j["bcr_e"]], W=[pbc])
                    pkq_v = pkq[:, :].rearrange("p (t h i) -> p t h i", t=2, i=128)
                    for hi in range(HG):
                        mm(pkq_v[:, 0, hi, :], kT[:, hi, cs], kT[:, hi, cs], True, True, R=[kT], W=[pkq])
                        mm(pkq_v[:, 1, hi, :], kT[:, hi, cs], qT[:, hi, cs], True, True, R=[kT, qT], W=[pkq])

                def pc_():
                    pbc = st["pbc"]
                    bc_gc = pbc[:, 0:W_].rearrange("p (h i) -> p h i", i=128)
                    tt("dve", t0[:, :, :], bc_gc, bc(gc_ap, [128, HG, 128], 2), ALU.subtract, R=[pbc, DEC], W=[t0])
                    tt("dve", gamT[:, :, :], t0[:, :, :], bc(CK("maskT%d" % d), [128, HG, 128], 1), ALU.add, R=[t0, CONST], W=[gamT])
                    tt("dve", gamS[:, :, :], bc(CK("maskS%d" % d), [128, HG, 128], 1), t0[:, :, :], ALU.subtract, R=[t0, CONST], W=[gamS])

                def pd_():
                    pbc = st["pbc"]
                    bc_egc = pbc[:, W_:2 * W_].rearrange("p (h i) -> p h i", i=128)
                    act(gamS[:, :, :], gamS[:, :, :], AF.Exp, R=[gamS], W=[gamS])
                    act(gamT[:, :, :], gamT[:, :, :], AF.Exp, R=[gamT], W=[gamT])
                    tt("dve", jqgT[:, :, :], qT[:, :, cs], bc_egc, ALU.mult, R=[qT, pbc], W=[jqgT])
                    tt("dve", jkgT[:, :, :], kT[:, :, cs], bc_egc, ALU.mult, R=[kT, pbc], W=[jkgT])
                    copy("act", NP0[:, :, 1, :], bc(CK("ident"), [128, HG, 128], 1), R=[CONST], W=[NP0])

                def pe_():
                    pkq = st["pkq"]
                    pkq_v = pkq[:, :].rearrange("p (t h i) -> p t h i", t=2, i=128)
                    tt("dve", gamS[:, :, :], pkq_v[:, 0, :, :], gamS[:, :, :], ALU.mult, R=[pkq, gamS], W=[gamS])
                    act_mul_heads(Mf, gamS, negb_ap, R=[gamS, DEC], W=[Mf])
                    tt("dve", jaqkT[:, :, :], pkq_v[:, 1, :, :], gamT[:, :, :], ALU.mult, R=[pkq, gamT], W=[jaqkT])

                def pf_():
                    pT = PSR.next()
                    st["pT"] = pT
                    pT_v = pT[:, 0:W_].rearrange("p (h i) -> p h i", i=128)
                    for hi in range(HG):
                        transpose(pT_v[:, hi, :], Mf[:, hi, :], CK("ident"), R=[Mf, CONST], W=[pT])
                    copy("act", Mm0[:, :, 0, :], Mf[:, :, :], R=[Mf], W=[Mm0])

                def pg_():
                    pT = st["pT"]
                    pT_v = pT[:, 0:W_].rearrange("p (h i) -> p h i", i=128)
                    copy("act", NP0[:, :, 0, :], pT_v, R=[pT], W=[NP0])

                return [pa_, pb_, pc_, pd_, pe_, pf_, pg_]

            def solve_level(j, lvl, d):
                cur = lvl % 2
                nxt = 1 - cur
                NPc, NPn = j["NP"][cur], j["NP"][nxt]
                Mc, Mn = j["Mm"][cur], j["Mm"][nxt]
                last = (lvl == 6)
                PSR = LR
                pa = PSR.next()
                pa_v = pa[:, 0:2 * W_].rearrange("p (h t i) -> p h t i", t=2, i=128)
                for hi in range(HG):
                    mm(pa[:, hi * 256:(hi + 1) * 256], Mc[:, hi, 0, :], NPc[:, hi, :, :].rearrange("p t i -> p (t i)"), True, True,
                       R=[Mc, NPc], W=[pa])
                if not last:
                    pb = PSR.next()
                    pb_v = pb[:, 0:2 * W_].rearrange("p (h t i) -> p h t i", t=2, i=128)
                    for hi in range(HG):
                        mm(pb[:, hi * 256:(hi + 1) * 256], NPc[:, hi, 0, :], Mc[:, hi, :, :].rearrange("p t i -> p (t i)"), True, True,
                           R=[Mc, NPc], W=[pb])
                    copy("act", NPn[:, :, 0, :], pa_v[:, :, 0, :], R=[pa], W=[NPn])
                    copy("act", Mn[:, :, 0, :], pb_v[:, :, 0, :], R=[pb], W=[Mn])
                tt("dve", NPn[:, :, 1, :], pa_v[:, :, 1, :], NPc[:, :, 1, :].bitcast(F32), ALU.add, R=[pa, NPc], W=[NPn])

            def finish_scan_phases(j, d, c, par):
                cs = slice(c * 128, (c + 1) * 128)
                PSR = LR
                St = Sst[d]
                NPf = j["NP"][1]
                TDT = j["TDT"]
                jkgT, jkdec, jqgT, jaqkT = (j[n_ + "2"][par] for n_ in ("kgT", "kdec", "qgT", "aqkT"))
                Sb = Sbf[d]
                st = {}

                def f1():
                    act_mul_heads(TDT, NPf[:, :, 1, :].bitcast(F32), GB[:, 2 + d, c, hs], R=[NPf, GB], W=[TDT])

                def f2():
                    pz = PSR.next()
                    st["pz"] = pz
                    pz_v = pz[:, 0:W_].rearrange("p (h i) -> p h i", i=128)
                    for hi in range(HG):
                        mm(pz_v[:, hi, :], jkgT[:, hi, :], St[:, hi, :], True, True, R=[jkgT, St], W=[pz])

                def f3():
                    pz_v = st["pz"][:, 0:W_].rearrange("p (h i) -> p h i", i=128)
                    tt("dve", j["Dm"][:, :, :], vtok[:, c, :, :], pz_v, ALU.subtract, R=[vtok, st["pz"]], W=[j["Dm"]])

                def s1():
                    p1 = PSR.next()
                    st["p1"] = p1
                    p1_v = p1[:, 0:W_].rearrange("p (h i) -> p h i", i=128)
                    for hi in range(HG):
                        mm(p1_v[:, hi, :], TDT[:, hi, :], j["Dm"][:, hi, :], True, True, R=[TDT, j["Dm"]], W=[p1])

                def s2():
                    p1_v = st["p1"][:, 0:W_].rearrange("p (h i) -> p h i", i=128)
                    copy("act", j["vnew"][:, :, :], p1_v, R=[st["p1"]], W=[j["vnew"]])
                    if P2_BF16:
                        copy("dve", j["vnb"][:, :, :], p1_v, R=[st["p1"]], W=[j["vnb"]])

                def s3():
                    p2 = PSR.next()
                    p3 = PSR.next()
                    st["p2"], st["p3"] = p2, p3
                    p2_v = p2[:, 0:W_].rearrange("p (h i) -> p h i", i=128)
                    p3_v = p3[:, 0:W_].rearrange("p (h i) -> p h i", i=128)
                    for hi in range(HG):
                        if P2_BF16:
                            mm(p2_v[:, hi, :], Sb[:, hi, :], jqgT[:, hi, :], True, False, R=[Sb, jqgT], W=[p2])
                            mm(p2_v[:, hi, :], j["vnb"][:, hi, :], jaqkT[:, hi, :], False, True, R=[j["vnb"], jaqkT], W=[p2])
                        else:
                            mm(p2_v[:, hi, :], St[:, hi, :], jqgT[:, hi, :], True, False, R=[St, jqgT], W=[p2])
                            mm(p2_v[:, hi, :], j["vnew"][:, hi, :], jaqkT[:, hi, :], False, True, R=[j["vnew"], jaqkT], W=[p2])
                    for hi in range(HG):
                        mm(p3_v[:, hi, :], jkdec[:, hi, :], j["vnew"][:, hi, :], True, True, R=[jkdec, j["vnew"]], W=[p3])

                def s4():
                    p2_v = st["p2"][:, 0:W_].rearrange("p (h i) -> p h i", i=128)
                    p3_v = st["p3"][:, 0:W_].rearrange("p (h i) -> p h i", i=128)
                    for hi_ in range(HG):
                        egl_ap = DEC[:, d * 5 + 3, c, h0 + hi_:h0 + hi_ + 1]
                        stt("dve", St[:, hi_, :], St[:, hi_, :], egl_ap, p3_v[:, hi_, :], ALU.mult, ALU.add,
                            R=[St, DEC, st["p3"]], W=[St])
                    if P2_BF16:
                        copy("act", Sb[:, :, :], St[:, :, :], R=[St], W=[Sb])
                    tt("dve", oT[:, :, cs], oT[:, :, cs], p2_v, ALU.add, R=[oT, st["p2"]], W=[oT])

                return [f1, f2, f3, s1, s2, s3, s4]

            LR = Ring(psum_f[0:4])
            S1R = Ring(psum_f[4:8])
            js = [jobs[0][0], jobs[1][0]]
            ph0 = [stage1_phases(js[d], d, [0, NT - 1][d], 0) for d in range(2)]
            for k in range(len(ph0[0])):
                for d in range(2):
                    ph0[d][k]()
            for step in range(NT):
                cc = [step, NT - 1 - step]
                par = step % 2
                for lvl in range(7):
                    for d in range(2):
                        solve_level(js[d], lvl, d)
                fs = [finish_scan_phases(js[d], d, cc[d], par) for d in range(2)]
                fs_seq = [fs[d][k] for k in range(len(fs[0])) for d in range(2)]
                s1_seq = []
                if step + 1 < NT:
                    cn = [step + 1, NT - 2 - step]
                    phn = [stage1_phases(js[d], d, cn[d], 1 - par) for d in range(2)]
                    s1_seq = [phn[d][k] for k in range(len(phn[0])) for d in range(2)]
                for k in range(max(len(fs_seq), len(s1_seq))):
                    if k < len(fs_seq):
                        fs_seq[k]()
                    if k < len(s1_seq):
                        s1_seq[k]()
            if ps_ == 0:
                tap("oT", oT[:, :, :], [128, HG, S], [oT])
            P.barrier()
            A.release(mB1)
            if stop == "B2":
                P.emit(final_ops=final_ops)
                return nc

            wz = WStream(8, 128, 2, 2)
            sq = A.alloc([128, S])
            rn = A.alloc([128, S])
            zs = A.alloc([128, S])
            for hi in range(HG):
                h = h0 + hi
                wb = wz.load(w_in_d[:, C_ZA + h * 128:C_ZA + (h + 1) * 128])
                for tb in range(4):
                    pst = PSR.next()
                    for c in range(8):
                        mm(pst[:, :], wb[:, c, :], hT[:, c, tb * 512:(tb + 1) * 512], c == 0, c == 7,
                           R=[wb, (hT, slice(tb * 4, tb * 4 + 4))], W=[pst])
                    act(zs[:, tb * 512:(tb + 1) * 512], pst[:, :], AF.Silu, R=[pst], W=[zs])
                tt("pool", sq[:, :], oT[:, hi, :], oT[:, hi, :], ALU.mult, R=[oT], W=[sq])
                for tb in range(4):
                    pst = PSR.next()
                    mm(pst[:, :], CK("ones"), sq[:, tb * 512:(tb + 1) * 512], True, True, R=[CONST, sq], W=[pst])
                    act(rn[:, tb * 512:(tb + 1) * 512], pst[:, :], AF.Ln, R=[pst], W=[rn], scale=1.0 / 128, bias=EPS)
                act(rn[:, :], rn[:, :], AF.Exp, R=[rn], W=[rn], scale=-0.5)
                stt("dve", zs[:, :], zs[:, :], NORMA, rn[:, :], ALU.mult, ALU.mult, R=[zs, VEC, rn], W=[zs])
                tt("dve", oaT[:, h, :], oT[:, hi, :], zs[:, :], ALU.mult, R=[oT, zs], W=[(oaT, h)])
            P.barrier()
        if "oaT" in taps:
            A.release(mB0)
            tmpo = A.alloc([128, 8, S])
            copy("dve", tmpo[:, :, :], oaT[:, :, :], R=[oaT], W=[tmpo])
            tap("oaT", tmpo[:, :, :], [128, 8, S], [tmpo])
            P.barrier()
        A.release(mB)
        if stop == "B":
            P.emit(final_ops=final_ops)
            return nc

        obT = A.alloc([128, 4, S], BF16, ntrk=4)
        mC = A.mark()
        TBL = A.alloc([128, 12, 256], BF16)
        tblst = A.alloc([128, 12, 256])
        amask = A.alloc([128, 256])
        dma(tblst[:], tbl_d, W=[tblst])
        dma(amask[:], amask_d, W=[amask])
        tt("pool", TBL[:, :, :], tblst[:, :, :], bc(amask[:, :], [128, 12, 256], 1), ALU.add, R=[tblst, amask], W=[TBL])
        acc_o = A.alloc([128, S])
        acc_d = A.alloc([128, S])
        wqk = WStream(8, 128, 2, 3)
        QTr = Ring([A.alloc([128, S], BF16) for _ in range(2)])
        KTr = Ring([A.alloc([128, S], BF16) for _ in range(2)])
        Vr = Ring([A.alloc([128, NT, 128], BF16) for _ in range(2)])
        ETr = Ring([A.alloc([128, 256], BF16) for _ in range(3)])
        rden = A.alloc([128, S])
        for hh in range(4):
            memset("pool", acc_o[:, :], 0.0, W=[acc_o])
            memset("pool", acc_d[:, :], 0.0, W=[acc_d])
            for g, dil in enumerate(DILS):
                h = g * 4 + hh
                L = S // dil
                QT = QTr.next()
                KT = KTr.next()
                V = Vr.next()
                wq_b = wqk.load(w_in_d[:, C_QB + h * 128:C_QB + (h + 1) * 128])
                wk_b = wqk.load(w_in_d[:, C_KB + h * 128:C_KB + (h + 1) * 128])
                wv_b = wqk.load(w_in_d[:, C_VB + h * 128:C_VB + (h + 1) * 128])
                for tb in range(4):
                    pst = PSR.next()
                    for c in range(8):
                        mm(pst[:, :], wq_b[:, c, :], hT[:, c, tb * 512:(tb + 1) * 512], c == 0, c == 7,
                           R=[wq_b, (hT, slice(tb * 4, tb * 4 + 4))], W=[pst])
                    P.op("act", lambda e, QT=QT, pst=pst, tb=tb: e.mul(out=QT[:, tb * 512:(tb + 1) * 512], in_=pst[:, :], mul=128.0 ** -0.5),
                         R=[pst], W=[QT])
                    pst = PSR.next()
                    for c in range(8):
                        mm(pst[:, :], wk_b[:, c, :], hT[:, c, tb * 512:(tb + 1) * 512], c == 0, c == 7,
                           R=[wk_b, (hT, slice(tb * 4, tb * 4 + 4))], W=[pst])
                    copy("dve", KT[:, tb * 512:(tb + 1) * 512], pst[:, :], R=[pst], W=[KT])
                ntile_r = L // 128
                for tq in range(4):
                    pst = PSR.next()
                    pv = pst[:, :].rearrange("p (a b) -> p a b", b=128)
                    for jj in range(4):
                        ti = tq * 4 + jj
                        r, jt = ti // ntile_r, ti % ntile_r
                        t0_ = jt * 128 * dil + r
                        for c in range(8):
                            mm(pv[:, jj, :], hT[:, c, sst(t0_, 128, dil)], wv_b[:, c, :], c == 0, c == 7,
                               R=[wv_b, hT], W=[pst])
                    copy("act", V[:, tq * 4:(tq + 1) * 4, :], pv[:, :, :], R=[pst], W=[V])
                tiles_ = []
                for r in range(dil):
                    for kt in range(ntile_r):
                        ti = r * ntile_r + kt
                        k0 = kt * 128
                        qlo = max(0, k0 - 64)
                        qhi = min(L, k0 + 192)
                        nq = qhi - qlo
                        f0 = qlo - (k0 - 64)
                        ks = sst(k0 * dil + r, 128, dil)
                        qs = sst(qlo * dil + r, nq, dil)
                        tiles_.append((ti, nq, f0, ks, qs))

                def att1(td, QT=QT, KT=KT, h=h):
                    ti, nq, f0, ks, qs = td
                    pS = PSR.next()
                    mm(pS[:, 0:nq], KT[:, ks], QT[:, qs], True, False, R=[KT, QT], W=[pS])
                    mm(pS[:, 0:nq], IDB, TBL[:, h, f0:f0 + nq], False, True, R=[CB16, TBL], W=[pS])
                    ET = ETr.next()
                    act(ET[:, 0:nq], pS[:, 0:nq], AF.Exp, R=[pS], W=[ET])
                    return ET

                def att2(td, ET, V=V):
                    ti, nq, f0, ks, qs = td
                    pO = PSR.next()
                    mm(pO[:, 0:nq], V[:, ti, :], ET[:, 0:nq], True, True, R=[V, ET], W=[pO])
                    mm(pO[:, 256:256 + nq], ONESB, ET[:, 0:nq], True, True, R=[CB16, ET], W=[pO])
                    tt("dve", acc_o[:, qs], acc_o[:, qs], pO[:, 0:nq], ALU.add, R=[acc_o, pO], W=[acc_o])
                    tt("dve", acc_d[:, qs], acc_d[:, qs], pO[:, 256:256 + nq], ALU.add, R=[acc_d, pO], W=[acc_d])

                et_prev = att1(tiles_[0])
                for i_ in range(len(tiles_)):
                    et_next = att1(tiles_[i_ + 1]) if i_ + 1 < len(tiles_) else None
                    att2(tiles_[i_], et_prev)
                    et_prev = et_next
            act(rden[:, :], acc_d[:, :], AF.Ln, R=[acc_d], W=[rden])
            act(rden[:, :], rden[:, :], AF.Exp, R=[rden], W=[rden], scale=-1.0)
            tt("dve", obT[:, hh, :], acc_o[:, :], rden[:, :], ALU.mult, R=[acc_o, rden], W=[(obT, hh)])
        if "obT" in taps:
            P.barrier()
            A.release(mC)
            tmpo = A.alloc([128, 4, S])
            copy("dve", tmpo[:, :, :], obT[:, :, :], R=[obT], W=[tmpo])
            tap("obT", tmpo[:, :, :], [128, 4, S], [tmpo])
        P.barrier()
        A.release(mC)
        if stop == "C":
            P.emit(final_ops=final_ops)
            return nc

        mD = A.mark()
        mergedT_lo = A.top
        mergedT = A.alloc([128, 8, S], BF16)
        mergedT_hi = A.top
        wg8 = WStream(8, 128, 3, 6, cast_engs=("act",))
        wg4 = WStream(4, 128, 2, 2, cast_engs=("act",))
        sgr = Ring([A.alloc([128, 512]) for _ in range(2)])
        m1r = Ring([A.alloc([128, 512]) for _ in range(2)])

        def load_d1(ft):
            fs = slice(ft * 128, (ft + 1) * 128)
            return (wg8.load(w_in_d[:, C_GA + ft * 128:C_GA + (ft + 1) * 128]),
                    wg8.load(w_in_d[:, C_GB + ft * 128:C_GB + (ft + 1) * 128]),
                    wg8.load(wba_d[:, fs]),
                    wg4.load(wbb_d[:, fs]))

        wnext = load_d1(0)
        for ft in range(8):
            wga, wgb, wba, wbb = wnext
            if ft + 1 < 8:
                wnext = load_d1(ft + 1)
            for tb in range(4):
                tsl = slice(tb * 512, (tb + 1) * 512)
                hR = (hT, slice(tb * 4, tb * 4 + 4))
                pga = PSR.next()
                for c in range(8):
                    mm(pga[:, :], wga[:, c, :], hT[:, c, tsl], c == 0, c == 7, R=[wga, hR], W=[pga])
                pba = PSR.next()
                for c in range(8):
                    mm(pba[:, :], wba[:, c, :], oaT[:, c, tsl], c == 0, c == 7, R=[wba, oaT], W=[pba])
                sga = sgr.next()
                act(sga[:, :], pga[:, :], AF.Sigmoid, R=[pga], W=[sga])
                m1 = m1r.next()
                tt("dve", m1[:, :], sga[:, :], pba[:, :], ALU.mult, R=[sga, pba], W=[m1])
                pgb = PSR.next()
                for c in range(8):
                    mm(pgb[:, :], wgb[:, c, :], hT[:, c, tsl], c == 0, c == 7, R=[wgb, hR], W=[pgb])
                pbb = PSR.next()
                for c in range(4):
                    mm(pbb[:, :], wbb[:, c, :], obT[:, c, tsl], c == 0, c == 3, R=[wbb, obT], W=[pbb])
                sgb = sgr.next()
                act(sgb[:, :], pgb[:, :], AF.Sigmoid, R=[pgb], W=[sgb])
                tt("dve", sgb[:, :], sgb[:, :], pbb[:, :], ALU.mult, R=[sgb, pbb], W=[sgb])
                tt("dve", mergedT[:, ft, tsl], m1[:, :], sgb[:, :], ALU.add, R=[m1, sgb], W=[mergedT])
        P.barrier()
        A.release(mD)

        AL = Arena(arena_h, mergedT_lo, base=consts_mark)
        AH = Arena(arena_h, ARENA_WORDS, base=mergedT_hi)
        W1 = AL.alloc([128, 32, 8, 128], BF16, ntrk=32)
        woutb = AL.alloc([128, 8, D], BF16)
        LNP = AH.alloc([128, 2, D])
        dma(LNP[:], lnpost_d, W=[LNP])
        wo_st = Ring([AH.alloc([128, D]) for _ in range(2)])
        for c in range(8):
            st = wo_st.next()
            dma(st[:], wout_d[c * 128:(c + 1) * 128, :], W=[st])
            copy("act", woutb[:, c, :], st[:], R=[st], W=[woutb])
        yr = Ring([AH.alloc([128, D]) for _ in range(2)])
        xr = Ring([AH.alloc([128, D]) for _ in range(3)])
        junk = AH.alloc([128, D])
        ssr = Ring([AH.alloc([128, 1]) for _ in range(4)])
        rsr = Ring([AH.alloc([128, 1]) for _ in range(4)])
        w1st = Ring([AH.alloc([128, 2048]) for _ in range(2)])

        def norm_residual(y, lnp_ap, lnp_tile, xres, ss, rstd, junk_t):
            rms_rstd(y[:, :], D, [y], junk_t, ss, rstd)
            stt("dve", y[:, :], y[:, :], rstd[:, 0:1], lnp_ap, ALU.mult, ALU.mult, R=[y, rstd, lnp_tile], W=[y])
            tt("dve", y[:, :], y[:, :], xres[:, :], ALU.add, R=[y, xres], W=[y])

        def load_w1(i):
            c, hf = i // 2, i % 2
            st = w1st.next()
            dma(st[:], wff1_d[c * 128:(c + 1) * 128, hf * 2048:(hf + 1) * 2048], W=[st])
            copy("act", W1[:, hf * 16:(hf + 1) * 16, c, :], st[:].rearrange("p (f n) -> p f n", n=128), R=[st], W=[W1])

        for t in range(NT):
            y = yr.next()
            xt = xr.next()
            dma(xt[:], x_d[t * 128:(t + 1) * 128, :], W=[xt])
            load_w1(t)
            for half in range(2):
                pst = PSR.next()
                for c in range(8):
                    mm(pst[:, :], mergedT[:, c, t * 128:(t + 1) * 128], woutb[:, c, half * 512:(half + 1) * 512], c == 0, c == 7,
                       R=[mergedT, woutb], W=[pst])
                copy("act", y[:, half * 512:(half + 1) * 512], pst[:, :], R=[pst], W=[y])
            norm_residual(y, LNP[:, 0, :], LNP, xt, ssr.next(), rsr.next(), junk)
            dma(out_d[t * 128:(t + 1) * 128, :], y[:, :], R=[y], eng="pool")
        P.barrier()

        A.release(consts_mark)
        W1e = A.alloc([128, 32, 8, 128], BF16)
        W2 = A.alloc([128, 32, D], BF16, ntrk=32)
        LNP = A.alloc([128, D])
        dma(LNP[:], lnpost_d[:, 1, :], W=[LNP])
        w2st = Ring([A.alloc([128, 512]) for _ in range(3)])
        xr = Ring([A.alloc([128, D]) for _ in range(2)])
        yr = Ring([A.alloc([128, D]) for _ in range(2)])
        hb1 = A.alloc([128, D], BF16)
        h2r = Ring([A.alloc([128, 8, 256], BF16) for _ in range(2)])
        rlr = Ring([A.alloc([128, 256]) for _ in range(2)])
        fcr = Ring([A.alloc([128, 256], BF16) for _ in range(3)])
        ssr = Ring([A.alloc([128, 1]) for _ in range(4)])
        rsr = Ring([A.alloc([128, 1]) for _ in range(4)])
        acc_banks = psum_f[0:4]
        ffr = Ring(psum_f[4:7])
        NBLK = S // 256
        k2 = [0]

        def load_w2(fc):
            for hf in range(2):
                st = w2st.next()
                dma(st[:], wff2_d[fc * 128:(fc + 1) * 128, hf * 512:(hf + 1) * 512], W=[st])
                copy("act", W2[:, fc, hf * 512:(hf + 1) * 512], st[:], R=[st], W=[(W2, fc)])
                k2[0] += 1

        def prep_block(b):
            h2 = h2r.next()
            for tl in range(2):
                t = b * 2 + tl
                xt = xr.next()
                dma(xt[:], out_d[t * 128:(t + 1) * 128, :], W=[xt])
                norm_transpose(xt, LNW2, h2, tl * 128, [h2], hb1, ssr.next(), rsr.next(), hb1)
            return h2

        def ff1(h2, fc):
            pst = ffr.next()
            for c in range(8):
                mm(pst[:, 0:256], W1[:, fc, c, :], h2[:, c, :], c == 0, c == 7, R=[(W1, fc), h2], W=[pst])
            rl = rlr.next()
            act(rl[:, :], pst[:, 0:256], AF.Relu, R=[pst], W=[rl])
            fc_t = fcr.next()
            tt("dve", fc_t[:, :], rl[:, :], rl[:, :], ALU.mult, R=[rl], W=[fc_t])
            return fc_t

        def ff2(fc_t, fc):
            for tl in range(2):
                for hf in range(2):
                    bank = acc_banks[tl * 2 + hf]
                    mm(bank[:, :], fc_t[:, tl * 128:(tl + 1) * 128], W2[:, fc, hf * 512:(hf + 1) * 512], fc == 0, fc == 31,
                       R=[fc_t, (W2, fc)], W=[bank])

        h2_next = prep_block(0)
        for b in range(NBLK):
            h2 = h2_next
            if b == 0:
                load_w2(0)
                load_w2(1)
            prev = ff1(h2, 0)
            for fc in range(32):
                if b == 0 and fc + 2 < 32:
                    load_w2(fc + 2)
                nxt = ff1(h2, fc + 1) if fc + 1 < 32 else None
                ff2(prev, fc)
                prev = nxt
            if b + 1 < NBLK:
                h2_next = prep_block(b + 1)
            for tl in range(2):
                t = b * 2 + tl
                y = yr.next()
                xt = xr.next()
                dma(xt[:], out_d[t * 128:(t + 1) * 128, :], W=[xt])
                for hf in range(2):
                    copy("act", y[:, hf * 512:(hf + 1) * 512], acc_banks[tl * 2 + hf][:, :], R=[acc_banks[tl * 2 + hf]], W=[y])
                norm_residual(y, LNP[:, :], LNP, xt, ssr.next(), rsr.next(), hb1)
                final_ops.append(dma(out_d[t * 128:(t + 1) * 128, :], y[:, :], R=[y], eng="pool"))
        P.emit(final_ops=final_ops)
    return nc


def host_inputs(inputs):
    f = lambda a: np.ascontiguousarray(np.asarray(a, dtype=np.float32))
    rel_bias = f(inputs["rel_bias"])
    tbl, amask = _attn_tables(rel_bias)
    vec = np.zeros((128, 8 + 8 + 1 + 120 + 32), np.float32)
    vec[:, 0:8] = f(inputs["ln_mix_pre"])[0].reshape(8, 128).T
    vec[:, 8:16] = f(inputs["ln_mlp_pre"])[0].reshape(8, 128).T
    vec[:, 16] = f(inputs["norm_a"])[0]
    cw = f(inputs["conv_w"])[0]
    vec[:, 17:137] = cw.reshape(5, 24, 128).transpose(2, 1, 0).reshape(128, 120)
    vec[:, 137:145] = f(inputs["a_log_f"])[0][None, :]
    vec[:, 145:153] = f(inputs["a_log_b"])[0][None, :]
    vec[:, 153:161] = f(inputs["dt_bias_f"])[0][None, :]
    vec[:, 161:169] = f(inputs["dt_bias_b"])[0][None, :]
    lnpost = np.stack([np.broadcast_to(f(inputs["ln_mix_post"])[0], (128, D)),
                       np.broadcast_to(f(inputs["ln_mlp_post"])[0], (128, D))], axis=1)
    shared = {
        "w_in": f(inputs["w_in"])[0],
        "w_branch_a": f(inputs["w_branch_a"])[0],
        "w_branch_b": f(inputs["w_branch_b"])[0],
        "w_out": f(inputs["w_out"])[0],
        "w_ff1": f(inputs["w_ff1"])[0],
        "w_ff2": f(inputs["w_ff2"])[0],
        "consts": _consts(),
        "attn_tbl": tbl,
        "attn_mask": amask,
        "vecs": vec,
        "lnpost": np.ascontiguousarray(lnpost),
    }
    return shared


_NC_CACHE = {}


def kernel(**inputs):
    x = np.ascontiguousarray(np.asarray(inputs["x"], dtype=np.float32))
    B = x.shape[0]
    shared = host_inputs(inputs)
    if "nc" not in _NC_CACHE:
        _NC_CACHE["nc"] = build()
    nc = _NC_CACHE["nc"]
    in_maps = []
    for b in range(B):
        m = dict(shared)
        m["x"] = x[b]
        in_maps.append(m)
    res = run_bass_kernel_spmd(nc, in_maps, core_ids=list(range(B)))
    return np.stack([np.asarray(r["out"]) for r in res.results], axis=0).astype(np.float32)
```

```python
import math
from contextlib import ExitStack

import numpy as np
import concourse.bass as bass
import concourse.mybir as mybir
from concourse.bass_utils import run_bass_kernel_spmd

F32 = mybir.dt.float32
BF16 = mybir.dt.bfloat16
F32R = mybir.dt.float32r
AF = mybir.ActivationFunctionType
ALU = mybir.AluOpType

S = 2048
D = 1024
NT = S // 128
NH = 8
HG = 2
P2_BF16 = False
C_QA, C_KA, C_VA, C_ZA = 0, 1024, 2048, 3072
C_AF = 4096
C_QB = 4128
C_KB = C_QB + 1536
C_VB = C_KB + 1536
C_GA = C_VB + 1536
C_GB = C_GA + 1024
IN_COLS = C_GB + 1024
DFF = 4096
EPS = 1e-6
NEGBIG = -30000.0
DILS = (1, 4, 16)


class Trk:
    __slots__ = ("w", "r")

    def __init__(self):
        self.w = None
        self.r = []


class Tile:
    def __init__(self, ap, ntrk=1):
        self.ap = ap
        self.trk = [Trk() for _ in range(ntrk)]

    def __getitem__(self, k):
        return self.ap[k]


def _trks(xs):
    out = []
    for x in xs:
        if isinstance(x, Tile):
            out.extend(x.trk)
        elif isinstance(x, Trk):
            out.append(x)
        elif isinstance(x, tuple):
            t, i = x
            if isinstance(i, int):
                out.append(t.trk[i])
            else:
                out.extend(t.trk[i])
        elif isinstance(x, list):
            out.extend(_trks(x))
        else:
            raise TypeError(type(x))
    return out


class Op:
    __slots__ = ("eng", "fn", "deps", "sig", "cnt", "dma", "dsem", "dval", "dprev")

    def __init__(self, eng, fn, dma):
        self.eng = eng
        self.fn = fn
        self.deps = []
        self.sig = False
        self.cnt = 0
        self.dma = dma
        self.dsem = None
        self.dval = None
        self.dprev = 0


ENGS = ("pe", "act", "dve", "pool", "sp")
ATTACH_WAIT = True
HANDLE = {"pe": "tensor", "act": "scalar", "dve": "vector", "pool": "gpsimd", "sp": "sync"}


class Prog:
    def __init__(self, nc, n_dma_sems=32):
        self.nc = nc
        self.ops = {e: [] for e in ENGS}
        self.all_ops = []
        self.n_dma_sems = n_dma_sems
        self.pending_dma = []
        self.barrier_op = {e: None for e in ENGS}

    def op(self, eng, fn, R=(), W=(), dma=False):
        o = Op(eng, fn, dma)
        deps = []
        rt = _trks(R)
        wt = _trks(W)
        raw = set()
        for tr in rt:
            if tr.w is not None:
                deps.append(tr.w)
                raw.add(id(tr.w))
        for tr in wt:
            if tr.w is not None:
                deps.append(tr.w)
            deps.extend(tr.r)
        seen = set()
        d2 = []
        for d in deps:
            if id(d) not in seen and d is not o:
                seen.add(id(d))
                d2.append(d)
        o.deps = d2
        for tr in rt:
            tr.r.append(o)
        for tr in wt:
            tr.w = o
            tr.r = []
        self.ops[eng].append(o)
        self.all_ops.append(o)
        if dma:
            self.pending_dma.append(o)
        return o

    def barrier(self):
        lasts = []
        for e in ("pe", "act", "dve", "pool"):
            for o in reversed(self.ops[e]):
                if not o.dma and o.fn is not None:
                    lasts.append(o)
                    break
        deps = lasts + list(self.pending_dma)
        self.pending_dma = []
        for e in ENGS:
            b = Op(e, None, False)
            b.deps = [d for d in deps]
            self.ops[e].append(b)
            self.all_ops.append(b)

    def emit(self, final_ops=()):
        nc = self.nc
        for o in self.all_ops:
            nd = []
            for d in o.deps:
                if d.dma:
                    nd.append(d)
                    continue
                if d.eng == o.eng and not o.dma:
                    if o.eng == "pe":
                        continue
                    if o.fn is None:
                        continue
                nd.append(d)
            o.deps = nd
            for d in nd:
                d.sig = True
        for o in final_ops:
            o.sig = True
        for e in ENGS:
            c = 0
            for o in self.ops[e]:
                if o.dma or o.fn is None:
                    o.cnt = c
                    continue
                if o.sig:
                    c += 1
                o.cnt = c
        dvals = [0] * self.n_dma_sems
        rr = 0
        for o in self.all_ops:
            if o.dma:
                i = rr % self.n_dma_sems
                rr += 1
                o.dsem = i
                o.dprev = dvals[i]
                dvals[i] += 16
                o.dval = dvals[i]
        with ExitStack() as es:
            sems = {e: es.enter_context(nc.semaphore("s_" + e)) for e in ("pe", "act", "dve", "pool")}
            dsems = [es.enter_context(nc.semaphore("dm%d" % i)) for i in range(self.n_dma_sems)]
            block = es.enter_context(nc.Block())

            def run_engine(ename, eng):
                known = {}

                def wait(key, sem, val):
                    if val <= 0 or known.get(key, 0) >= val:
                        return
                    eng.wait_ge(sem, val)
                    known[key] = val

                def wait_op(d):
                    if d.dma:
                        wait(("d", d.dsem), dsems[d.dsem], d.dval)
                    else:
                        wait(d.eng, sems[d.eng], d.cnt)

                def need(d):
                    if d.dma:
                        key, sem, val = ("d", d.dsem), dsems[d.dsem], d.dval
                    else:
                        key, sem, val = d.eng, sems[d.eng], d.cnt
                    if val <= 0 or known.get(key, 0) >= val:
                        return None
                    return key, sem, val

                for o in self.ops[ename]:
                    if o.fn is None or o.dma or not ATTACH_WAIT:
                        for d in o.deps:
                            wait_op(d)
                        if o.fn is None:
                            continue
                    if o.dma:
                        wait(("d", o.dsem), dsems[o.dsem], o.dprev)
                        ins = o.fn(eng)
                        ins.then_inc(dsems[o.dsem], 16)
                    else:
                        last = None
                        if ATTACH_WAIT:
                            pend = {}
                            for d in o.deps:
                                nd = need(d)
                                if nd is not None:
                                    k_, sem_, val_ = nd
                                    if k_ not in pend or pend[k_][1] < val_:
                                        pend[k_] = (sem_, val_)
                            items = list(pend.items())
                            for k_, (sem_, val_) in items[:-1]:
                                wait(k_, sem_, val_)
                            if items:
                                last = items[-1]
                        ins = o.fn(eng)
                        if last is not None:
                            k_, (sem_, val_) = last
                            ins._wait_ge(sem_, val_)
                            known[k_] = val_
                        if o.sig:
                            ins.then_inc(sems[ename], 1)
                if ename == "sp":
                    for o in final_ops:
                        wait_op(o)

            for ename in ENGS:
                def mk(ename):
                    def f(eng):
                        run_engine(ename, eng)
                    return f
                getattr(block, HANDLE[ename])(mk(ename))


class Arena:
    def __init__(self, handle, nwords, base=0):
        self.h = handle
        self.n = nwords
        self.top = base

    def mark(self):
        return self.top

    def release(self, m):
        self.top = m

    def alloc(self, shape, dt=F32, ntrk=1):
        free = int(np.prod(shape[1:]))
        words = free if dt in (F32, F32R) else (free + 1) // 2
        a = self.top
        self.top += words
        if self.top > self.n:
            raise MemoryError("arena overflow: need %d have %d" % (self.top, self.n))
        v = self.h[0:shape[0], a:a + words]
        if dt not in (F32, F32R):
            v = v.bitcast(dt)
            if words * 2 != free:
                v = v[:, 0:free]
        if len(shape) == 3:
            v = v.rearrange("p (a b) -> p a b", b=shape[2])
        elif len(shape) == 4:
            v = v.rearrange("p (a b c) -> p a b c", b=shape[2], c=shape[3])
        return Tile(v, ntrk)


class Ring:
    def __init__(self, tiles):
        self.tiles = tiles
        self.i = 0

    def next(self):
        t = self.tiles[self.i % len(self.tiles)]
        self.i += 1
        return t


def sst(start, n, step):
    return slice(start, start + (n - 1) * step + 1, step)


def bc(ap, shape, axis):
    return ap.unsqueeze(axis).to_broadcast(list(shape))


def _t5_bucket(rel):
    nb = 16
    ret = (rel > 0).astype(np.int32) * nb
    n = np.abs(rel)
    max_exact = nb // 2
    large = max_exact + (np.log(np.maximum(n, 1) / max_exact) / math.log(1024 / max_exact)
                         * (nb - max_exact)).astype(np.int32)
    large = np.minimum(large, nb - 1)
    return ret + np.where(n < max_exact, n, large).astype(np.int32)


def _consts():
    p = np.arange(128)[:, None]
    f = np.arange(128)[None, :]
    c = {}
    c["ident"] = np.eye(128, dtype=np.float32)
    c["ones"] = np.ones((128, 128), np.float32)
    c["tri0"] = (p <= f).astype(np.float32)
    c["tri1"] = (p >= f).astype(np.float32)
    c["maskT0"] = np.where(f >= p, 0.0, -1e9).astype(np.float32)
    c["maskT1"] = np.where(f <= p, 0.0, -1e9).astype(np.float32)
    c["maskS0"] = np.where(f < p, 0.0, -1e9).astype(np.float32)
    c["maskS1"] = np.where(f > p, 0.0, -1e9).astype(np.float32)
    c["negst0"] = np.where(f > p, -1.0, 0.0).astype(np.float32)
    c["negst1"] = np.where(f < p, -1.0, 0.0).astype(np.float32)
    return np.stack([c[k] for k in CONST_NAMES], axis=1)


CONST_NAMES = ["ident", "ones", "tri0", "tri1", "maskT0", "maskT1", "maskS0", "maskS1", "negst0", "negst1"]


def _attn_tables(rel_bias):
    p = np.arange(128)[:, None]
    f = np.arange(256)[None, :]
    off = p - f + 64
    band = np.abs(off) <= 64
    tbl = np.zeros((128, 12, 256), np.float32)
    for g, dil in enumerate(DILS):
        bidx = _t5_bucket(off * dil)
        for hh in range(4):
            h = g * 4 + hh
            tbl[:, h, :] = rel_bias[bidx, h]
    mask = np.where(band, 0.0, NEGBIG).astype(np.float32)
    return tbl, mask


def build(taps=(), stop=None):
    taps = set(taps)
    nc = bass.Bass("TRN2", target_bir_lowering=False)

    def din(name, shape):
        return nc.dram_tensor(name, list(shape), F32, kind="ExternalInput").ap()

    x_d = din("x", [S, D])
    w_in_d = din("w_in", [D, IN_COLS])
    wba_d = din("w_branch_a", [1024, D])
    wbb_d = din("w_branch_b", [512, D])
    wout_d = din("w_out", [D, D])
    wff1_d = din("w_ff1", [D, DFF])
    wff2_d = din("w_ff2", [DFF, D])
    consts_d = din("consts", [128, len(CONST_NAMES), 128])
    tbl_d = din("attn_tbl", [128, 12, 256])
    amask_d = din("attn_mask", [128, 256])
    vec_d = din("vecs", [128, 8 + 8 + 1 + 24 * 5 + 32])
    lnpost_d = din("lnpost", [128, 2, D])
    out_d = nc.dram_tensor("out", [S, D], F32, kind="ExternalOutput").ap()
    tap_d = {}

    def tapout(name, shape):
        tap_d[name] = nc.dram_tensor("tap_" + name, list(shape), F32, kind="ExternalOutput").ap()
        return tap_d[name]

    es = ExitStack()
    with es:
        ARENA_R_WORDS = 4096
        ARENA_WORDS = 48700 - ARENA_R_WORDS
        arena_h = es.enter_context(nc.sbuf_tensor("arena", [128, ARENA_WORDS], F32))
        A = Arena(arena_h, ARENA_WORDS)
        arena_r_h = es.enter_context(nc.sbuf_tensor("arena_r", [128, ARENA_R_WORDS], F32R))
        AR = Arena(arena_r_h, ARENA_R_WORDS)
        psum_f = [Tile(es.enter_context(nc.psum_tensor("ps%d" % i, [128, 512], F32))) for i in range(8)]
        psum_b = Tile(psum_f[7][:, :].bitcast(BF16).rearrange("p (a b) -> p a b", b=128))
        PSR = Ring(psum_f[0:7])
        P = Prog(nc)
        final_ops = []

        def dma(out_ap, in_ap, R=(), W=(), eng="sp"):
            return P.op(eng, lambda e: e.dma_start(out=out_ap, in_=in_ap), R=R, W=W, dma=True)

        def mm(out_ap, lhsT, rhs, start, stop, R, W):
            return P.op("pe", lambda e: e.matmul(out_ap, lhsT=lhsT, rhs=rhs, start=start, stop=stop), R=R, W=W)

        def transpose(out_ap, in_ap, ident_ap, R, W):
            return P.op("pe", lambda e: e.transpose(out=out_ap, in_=in_ap, identity=ident_ap), R=R, W=W)

        def act(out_ap, in_ap, func, R, W, **kw):
            return P.op("act", lambda e: e.activation(out=out_ap, in_=in_ap, func=func, **kw), R=R, W=W)

        def tt(eng, out_ap, in0, in1, op, R, W):
            return P.op(eng, lambda e: e.tensor_tensor(out=out_ap, in0=in0, in1=in1, op=op), R=R, W=W)

        def ts(eng, out_ap, in0, s1, s2, op0, op1, R, W):
            if s2 is None:
                return P.op(eng, lambda e: e.tensor_scalar(out=out_ap, in0=in0, scalar1=s1, scalar2=None, op0=op0), R=R, W=W)
            return P.op(eng, lambda e: e.tensor_scalar(out=out_ap, in0=in0, scalar1=s1, scalar2=s2, op0=op0, op1=op1), R=R, W=W)

        def stt(eng, out_ap, in0, scalar, in1, op0, op1, R, W):
            return P.op(eng, lambda e: e.scalar_tensor_tensor(out=out_ap, in0=in0, scalar=scalar, in1=in1, op0=op0, op1=op1), R=R, W=W)

        def copy(eng, out_ap, in_ap, R, W):
            if eng == "act":
                return P.op("act", lambda e: e.copy(out=out_ap, in_=in_ap), R=R, W=W)
            return P.op(eng, lambda e: e.tensor_copy(out=out_ap, in_=in_ap), R=R, W=W)

        def memset(eng, ap, val, W):
            return P.op(eng, lambda e: e.memset(ap, val), W=W)

        def tap(name, tile_ap, shape, R):
            if name in taps:
                d = tapout(name, shape)
                o = dma(d, tile_ap, R=R)
                final_ops.append(o)

        CONST = A.alloc([128, len(CONST_NAMES), 128])
        dma(CONST[:], consts_d, W=[CONST])
        cidx = {n: i for i, n in enumerate(CONST_NAMES)}

        def CK(name):
            return CONST[:, cidx[name], :]

        VEC = A.alloc([128, 8 + 8 + 1 + 120 + 32])
        dma(VEC[:], vec_d, W=[VEC])
        LNW1 = VEC[:, 0:8]
        LNW2 = VEC[:, 8:16]
        NORMA = VEC[:, 16:17]
        CW = VEC[:, 17:137].rearrange("p (t k) -> p t k", k=5)
        ALOG = [VEC[:, 137:145], VEC[:, 145:153]]
        DTB = [VEC[:, 153:161], VEC[:, 161:169]]
        CB16 = A.alloc([128, 2, 128], BF16)
        copy("dve", CB16[:, 0, :], CK("ident"), R=[CONST], W=[CB16])
        copy("dve", CB16[:, 1, :], CK("ones"), R=[CONST], W=[CB16])
        IDB = CB16[:, 0, :]
        ONESB = CB16[:, 1, :]

        consts_mark = A.mark()
        hT = A.alloc([128, 8, S], BF16, ntrk=NT)
        oaT = A.alloc([128, 8, S], BF16, ntrk=8)
        persist_mark = A.mark()

        def rms_rstd(src_ap, width, R, junk_tile, ss_tile, rstd_tile):
            act(junk_tile[:, 0:width], src_ap, AF.Square, R=R, W=[junk_tile, ss_tile], accum_out=ss_tile[:, 0:1])
            act(rstd_tile[:, 0:1], ss_tile[:, 0:1], AF.Sqrt, R=[ss_tile], W=[rstd_tile], scale=1.0 / width, bias=EPS)
            P.op("dve", lambda e: e.reciprocal(out=rstd_tile[:, 0:1], in_=rstd_tile[:, 0:1]), R=[rstd_tile], W=[rstd_tile])

        def norm_transpose(src_tile, lnw_ap, dstT, col0, dst_trk, junk, ss, rstd, hb, pb_t=None, defer=False):
            pb_t = psum_b if pb_t is None else pb_t
            rms_rstd(src_tile[:, :], D, [src_tile], junk, ss, rstd)
            ts("dve", hb[:, :], src_tile[:, :], rstd[:, 0:1], None, ALU.mult, None, R=[src_tile, rstd], W=[hb])
            for c in range(8):
                transpose(pb_t[:, c, :], hb[:, c * 128:(c + 1) * 128], IDB, R=[hb, CB16], W=[pb_t])

            def evac():
                tt("dve", dstT[:, :, col0:col0 + 128], pb_t[:, :, :], bc(lnw_ap, [128, 8, 128], 2), ALU.mult,
                   R=[pb_t, VEC], W=dst_trk)
            if defer:
                return evac
            evac()

        class WStream:
            def __init__(self, kc, n, nstage, nbf, cast_engs=("pool",)):
                self.kc, self.n = kc, n
                self.stage = Ring([A.alloc([128, kc, n]) for _ in range(nstage)])
                self.bf = Ring([A.alloc([128, kc, n], BF16) for _ in range(nbf)])
                self.cast_engs = cast_engs
                self.k = 0

            def load(self, dram_ap):
                st = self.stage.next()
                bf = self.bf.next()
                dma(st[:], dram_ap.rearrange("(c p) n -> p c n", p=128), W=[st])
                eng = self.cast_engs[self.k % len(self.cast_engs)]
                self.k += 1
                copy(eng, bf[:], st[:], R=[st], W=[bf])
                return bf

        mA = A.mark()
        xr = Ring([A.alloc([128, D]) for _ in range(3)])
        junk = A.alloc([128, D])
        hbr = Ring([A.alloc([128, D], BF16) for _ in range(2)])
        ssr = Ring([A.alloc([128, 1]) for _ in range(4)])
        rsr = Ring([A.alloc([128, 1]) for _ in range(4)])
        psum_b2 = Tile(psum_f[6][:, :].bitcast(BF16).rearrange("p (a b) -> p a b", b=128))
        pend = None
        for t in range(NT):
            xt = xr.next()
            dma(xt[:], x_d[t * 128:(t + 1) * 128, :], W=[xt])
            ev = norm_transpose(xt, LNW1, hT, t * 128, [(hT, t)], junk, ssr.next(), rsr.next(), hbr.next(),
                                pb_t=(psum_b, psum_b2)[t % 2], defer=True)
            if pend is not None:
                pend()
            pend = ev
        pend()
        if "hT" in taps:
            tmph = A.alloc([128, 8, S])
            copy("dve", tmph[:, :, :], hT[:, :, :], R=[hT], W=[tmph])
            tap("hT", tmph[:, :, :], [128, 8, S], [tmph])
        P.barrier()
        A.release(mA)
        if stop == "A":
            P.emit(final_ops=final_ops)
            return nc

        mB = A.mark()
        GB = A.alloc([128, 4, NT, 8])
        DEC = A.alloc([128, 10, NT, 8])
        mB0 = A.mark()
        wab = WStream(8, 32, 1, 1)
        wab_b = wab.load(w_in_d[:, C_AF:C_AF + 32])
        psab = PSR.next()
        psab_v = psab[:, :].rearrange("p (t c) -> p t c", c=32)
        for t in range(NT):
            for c in range(8):
                mm(psab_v[:, t, :], hT[:, c, t * 128:(t + 1) * 128], wab_b[:, c, :], c == 0, c == 7,
                   R=[(hT, t), wab_b], W=[psab])
        nea = A.alloc([128, 2, 8])
        for d in range(2):
            act(nea[:, d, :], ALOG[d], AF.Exp, R=[VEC], W=[nea])
        ts("dve", nea[:, :, :], nea[:, :, :], -1.0, None, ALU.mult, None, R=[nea], W=[nea])
        tmpab = A.alloc([128, NT, 8])
        for d in range(2):
            tt("dve", tmpab[:, :, :], psab_v[:, :, d * 8:(d + 1) * 8], bc(DTB[d], [128, NT, 8], 1), ALU.add,
               R=[psab, VEC], W=[tmpab])
            act(tmpab[:, :, :], tmpab[:, :, :], AF.Exp, R=[tmpab], W=[tmpab])
            act(tmpab[:, :, :], tmpab[:, :, :], AF.Ln, R=[tmpab], W=[tmpab], bias=1.0)
            tt("dve", GB[:, d, :, :], tmpab[:, :, :], bc(nea[:, d, :], [128, NT, 8], 1), ALU.mult,
               R=[tmpab, nea], W=[GB])
            act(GB[:, 2 + d, :, :], psab_v[:, :, 16 + d * 8:16 + (d + 1) * 8], AF.Sigmoid, R=[psab], W=[GB])
        if "gb" in taps:
            tap("gb", GB[:, :, :, :], [128, 4, NT, 8], [GB])
        for d in range(2):
            pgc = PSR.next()
            g_all = GB[:, d, :, :].rearrange("p t h -> p (t h)")
            mm(pgc[:, 0:128], CK("tri%d" % d), g_all, True, True, R=[CONST, GB], W=[pgc])
            mm(pgc[:, 128:256], CK("ones"), g_all, True, True, R=[CONST, GB], W=[pgc])
            dv = lambda q, d=d: DEC[:, d * 5 + q, :, :].rearrange("p t h -> p (t h)")
            copy("act", dv(0), pgc[:, 0:128], R=[pgc], W=[DEC])
            act(dv(1), pgc[:, 0:128], AF.Exp, R=[pgc], W=[DEC])
            tt("dve", dv(2), pgc[:, 128:256], dv(0), ALU.subtract, R=[pgc, DEC], W=[DEC])
            act(dv(2), dv(2), AF.Exp, R=[DEC], W=[DEC])
            act(dv(3), pgc[:, 128:256], AF.Exp, R=[pgc], W=[DEC])
            ts("dve", dv(4), GB[:, 2 + d, :, :].rearrange("p t h -> p (t h)"), -1.0, None, ALU.mult, None, R=[GB], W=[DEC])
        P.barrier()
        A.release(mB0)
        if stop == "B0":
            P.emit(final_ops=final_ops)
            return nc

        wq = None
        for ps_ in range(NH // HG):
            h0 = ps_ * HG
            A.release(mB0)
            qT = A.alloc([128, HG, S], BF16)
            kT = A.alloc([128, HG, S], BF16)
            ktok = A.alloc([128, NT, HG, 128], BF16)
            vtok = A.alloc([128, NT, HG, 128])
            oT = A.alloc([128, HG, S])
            memset("pool", oT[:, :, :], 0.0, W=[oT])
            mB1 = A.mark()
            wst = WStream(8, 128, 1, 1, cast_engs=("act",))
            prawr = Ring([A.alloc([128, S + 4]) for _ in range(2)])
            accr = Ring([A.alloc([128, S]) for _ in range(2)])
            rnbr = Ring([A.alloc([128, 512]) for _ in range(1)])
            for pr in prawr.tiles:
                memset("pool", pr[:, 0:2], 0.0, W=[pr])
                memset("pool", pr[:, S + 2:S + 4], 0.0, W=[pr])
            units = [(hi, which) for hi in range(HG) for which in range(3)]

            wb_of = {}

            def st_W(u):
                hi, which = units[u]
                col0 = which * 1024 + (h0 + hi) * 128
                wb_of[u] = wst.load(w_in_d[:, col0:col0 + 128])

            def st_P(u):
                hi, which = units[u]
                wb = wb_of[u]
                praw = prawr.tiles[u % 2]
                for tb in range(4):
                    pst = PSR.next()
                    for c in range(8):
                        mm(pst[:, :], wb[:, c, :], hT[:, c, tb * 512:(tb + 1) * 512], c == 0, c == 7,
                           R=[wb, (hT, slice(tb * 4, tb * 4 + 4))], W=[pst])
                    copy("dve", praw[:, 2 + tb * 512:2 + (tb + 1) * 512], pst[:, :], R=[pst], W=[praw])

            def st_C(u):
                hi, which = units[u]
                h = h0 + hi
                praw = prawr.tiles[u % 2]
                acc = accr.tiles[u % 2]
                ctile = which * 8 + h
                ts("dve", acc[:, :], praw[:, 0:S], CW[:, ctile, 0:1], None, ALU.mult, None, R=[praw, VEC], W=[acc])
                for k in range(1, 5):
                    stt("dve", acc[:, :], praw[:, k:k + S], CW[:, ctile, k:k + 1], acc[:, :], ALU.mult, ALU.add,
                        R=[praw, VEC, acc], W=[acc])
                act(acc[:, :], acc[:, :], AF.Silu, R=[acc], W=[acc])
                if which < 2:
                    act(praw[:, 2:2 + S], acc[:, :], AF.Square, R=[acc, praw], W=[praw])

            n_ps = {}

            def st_Na(u):
                hi, which = units[u]
                if which == 2:
                    return
                praw = prawr.tiles[u % 2]
                lst = []
                for tb in range(4):
                    pst = PSR.next()
                    mm(pst[:, :], CK("ones"), praw[:, 2 + tb * 512:2 + (tb + 1) * 512], True, True, R=[CONST, praw], W=[pst])
                    act(pst[:, :], pst[:, :], AF.Ln, R=[pst], W=[pst], bias=EPS)
                    act(pst[:, :], pst[:, :], AF.Exp, R=[pst], W=[pst], scale=-0.5)
                    lst.append(pst)
                n_ps[u] = lst

            def st_Nb(u):
                hi, which = units[u]
                if which == 2:
                    return
                acc = accr.tiles[u % 2]
                for tb in range(4):
                    bs = slice(tb * 512, (tb + 1) * 512)
                    rnb = n_ps[u][tb]
                    if which == 0:
                        stt("dve", qT[:, hi, bs], acc[:, bs], 128.0 ** -0.5, rnb[:, :], ALU.mult, ALU.mult,
                            R=[acc, rnb], W=[qT])
                    else:
                        tt("dve", kT[:, hi, bs], acc[:, bs], rnb[:, :], ALU.mult, R=[acc, rnb], W=[kT])

            def st_T(u):
                hi, which = units[u]
                if which == 0:
                    return
                acc = accr.tiles[u % 2]
                if which == 1:
                    for th_ in range(2):
                        for j in range(8):
                            t = th_ * 8 + j
                            transpose(psum_b[:, j, :], kT[:, hi, t * 128:(t + 1) * 128], IDB, R=[kT, CB16], W=[psum_b])
                        copy("act", ktok[:, th_ * 8:(th_ + 1) * 8, hi, :], psum_b[:, :, :], R=[psum_b], W=[ktok])
                    return
                dst = vtok
                for tq in range(4):
                    pst = PSR.next()
                    pv = pst[:, :].rearrange("p (a b) -> p a b", b=128)
                    for j in range(4):
                        t = tq * 4 + j
                        transpose(pv[:, j, :], acc[:, t * 128:(t + 1) * 128], CK("ident"), R=[acc, CONST], W=[pst])
                    copy("act", dst[:, tq * 4:(tq + 1) * 4, hi, :], pv[:, :, :], R=[pst], W=[dst])

            nu = len(units)
            for it in range(nu + 3):
                if it < nu:
                    st_W(it)
                if 0 <= it - 3 < nu:
                    st_T(it - 3)
                if 0 <= it - 2 < nu:
                    st_Na(it - 2)
                if 0 <= it - 1 < nu:
                    st_C(it - 1)
                if 0 <= it - 2 < nu:
                    st_Nb(it - 2)
                if it < nu:
                    st_P(it)
            P.barrier()
            A.release(mB1)
            if ps_ == 0:
                if "qT" in taps:
                    tmpq = A.alloc([128, HG, S])
                    copy("dve", tmpq[:, :, :], qT[:, :, :], R=[qT], W=[tmpq])
                    tap("qT", tmpq[:, :, :], [128, HG, S], [tmpq])
                    tmpk = A.alloc([128, HG, S])
                    copy("dve", tmpk[:, :, :], kT[:, :, :], R=[kT], W=[tmpk])
                    tap("kT", tmpk[:, :, :], [128, HG, S], [tmpk])
                    P.barrier()
                    A.release(mB1)
                tap("ktok", ktok[:, :, :, :], [128, NT, HG, 128], [ktok])
                tap("vtok", vtok[:, :, :, :], [128, NT, HG, 128], [vtok])
            if stop == "B1":
                P.emit(final_ops=final_ops)
                return nc

            W_ = HG * 128
            Sst = [A.alloc([128, HG, 128]) for _ in range(2)]
            Sbf = [A.alloc([128, HG, 128], BF16) for _ in range(2)]
            for d in range(2):
                memset("pool", Sst[d][:, :, :], 0.0, W=[Sst[d]])
                memset("pool", Sbf[d][:, :, :], 0.0, W=[Sbf[d]])

            SCAN_F32 = ("aqkT", "qgT", "kdec", "wT", "vnew")
            SCAN_BF = ()
            def mkjob(zero_pad=True):
                j = {}
                j["bcr_gc"] = A.alloc([128, HG, 128])
                j["bcr_e"] = A.alloc([128, HG, 128], BF16)
                for n in ("t0", "gamT", "gamS", "Mf", "Dm", "tmpS", "TDT", "vnew"):
                    j[n] = A.alloc([128, HG, 128])
                j["vnb"] = A.alloc([128, HG, 128], BF16)
                for n in ("kgT", "kdec"):
                    j[n + "2"] = [A.alloc([128, HG, 128]) for _ in range(2)]
                for n in ("aqkT", "qgT"):
                    j[n + "2"] = [A.alloc([128, HG, 128], BF16 if P2_BF16 else F32) for _ in range(2)]
                j["NP"] = [AR.alloc([128, HG, 2, 128], F32R) for _ in range(2)]
                j["Mm"] = [AR.alloc([128, HG, 2, 128], F32R) for _ in range(2)]
                for m_ in (j["Mm"] if zero_pad else []):
                    ts("pool", m_[:, :, 1, :], bc(CK("ident"), [128, HG, 128], 1), 0.0, None, ALU.mult, None, R=[CONST], W=[m_])
                return j

            AR.release(0)
            jobs = [[mkjob(ps_ == 0) for _ in range(1)] for _ in range(2)]
            hs = slice(h0, h0 + HG)

            def act_mul_heads(out3, in3, scal2, R, W):
                for hi_ in range(HG):
                    P.op("act", lambda e, hi_=hi_: e.mul(out=out3[:, hi_, :], in_=in3[:, hi_, :], mul=scal2[:, hi_:hi_ + 1]),
                         R=R, W=W)

            def stage1_phases(j, d, c, par):
                cs = slice(c * 128, (c + 1) * 128)
                gc_ap = DEC[:, d * 5 + 0, c, hs]
                egc_ap = DEC[:, d * 5 + 1, c, hs]
                edec_ap = DEC[:, d * 5 + 2, c, hs]
                negb_ap = DEC[:, d * 5 + 4, c, hs]
                st = {}
                PSR = S1R
                jkgT, jkdec, jqgT, jaqkT = (j[n_ + "2"][par] for n_ in ("kgT", "kdec", "qgT", "aqkT"))
                t0, gamT, gamS, Mf = j["t0"], j["gamT"], j["gamS"], j["Mf"]
                NP0, Mm0 = j["NP"][0], j["Mm"][0]

                def pa_():
                    tt("dve", j["bcr_gc"][:, :, :], bc(CK("ident"), [128, HG, 128], 1), bc(gc_ap, [128, HG, 128], 2), ALU.mult,
                       R=[CONST, DEC], W=[j["bcr_gc"]])
                    tt("dve", j["bcr_e"][:, :, :], bc(CK("ident"), [128, HG, 128], 1), bc(egc_ap, [128, HG, 128], 2), ALU.mult,
                       R=[CONST, DEC], W=[j["bcr_e"]])
                    act_mul_heads(jkdec, ktok[:, c, :, :], edec_ap, R=[ktok, DEC], W=[jkdec])

                def pb_():
                    pbc = PSR.next()
                    pkq = PSR.next()
                    st["pbc"], st["pkq"] = pbc, pkq
                    mm(pbc[:, 0:W_], CK("ones"), j["bcr_gc"][:, :, :].rearrange("p a b -> p (a b)"), True, True,
                       R=[CONST, j["bcr_gc"]], W=[pbc])
                    mm(pbc[:, W_:2 * W_], ONESB, j["bcr_e"][:, :, :].rearrange("p a b -> p (a b)"), True, True,
                       R=[CB16, j["bcr_e"]], W=[pbc])
                    pkq_v = pkq[:, :].rearrange("p (t h i) -> p t h i", t=2, i=128)
                    for hi in range(HG):
                        mm(pkq_v[:, 0, hi, :], kT[:, hi, cs], kT[:, hi, cs], True, True, R=[kT], W=[pkq])
                        mm(pkq_v[:, 1, hi, :], kT[:, hi, cs], qT[:, hi, cs], True, True, R=[kT, qT], W=[pkq])

                def pc_():
                    pbc = st["pbc"]
                    bc_gc = pbc[:, 0:W_].rearrange("p (h i) -> p h i", i=128)
                    tt("dve", t0[:, :, :], bc_gc, bc(gc_ap, [128, HG, 128], 2), ALU.subtract, R=[pbc, DEC], W=[t0])
                    tt("dve", gamT[:, :, :], t0[:, :, :], bc(CK("maskT%d" % d), [128, HG, 128], 1), ALU.add, R=[t0, CONST], W=[gamT])
                    tt("dve", gamS[:, :, :], bc(CK("maskS%d" % d), [128, HG, 128], 1), t0[:, :, :], ALU.subtract, R=[t0, CONST], W=[gamS])

                def pd_():
                    pbc = st["pbc"]
                    bc_egc = pbc[:, W_:2 * W_].rearrange("p (h i) -> p h i", i=128)
                    act(gamS[:, :, :], gamS[:, :, :], AF.Exp, R=[gamS], W=[gamS])
                    act(gamT[:, :, :], gamT[:, :, :], AF.Exp, R=[gamT], W=[gamT])
                    tt("dve", jqgT[:, :, :], qT[:, :, cs], bc_egc, ALU.mult, R=[qT, pbc], W=[jqgT])
                    tt("dve", jkgT[:, :, :], kT[:, :, cs], bc_egc, ALU.mult, R=[kT, pbc], W=[jkgT])
                    copy("act", NP0[:, :, 1, :], bc(CK("ident"), [128, HG, 128], 1), R=[CONST], W=[NP0])

                def pe_():
                    pkq = st["pkq"]
                    pkq_v = pkq[:, :].rearrange("p (t h i) -> p t h i", t=2, i=128)
                    for hi_ in range(HG):
                        stt("dve", Mf[:, hi_, :], pkq_v[:, 0, hi_, :], negb_ap[:, hi_:hi_ + 1], gamS[:, hi_, :], ALU.mult, ALU.mult,
                            R=[pkq, DEC, gamS], W=[Mf])
                    tt("dve", jaqkT[:, :, :], pkq_v[:, 1, :, :], gamT[:, :, :], ALU.mult, R=[pkq, gamT], W=[jaqkT])

                def pf_():
                    pT = PSR.next()
                    st["pT"] = pT
                    pT_v = pT[:, 0:W_].rearrange("p (h i) -> p h i", i=128)
                    for hi in range(HG):
                        transpose(pT_v[:, hi, :], Mf[:, hi, :], CK("ident"), R=[Mf, CONST], W=[pT])
                    copy("act", Mm0[:, :, 0, :], Mf[:, :, :], R=[Mf], W=[Mm0])

                def pg_():
                    pT = st["pT"]
                    pT_v = pT[:, 0:W_].rearrange("p (h i) -> p h i", i=128)
                    copy("act", NP0[:, :, 0, :], pT_v, R=[pT], W=[NP0])

                return [pa_, pb_, pc_, pd_, pe_, pf_, pg_]

            def solve_level(j, lvl, d):
                cur = lvl % 2
                nxt = 1 - cur
                NPc, NPn = j["NP"][cur], j["NP"][nxt]
                Mc, Mn = j["Mm"][cur], j["Mm"][nxt]
                last = (lvl == 6)
                PSR = LR
                pa = PSR.next()
                pa_v = pa[:, 0:2 * W_].rearrange("p (h t i) -> p h t i", t=2, i=128)
                for hi in range(HG):
                    mm(pa[:, hi * 256:(hi + 1) * 256], Mc[:, hi, 0, :], NPc[:, hi, :, :].rearrange("p t i -> p (t i)"), True, True,
                       R=[Mc, NPc], W=[pa])
                if not last:
                    pb = PSR.next()
                    pb_v = pb[:, 0:2 * W_].rearrange("p (h t i) -> p h t i", t=2, i=128)
                    for hi in range(HG):
                        mm(pb[:, hi * 256:(hi + 1) * 256], NPc[:, hi, 0, :], Mc[:, hi, :, :].rearrange("p t i -> p (t i)"), True, True,
                           R=[Mc, NPc], W=[pb])
                    copy("act", NPn[:, :, 0, :], pa_v[:, :, 0, :], R=[pa], W=[NPn])
                    copy("act", Mn[:, :, 0, :], pb_v[:, :, 0, :], R=[pb], W=[Mn])
                tt("dve", NPn[:, :, 1, :], pa_v[:, :, 1, :], NPc[:, :, 1, :].bitcast(F32), ALU.add, R=[pa, NPc], W=[NPn])

            def finish_scan_phases(j, d, c, par):
                cs = slice(c * 128, (c + 1) * 128)
                PSR = LR
                St = Sst[d]
                NPf = j["NP"][1]
                TDT = j["TDT"]
                jkgT, jkdec, jqgT, jaqkT = (j[n_ + "2"][par] for n_ in ("kgT", "kdec", "qgT", "aqkT"))
                Sb = Sbf[d]
                st = {}

                def f1():
                    act_mul_heads(TDT, NPf[:, :, 1, :].bitcast(F32), GB[:, 2 + d, c, hs], R=[NPf, GB], W=[TDT])

                def f2():
                    pz = PSR.next()
                    st["pz"] = pz
                    pz_v = pz[:, 0:W_].rearrange("p (h i) -> p h i", i=128)
                    for hi in range(HG):
                        mm(pz_v[:, hi, :], jkgT[:, hi, :], St[:, hi, :], True, True, R=[jkgT, St], W=[pz])

                def f3():
                    pz_v = st["pz"][:, 0:W_].rearrange("p (h i) -> p h i", i=128)
                    tt("dve", j["Dm"][:, :, :], vtok[:, c, :, :], pz_v, ALU.subtract, R=[vtok, st["pz"]], W=[j["Dm"]])

                def s1():
                    p1 = PSR.next()
                    st["p1"] = p1
                    p1_v = p1[:, 0:W_].rearrange("p (h i) -> p h i", i=128)
                    for hi in range(HG):
                        mm(p1_v[:, hi, :], TDT[:, hi, :], j["Dm"][:, hi, :], True, True, R=[TDT, j["Dm"]], W=[p1])

                def s2():
                    p1_v = st["p1"][:, 0:W_].rearrange("p (h i) -> p h i", i=128)
                    copy("act", j["vnew"][:, :, :], p1_v, R=[st["p1"]], W=[j["vnew"]])
                    if P2_BF16:
                        copy("dve", j["vnb"][:, :, :], p1_v, R=[st["p1"]], W=[j["vnb"]])

                def s3():
                    p2 = PSR.next()
                    p3 = PSR.next()
                    st["p2"], st["p3"] = p2, p3
                    p2_v = p2[:, 0:W_].rearrange("p (h i) -> p h i", i=128)
                    p3_v = p3[:, 0:W_].rearrange("p (h i) -> p h i", i=128)
                    for hi in range(HG):
                        if P2_BF16:
                            mm(p2_v[:, hi, :], Sb[:, hi, :], jqgT[:, hi, :], True, False, R=[Sb, jqgT], W=[p2])
                            mm(p2_v[:, hi, :], j["vnb"][:, hi, :], jaqkT[:, hi, :], False, True, R=[j["vnb"], jaqkT], W=[p2])
                        else:
                            mm(p2_v[:, hi, :], St[:, hi, :], jqgT[:, hi, :], True, False, R=[St, jqgT], W=[p2])
                            mm(p2_v[:, hi, :], j["vnew"][:, hi, :], jaqkT[:, hi, :], False, True, R=[j["vnew"], jaqkT], W=[p2])
                    for hi in range(HG):
                        mm(p3_v[:, hi, :], jkdec[:, hi, :], j["vnew"][:, hi, :], True, True, R=[jkdec, j["vnew"]], W=[p3])

                def s4():
                    p2_v = st["p2"][:, 0:W_].rearrange("p (h i) -> p h i", i=128)
                    p3_v = st["p3"][:, 0:W_].rearrange("p (h i) -> p h i", i=128)
                    for hi_ in range(HG):
                        egl_ap = DEC[:, d * 5 + 3, c, h0 + hi_:h0 + hi_ + 1]
                        stt("dve", St[:, hi_, :], St[:, hi_, :], egl_ap, p3_v[:, hi_, :], ALU.mult, ALU.add,
                            R=[St, DEC, st["p3"]], W=[St])
                    if P2_BF16:
                        copy("act", Sb[:, :, :], St[:, :, :], R=[St], W=[Sb])
                    tt("dve", oT[:, :, cs], oT[:, :, cs], p2_v, ALU.add, R=[oT, st["p2"]], W=[oT])

                return [f1, f2, f3, s1, s2, s3, s4]

            LR = Ring(psum_f[0:4])
            S1R = Ring(psum_f[4:8])
            js = [jobs[0][0], jobs[1][0]]
            ph0 = [stage1_phases(js[d], d, [0, NT - 1][d], 0) for d in range(2)]
            for k in range(len(ph0[0])):
                for d in range(2):
                    ph0[d][k]()
            for step in range(NT):
                cc = [step, NT - 1 - step]
                par = step % 2
                for lvl in range(7):
                    for d in range(2):
                        solve_level(js[d], lvl, d)
                fs = [finish_scan_phases(js[d], d, cc[d], par) for d in range(2)]
                fs_seq = [fs[d][k] for k in range(len(fs[0])) for d in range(2)]
                s1_seq = []
                if step + 1 < NT:
                    cn = [step + 1, NT - 2 - step]
                    phn = [stage1_phases(js[d], d, cn[d], 1 - par) for d in range(2)]
                    s1_seq = [phn[d][k] for k in range(len(phn[0])) for d in range(2)]
                for k in range(max(len(fs_seq), len(s1_seq))):
                    if k < len(fs_seq):
                        fs_seq[k]()
                    if k < len(s1_seq):
                        s1_seq[k]()
            if ps_ == 0:
                tap("oT", oT[:, :, :], [128, HG, S], [oT])
            P.barrier()
            A.release(mB1)
            if stop == "B2":
                P.emit(final_ops=final_ops)
                return nc

            wz = WStream(8, 128, 2, 2)
            sq = A.alloc([128, S])
            rn = A.alloc([128, S])
            zs = A.alloc([128, S])
            for hi in range(HG):
                h = h0 + hi
                wb = wz.load(w_in_d[:, C_ZA + h * 128:C_ZA + (h + 1) * 128])
                for tb in range(4):
                    pst = PSR.next()
                    for c in range(8):
                        mm(pst[:, :], wb[:, c, :], hT[:, c, tb * 512:(tb + 1) * 512], c == 0, c == 7,
                           R=[wb, (hT, slice(tb * 4, tb * 4 + 4))], W=[pst])
                    act(zs[:, tb * 512:(tb + 1) * 512], pst[:, :], AF.Silu, R=[pst], W=[zs])
                tt("pool", sq[:, :], oT[:, hi, :], oT[:, hi, :], ALU.mult, R=[oT], W=[sq])
                for tb in range(4):
                    pst = PSR.next()
                    mm(pst[:, :], CK("ones"), sq[:, tb * 512:(tb + 1) * 512], True, True, R=[CONST, sq], W=[pst])
                    act(rn[:, tb * 512:(tb + 1) * 512], pst[:, :], AF.Ln, R=[pst], W=[rn], scale=1.0 / 128, bias=EPS)
                act(rn[:, :], rn[:, :], AF.Exp, R=[rn], W=[rn], scale=-0.5)
                stt("dve", zs[:, :], zs[:, :], NORMA, rn[:, :], ALU.mult, ALU.mult, R=[zs, VEC, rn], W=[zs])
                tt("dve", oaT[:, h, :], oT[:, hi, :], zs[:, :], ALU.mult, R=[oT, zs], W=[(oaT, h)])
            P.barrier()
        if "oaT" in taps:
            A.release(mB0)
            tmpo = A.alloc([128, 8, S])
            copy("dve", tmpo[:, :, :], oaT[:, :, :], R=[oaT], W=[tmpo])
            tap("oaT", tmpo[:, :, :], [128, 8, S], [tmpo])
            P.barrier()
        A.release(mB)
        if stop == "B":
            P.emit(final_ops=final_ops)
            return nc

        obT = A.alloc([128, 4, S], BF16, ntrk=4)
        mC = A.mark()
        TBL = A.alloc([128, 12, 256], BF16)
        tblst = A.alloc([128, 12, 256])
        amask = A.alloc([128, 256])
        dma(tblst[:], tbl_d, W=[tblst])
        dma(amask[:], amask_d, W=[amask])
        tt("pool", TBL[:, :, :], tblst[:, :, :], bc(amask[:, :], [128, 12, 256], 1), ALU.add, R=[tblst, amask], W=[TBL])
        acc_o = A.alloc([128, S])
        acc_d = A.alloc([128, S])
        wqk = WStream(8, 128, 2, 3)
        QTr = Ring([A.alloc([128, S], BF16) for _ in range(2)])
        KTr = Ring([A.alloc([128, S], BF16) for _ in range(2)])
        Vr = Ring([A.alloc([128, NT, 128], BF16) for _ in range(2)])
        ETr = Ring([A.alloc([128, 256], BF16) for _ in range(3)])
        rden = A.alloc([128, S])
        for hh in range(4):
            memset("pool", acc_o[:, :], 0.0, W=[acc_o])
            memset("pool", acc_d[:, :], 0.0, W=[acc_d])
            for g, dil in enumerate(DILS):
                h = g * 4 + hh
                L = S // dil
                QT = QTr.next()
                KT = KTr.next()
                V = Vr.next()
                wq_b = wqk.load(w_in_d[:, C_QB + h * 128:C_QB + (h + 1) * 128])
                wk_b = wqk.load(w_in_d[:, C_KB + h * 128:C_KB + (h + 1) * 128])
                wv_b = wqk.load(w_in_d[:, C_VB + h * 128:C_VB + (h + 1) * 128])
                for tb in range(4):
                    pst = PSR.next()
                    for c in range(8):
                        mm(pst[:, :], wq_b[:, c, :], hT[:, c, tb * 512:(tb + 1) * 512], c == 0, c == 7,
                           R=[wq_b, (hT, slice(tb * 4, tb * 4 + 4))], W=[pst])
                    P.op("act", lambda e, QT=QT, pst=pst, tb=tb: e.mul(out=QT[:, tb * 512:(tb + 1) * 512], in_=pst[:, :], mul=128.0 ** -0.5),
                         R=[pst], W=[QT])
                    pst = PSR.next()
                    for c in range(8):
                        mm(pst[:, :], wk_b[:, c, :], hT[:, c, tb * 512:(tb + 1) * 512], c == 0, c == 7,
                           R=[wk_b, (hT, slice(tb * 4, tb * 4 + 4))], W=[pst])
                    copy("dve", KT[:, tb * 512:(tb + 1) * 512], pst[:, :], R=[pst], W=[KT])
                ntile_r = L // 128
                for tq in range(4):
                    pst = PSR.next()
                    pv = pst[:, :].rearrange("p (a b) -> p a b", b=128)
                    for jj in range(4):
                        ti = tq * 4 + jj
                        r, jt = ti // ntile_r, ti % ntile_r
                        t0_ = jt * 128 * dil + r
                        for c in range(8):
                            mm(pv[:, jj, :], hT[:, c, sst(t0_, 128, dil)], wv_b[:, c, :], c == 0, c == 7,
                               R=[wv_b, hT], W=[pst])
                    copy("act", V[:, tq * 4:(tq + 1) * 4, :], pv[:, :, :], R=[pst], W=[V])
                tiles_ = []
                for r in range(dil):
                    for kt in range(ntile_r):
                        ti = r * ntile_r + kt
                        k0 = kt * 128
                        qlo = max(0, k0 - 64)
                        qhi = min(L, k0 + 192)
                        nq = qhi - qlo
                        f0 = qlo - (k0 - 64)
                        ks = sst(k0 * dil + r, 128, dil)
                        qs = sst(qlo * dil + r, nq, dil)
                        tiles_.append((ti, nq, f0, ks, qs))

                def att1(td, QT=QT, KT=KT, h=h):
                    ti, nq, f0, ks, qs = td
                    pS = PSR.next()
                    mm(pS[:, 0:nq], KT[:, ks], QT[:, qs], True, False, R=[KT, QT], W=[pS])
                    mm(pS[:, 0:nq], IDB, TBL[:, h, f0:f0 + nq], False, True, R=[CB16, TBL], W=[pS])
                    ET = ETr.next()
                    act(ET[:, 0:nq], pS[:, 0:nq], AF.Exp, R=[pS], W=[ET])
                    return ET

                def att2(td, ET, V=V):
                    ti, nq, f0, ks, qs = td
                    pO = PSR.next()
                    mm(pO[:, 0:nq], V[:, ti, :], ET[:, 0:nq], True, True, R=[V, ET], W=[pO])
                    mm(pO[:, 256:256 + nq], ONESB, ET[:, 0:nq], True, True, R=[CB16, ET], W=[pO])
                    tt("dve", acc_o[:, qs], acc_o[:, qs], pO[:, 0:nq], ALU.add, R=[acc_o, pO], W=[acc_o])
                    tt("dve", acc_d[:, qs], acc_d[:, qs], pO[:, 256:256 + nq], ALU.add, R=[acc_d, pO], W=[acc_d])

                et_prev = att1(tiles_[0])
                for i_ in range(len(tiles_)):
                    et_next = att1(tiles_[i_ + 1]) if i_ + 1 < len(tiles_) else None
                    att2(tiles_[i_], et_prev)
                    et_prev = et_next
            act(rden[:, :], acc_d[:, :], AF.Ln, R=[acc_d], W=[rden])
            act(rden[:, :], rden[:, :], AF.Exp, R=[rden], W=[rden], scale=-1.0)
            tt("dve", obT[:, hh, :], acc_o[:, :], rden[:, :], ALU.mult, R=[acc_o, rden], W=[(obT, hh)])
        if "obT" in taps:
            P.barrier()
            A.release(mC)
            tmpo = A.alloc([128, 4, S])
            copy("dve", tmpo[:, :, :], obT[:, :, :], R=[obT], W=[tmpo])
            tap("obT", tmpo[:, :, :], [128, 4, S], [tmpo])
        P.barrier()
        A.release(mC)
        if stop == "C":
            P.emit(final_ops=final_ops)
            return nc

        mD = A.mark()
        mergedT_lo = A.top
        mergedT = A.alloc([128, 8, S], BF16)
        mergedT_hi = A.top
        wg8 = WStream(8, 128, 3, 6, cast_engs=("act",))
        wg4 = WStream(4, 128, 2, 2, cast_engs=("act",))
        sgr = Ring([A.alloc([128, 512]) for _ in range(2)])
        m1r = Ring([A.alloc([128, 512]) for _ in range(2)])

        def load_d1(ft):
            fs = slice(ft * 128, (ft + 1) * 128)
            return (wg8.load(w_in_d[:, C_GA + ft * 128:C_GA + (ft + 1) * 128]),
                    wg8.load(w_in_d[:, C_GB + ft * 128:C_GB + (ft + 1) * 128]),
                    wg8.load(wba_d[:, fs]),
                    wg4.load(wbb_d[:, fs]))

        wnext = load_d1(0)
        for ft in range(8):
            wga, wgb, wba, wbb = wnext
            if ft + 1 < 8:
                wnext = load_d1(ft + 1)
            for tb in range(4):
                tsl = slice(tb * 512, (tb + 1) * 512)
                hR = (hT, slice(tb * 4, tb * 4 + 4))
                pga = PSR.next()
                for c in range(8):
                    mm(pga[:, :], wga[:, c, :], hT[:, c, tsl], c == 0, c == 7, R=[wga, hR], W=[pga])
                pba = PSR.next()
                for c in range(8):
                    mm(pba[:, :], wba[:, c, :], oaT[:, c, tsl], c == 0, c == 7, R=[wba, oaT], W=[pba])
                sga = sgr.next()
                act(sga[:, :], pga[:, :], AF.Sigmoid, R=[pga], W=[sga])
                m1 = m1r.next()
                tt("dve", m1[:, :], sga[:, :], pba[:, :], ALU.mult, R=[sga, pba], W=[m1])
                pgb = PSR.next()
                for c in range(8):
                    mm(pgb[:, :], wgb[:, c, :], hT[:, c, tsl], c == 0, c == 7, R=[wgb, hR], W=[pgb])
                pbb = PSR.next()
                for c in range(4):
                    mm(pbb[:, :], wbb[:, c, :], obT[:, c, tsl], c == 0, c == 3, R=[wbb, obT], W=[pbb])
                sgb = sgr.next()
                act(sgb[:, :], pgb[:, :], AF.Sigmoid, R=[pgb], W=[sgb])
                tt("dve", sgb[:, :], sgb[:, :], pbb[:, :], ALU.mult, R=[sgb, pbb], W=[sgb])
                tt("dve", mergedT[:, ft, tsl], m1[:, :], sgb[:, :], ALU.add, R=[m1, sgb], W=[mergedT])
        P.barrier()
        A.release(mD)

        AL = Arena(arena_h, mergedT_lo, base=consts_mark)
        AH = Arena(arena_h, ARENA_WORDS, base=mergedT_hi)
        W1 = AL.alloc([128, 32, 8, 128], BF16, ntrk=32)
        woutb = AL.alloc([128, 8, D], BF16)
        LNP = AH.alloc([128, 2, D])
        dma(LNP[:], lnpost_d, W=[LNP])
        wo_st = Ring([AH.alloc([128, D]) for _ in range(2)])
        for c in range(8):
            st = wo_st.next()
            dma(st[:], wout_d[c * 128:(c + 1) * 128, :], W=[st])
            copy("act", woutb[:, c, :], st[:], R=[st], W=[woutb])
        yr = Ring([AH.alloc([128, D]) for _ in range(2)])
        xr = Ring([AH.alloc([128, D]) for _ in range(3)])
        junk = AH.alloc([128, D])
        ssr = Ring([AH.alloc([128, 1]) for _ in range(4)])
        rsr = Ring([AH.alloc([128, 1]) for _ in range(4)])
        w1st = Ring([AH.alloc([128, 2048]) for _ in range(2)])

        def norm_residual(y, lnp_ap, lnp_tile, xres, ss, rstd, junk_t):
            rms_rstd(y[:, :], D, [y], junk_t, ss, rstd)
            stt("dve", y[:, :], y[:, :], rstd[:, 0:1], lnp_ap, ALU.mult, ALU.mult, R=[y, rstd, lnp_tile], W=[y])
            tt("dve", y[:, :], y[:, :], xres[:, :], ALU.add, R=[y, xres], W=[y])

        def load_w1(i):
            c, hf = i // 2, i % 2
            st = w1st.next()
            dma(st[:], wff1_d[c * 128:(c + 1) * 128, hf * 2048:(hf + 1) * 2048], W=[st])
            copy("act", W1[:, hf * 16:(hf + 1) * 16, c, :], st[:].rearrange("p (f n) -> p f n", n=128), R=[st], W=[W1])

        for t in range(NT):
            y = yr.next()
            xt = xr.next()
            dma(xt[:], x_d[t * 128:(t + 1) * 128, :], W=[xt])
            load_w1(t)
            for half in range(2):
                pst = PSR.next()
                for c in range(8):
                    mm(pst[:, :], mergedT[:, c, t * 128:(t + 1) * 128], woutb[:, c, half * 512:(half + 1) * 512], c == 0, c == 7,
                       R=[mergedT, woutb], W=[pst])
                copy("act", y[:, half * 512:(half + 1) * 512], pst[:, :], R=[pst], W=[y])
            norm_residual(y, LNP[:, 0, :], LNP, xt, ssr.next(), rsr.next(), junk)
            dma(out_d[t * 128:(t + 1) * 128, :], y[:, :], R=[y], eng="pool")
        P.barrier()

        A.release(consts_mark)
        W1e = A.alloc([128, 32, 8, 128], BF16)
        W2 = A.alloc([128, 32, D], BF16, ntrk=32)
        LNP = A.alloc([128, D])
        dma(LNP[:], lnpost_d[:, 1, :], W=[LNP])
        w2st = Ring([A.alloc([128, 512]) for _ in range(3)])
        xr = Ring([A.alloc([128, D]) for _ in range(2)])
        yr = Ring([A.alloc([128, D]) for _ in range(2)])
        hb1 = A.alloc([128, D], BF16)
        h2r = Ring([A.alloc([128, 8, 256], BF16) for _ in range(2)])
        rlr = Ring([A.alloc([128, 256]) for _ in range(2)])
        fcr = Ring([A.alloc([128, 256], BF16) for _ in range(3)])
        ssr = Ring([A.alloc([128, 1]) for _ in range(4)])
        rsr = Ring([A.alloc([128, 1]) for _ in range(4)])
        acc_banks = psum_f[0:4]
        ffr = Ring(psum_f[4:7])
        NBLK = S // 256
        k2 = [0]

        def load_w2(fc):
            for hf in range(2):
                st = w2st.next()
                dma(st[:], wff2_d[fc * 128:(fc + 1) * 128, hf * 512:(hf + 1) * 512], W=[st])
                copy("act", W2[:, fc, hf * 512:(hf + 1) * 512], st[:], R=[st], W=[(W2, fc)])
                k2[0] += 1

        def prep_block(b):
            h2 = h2r.next()
            for tl in range(2):
                t = b * 2 + tl
                xt = xr.next()
                dma(xt[:], out_d[t * 128:(t + 1) * 128, :], W=[xt])
                norm_transpose(xt, LNW2, h2, tl * 128, [h2], hb1, ssr.next(), rsr.next(), hb1)
            return h2

        def ff1(h2, fc):
            pst = ffr.next()
            for c in range(8):
                mm(pst[:, 0:256], W1[:, fc, c, :], h2[:, c, :], c == 0, c == 7, R=[(W1, fc), h2], W=[pst])
            rl = rlr.next()
            act(rl[:, :], pst[:, 0:256], AF.Relu, R=[pst], W=[rl])
            fc_t = fcr.next()
            tt("dve", fc_t[:, :], rl[:, :], rl[:, :], ALU.mult, R=[rl], W=[fc_t])
            return fc_t

        def ff2(fc_t, fc):
            for tl in range(2):
                for hf in range(2):
                    bank = acc_banks[tl * 2 + hf]
                    mm(bank[:, :], fc_t[:, tl * 128:(tl + 1) * 128], W2[:, fc, hf * 512:(hf + 1) * 512], fc == 0, fc == 31,
                       R=[fc_t, (W2, fc)], W=[bank])

        h2_next = prep_block(0)
        for b in range(NBLK):
            h2 = h2_next
            if b == 0:
                load_w2(0)
                load_w2(1)
            prev = ff1(h2, 0)
            for fc in range(32):
                if b == 0 and fc + 2 < 32:
                    load_w2(fc + 2)
                nxt = ff1(h2, fc + 1) if fc + 1 < 32 else None
                ff2(prev, fc)
                prev = nxt
            if b + 1 < NBLK:
                h2_next = prep_block(b + 1)
            for tl in range(2):
                t = b * 2 + tl
                y = yr.next()
                xt = xr.next()
                dma(xt[:], out_d[t * 128:(t + 1) * 128, :], W=[xt])
                for hf in range(2):
                    copy("act", y[:, hf * 512:(hf + 1) * 512], acc_banks[tl * 2 + hf][:, :], R=[acc_banks[tl * 2 + hf]], W=[y])
                norm_residual(y, LNP[:, :], LNP, xt, ssr.next(), rsr.next(), hb1)
                final_ops.append(dma(out_d[t * 128:(t + 1) * 128, :], y[:, :], R=[y], eng="pool"))
        P.emit(final_ops=final_ops)
    return nc


def host_inputs(inputs):
    f = lambda a: np.ascontiguousarray(np.asarray(a, dtype=np.float32))
    rel_bias = f(inputs["rel_bias"])
    tbl, amask = _attn_tables(rel_bias)
    vec = np.zeros((128, 8 + 8 + 1 + 120 + 32), np.float32)
    vec[:, 0:8] = f(inputs["ln_mix_pre"])[0].reshape(8, 128).T
    vec[:, 8:16] = f(inputs["ln_mlp_pre"])[0].reshape(8, 128).T
    vec[:, 16] = f(inputs["norm_a"])[0]
    cw = f(inputs["conv_w"])[0]
    vec[:, 17:137] = cw.reshape(5, 24, 128).transpose(2, 1, 0).reshape(128, 120)
    vec[:, 137:145] = f(inputs["a_log_f"])[0][None, :]
    vec[:, 145:153] = f(inputs["a_log_b"])[0][None, :]
    vec[:, 153:161] = f(inputs["dt_bias_f"])[0][None, :]
    vec[:, 161:169] = f(inputs["dt_bias_b"])[0][None, :]
    lnpost = np.stack([np.broadcast_to(f(inputs["ln_mix_post"])[0], (128, D)),
                       np.broadcast_to(f(inputs["ln_mlp_post"])[0], (128, D))], axis=1)
    shared = {
        "w_in": f(inputs["w_in"])[0],
        "w_branch_a": f(inputs["w_branch_a"])[0],
        "w_branch_b": f(inputs["w_branch_b"])[0],
        "w_out": f(inputs["w_out"])[0],
        "w_ff1": f(inputs["w_ff1"])[0],
        "w_ff2": f(inputs["w_ff2"])[0],
        "consts": _consts(),
        "attn_tbl": tbl,
        "attn_mask": amask,
        "vecs": vec,
        "lnpost": np.ascontiguousarray(lnpost),
    }
    return shared


_NC_CACHE = {}


def kernel(**inputs):
    x = np.ascontiguousarray(np.asarray(inputs["x"], dtype=np.float32))
    B = x.shape[0]
    shared = host_inputs(inputs)
    if "nc" not in _NC_CACHE:
        _NC_CACHE["nc"] = build()
    nc = _NC_CACHE["nc"]
    in_maps = []
    for b in range(B):
        m = dict(shared)
        m["x"] = x[b]
        in_maps.append(m)
    res = run_bass_kernel_spmd(nc, in_maps, core_ids=list(range(B)))
    return np.stack([np.asarray(r["out"]) for r in res.results], axis=0).astype(np.float32)
```
